# Optimizing a Trainium2 kernel written in Bass

```python
import jax, jax.numpy as jnp
from jax import lax
import numpy as np

D_MODEL = 1024
BATCH = 8
SEQ = 4096
DEPTH = 2

N_META = 16
D_CONV = 512
CONV_A_WIDTH = 3
DN_HEADS = 4
DN_HEAD_DIM = 128
DN_DIM = DN_HEADS * DN_HEAD_DIM
DN_CONV_WIDTH = 4
DN_CHUNK = 64
D_MIX = D_CONV + DN_DIM
IN_DIM = 3 * D_CONV + 4 * DN_DIM + 2 * DN_HEADS
SWA_HEADS = 16
SWA_KV_HEADS = 4
SWA_HEAD_DIM = 64
SWA_WINDOW = 128
SWA_BLOCK = 128
D_FF = 2816
FFN_CONV_WIDTH = 3
EPS = 1e-6
N_EVEN = (DEPTH + 1) // 2
N_ODD = DEPTH // 2

kernel_name = "hybrid_shortconv_gdn_swa_convffn_meta"


def rms_norm(x, w):
    xf = x.astype(jnp.float32)
    y = xf * lax.rsqrt(jnp.mean(xf * xf, -1, keepdims=True) + EPS)
    return (y * w.astype(jnp.float32)).astype(x.dtype)


def l2_norm(x):
    return x * lax.rsqrt(jnp.sum(x * x, -1, keepdims=True) + EPS)


def causal_dwconv(x, w):
    k = w.shape[0]
    return lax.conv_general_dilated(
        x, w[:, None, :].astype(x.dtype), window_strides=(1,), padding=((k - 1, 0),),
        dimension_numbers=('NWC', 'WIO', 'NWC'), feature_group_count=x.shape[-1])


def gated_delta_rule(q, k, v, beta, g):
    q, k, v, beta, g = (t.astype(jnp.float32) for t in (q, k, v, beta, g))
    b, l, h, dk = q.shape
    dv = v.shape[-1]
    c = DN_CHUNK
    n = l // c

    def chunks(t):
        t = t.reshape((b, n, c, h) + t.shape[3:])
        return jnp.moveaxis(t, 3, 1)

    q, k, v, beta, g = chunks(q), chunks(k), chunks(v), chunks(beta), chunks(g)
    decay = jnp.cumsum(g, -1)
    diff = decay[..., :, None] - decay[..., None, :]
    idx = jnp.arange(c)
    strict = idx[:, None] > idx[None, :]
    incl = idx[:, None] >= idx[None, :]
    dmask = jnp.exp(jnp.where(incl, diff, -jnp.inf))
    kk = jnp.einsum('bhnid,bhnjd->bhnij', k, k)
    a_strict = jnp.where(strict, beta[..., None] * kk * dmask, 0.0)
    t_mat = a_strict + jnp.eye(c, dtype=jnp.float32)
    rhs = jnp.concatenate([v * beta[..., None], k * (beta * jnp.exp(decay))[..., None]], -1)
    sol = lax.linalg.triangular_solve(t_mat, rhs, left_side=True, lower=True, unit_diagonal=True)
    u = sol[..., :dv]
    w = sol[..., dv:]
    qk = jnp.einsum('bhnid,bhnjd->bhnij', q, k) * dmask
    q_dec = q * jnp.exp(decay)[..., None]
    k_dec = k * jnp.exp(decay[..., -1:] - decay)[..., None]
    g_last = jnp.exp(decay[..., -1])

    def step(s, inp):
        u_n, w_n, qk_n, qd_n, kd_n, gl_n = inp
        v_new = u_n - jnp.einsum('bhcd,bhde->bhce', w_n, s)
        o = jnp.einsum('bhcd,bhde->bhce', qd_n, s) + jnp.einsum('bhij,bhje->bhie', qk_n, v_new)
        s = s * gl_n[..., None, None] + jnp.einsum('bhcd,bhce->bhde', kd_n, v_new)
        return s, o

    xs = tuple(jnp.moveaxis(t, 2, 0) for t in (u, w, qk, q_dec, k_dec, g_last))
    s0 = jnp.zeros((b, h, dk, dv), jnp.float32)
    _, o = lax.scan(step, s0, xs)
    return jnp.transpose(o, (1, 0, 3, 2, 4)).reshape(b, l, h, dv)


def even_mixer(h, w_in, conv_a_w, dn_conv_w, a_log, dt_bias, dn_norm_w, w_out):
    b, l, _ = h.shape
    p = h @ w_in
    sizes = [D_CONV, D_CONV, D_CONV, 3 * DN_DIM, DN_DIM, DN_HEADS, DN_HEADS]
    splits = [sum(sizes[:i + 1]) for i in range(len(sizes) - 1)]
    a_gate_in, a_gate_out, a_h, qkv, z, beta_raw, alpha_raw = jnp.split(p, splits, -1)
    y_a = a_gate_out * causal_dwconv(a_gate_in * a_h, conv_a_w)
    qkv = jax.nn.silu(causal_dwconv(qkv, dn_conv_w))
    q, k, v = jnp.split(qkv, 3, -1)
    hd = (b, l, DN_HEADS, DN_HEAD_DIM)
    q = l2_norm(q.reshape(hd).astype(jnp.float32)) * (DN_HEAD_DIM ** -0.5)
    k = l2_norm(k.reshape(hd).astype(jnp.float32))
    v = v.reshape(hd).astype(jnp.float32)
    beta = jax.nn.sigmoid(beta_raw.astype(jnp.float32))
    g = -jnp.exp(a_log.astype(jnp.float32)) * jax.nn.softplus(
        alpha_raw.astype(jnp.float32) + dt_bias.astype(jnp.float32))
    pad = (-N_META) % DN_CHUNK
    padw = lambda t: jnp.pad(t, ((0, 0), (pad, 0)) + ((0, 0),) * (t.ndim - 2))
    o = gated_delta_rule(padw(q), padw(k), padw(v), padw(beta), padw(g))[:, pad:]
    o = rms_norm(o, dn_norm_w) * jax.nn.silu(z.reshape(hd).astype(jnp.float32))
    y_b = o.reshape(b, l, DN_DIM).astype(h.dtype)
    return jnp.concatenate([y_a, y_b], -1) @ w_out


def sink_softmax(logits, sink):
    m = jnp.maximum(jnp.max(logits, -1, keepdims=True), sink)
    e = jnp.exp(logits - m)
    return e / (jnp.sum(e, -1, keepdims=True) + jnp.exp(sink - m))


def swa_mixer(h, wq, wk, wv, q_norm_w, k_norm_w, sinks, wo):
    b, l, _ = h.shape
    kv, grp, d = SWA_KV_HEADS, SWA_HEADS // SWA_KV_HEADS, SWA_HEAD_DIM
    q = rms_norm((h @ wq).reshape(b, l, kv, grp, d), q_norm_w) * (d ** -0.5)
    k = rms_norm((h @ wk).reshape(b, l, kv, d), k_norm_w)
    v = (h @ wv).reshape(b, l, kv, d)
    sink = sinks.astype(jnp.float32).reshape(kv, grp)
    qm, qr = q[:, :N_META], q[:, N_META:]
    km, kr = k[:, :N_META], k[:, N_META:]
    vm, vr = v[:, :N_META], v[:, N_META:]
    sm = jnp.einsum('bikgd,bjkd->bkgij', qm, km).astype(jnp.float32)
    mmask = jnp.tril(jnp.ones((N_META, N_META), bool))
    pm = sink_softmax(jnp.where(mmask, sm, -jnp.inf), sink[None, :, :, None, None])
    om = jnp.einsum('bkgij,bjkd->bikgd', pm.astype(v.dtype), vm).reshape(b, N_META, SWA_HEADS * d)
    s_real = l - N_META
    nb = s_real // SWA_BLOCK
    qb = qr.reshape(b, nb, SWA_BLOCK, kv, grp, d)
    kb = kr.reshape(b, nb, SWA_BLOCK, kv, d)
    vb = vr.reshape(b, nb, SWA_BLOCK, kv, d)
    band = lambda t: jnp.concatenate(
        [jnp.concatenate([jnp.zeros_like(t[:, :1]), t[:, :-1]], 1), t], 2)
    kband, vband = band(kb), band(vb)
    s_meta = jnp.einsum('bnikgd,bjkd->bnkgij', qb, km).astype(jnp.float32)
    s_band = jnp.einsum('bnikgd,bnjkd->bnkgij', qb, kband).astype(jnp.float32)
    i = jnp.arange(SWA_BLOCK)[:, None]
    j = jnp.arange(2 * SWA_BLOCK)[None, :]
    n = jnp.arange(nb)[:, None, None]
    rel = i + SWA_BLOCK - j
    valid = (rel >= 0) & (rel < SWA_WINDOW) & (n * SWA_BLOCK - SWA_BLOCK + j >= 0)
    s_band = jnp.where(valid[None, :, None, None], s_band, -jnp.inf)
    p = sink_softmax(jnp.concatenate([s_meta, s_band], -1), sink[None, None, :, :, None, None])
    p = p.astype(v.dtype)
    orr = (jnp.einsum('bnkgim,bmkd->bnikgd', p[..., :N_META], vm)
           + jnp.einsum('bnkgij,bnjkd->bnikgd', p[..., N_META:], vband))
    orr = orr.reshape(b, s_real, SWA_HEADS * d)
    return jnp.concatenate([om, orr], 1) @ wo


def conv_ffn(h, w_up, conv_w, w_down):
    gate, val = jnp.split(h @ w_up, 2, -1)
    gate = causal_dwconv(gate, conv_w)
    return (jax.nn.silu(gate) * val) @ w_down


def setup_inputs(seed: int = 0) -> dict:
    key = jax.random.key(seed)
    ks = jax.random.split(key, 24)
    nrm = lambda k, s, scale: jax.random.normal(k, s, jnp.float32) * scale
    gain = lambda k, s: 1.0 + 0.05 * jax.random.normal(k, s, jnp.float32)
    dt = jnp.exp(jax.random.uniform(ks[7], (N_EVEN, DN_HEADS), jnp.float32, np.log(1e-3), np.log(1e-1)))
    return {
        "x": nrm(ks[0], (BATCH, SEQ, D_MODEL), 1.0),
        "meta_tokens": nrm(ks[1], (N_META, D_MODEL), 1.0),
        "attn_norm_w": gain(ks[2], (DEPTH, D_MODEL)),
        "ffn_norm_w": gain(ks[3], (DEPTH, D_MODEL)),
        "mix_w_in": nrm(ks[4], (N_EVEN, D_MODEL, IN_DIM), D_MODEL ** -0.5),
        "conv_a_w": nrm(ks[5], (N_EVEN, CONV_A_WIDTH, D_CONV), CONV_A_WIDTH ** -0.5),
        "dn_conv_w": nrm(ks[6], (N_EVEN, DN_CONV_WIDTH, 3 * DN_DIM), DN_CONV_WIDTH ** -0.5),
        "dn_a_log": jnp.log(jax.random.uniform(ks[8], (N_EVEN, DN_HEADS), jnp.float32, 1.0, 16.0)),
        "dn_dt_bias": dt + jnp.log(-jnp.expm1(-dt)),
        "dn_norm_w": gain(ks[9], (N_EVEN, DN_HEAD_DIM)),
        "mix_w_out": nrm(ks[10], (N_EVEN, D_MIX, D_MODEL), D_MIX ** -0.5),
        "swa_wq": nrm(ks[11], (N_ODD, D_MODEL, SWA_HEADS * SWA_HEAD_DIM), D_MODEL ** -0.5),
        "swa_wk": nrm(ks[12], (N_ODD, D_MODEL, SWA_KV_HEADS * SWA_HEAD_DIM), D_MODEL ** -0.5),
        "swa_wv": nrm(ks[13], (N_ODD, D_MODEL, SWA_KV_HEADS * SWA_HEAD_DIM), D_MODEL ** -0.5),
        "swa_q_norm_w": gain(ks[14], (N_ODD, SWA_HEAD_DIM)),
        "swa_k_norm_w": gain(ks[15], (N_ODD, SWA_HEAD_DIM)),
        "swa_sinks": nrm(ks[16], (N_ODD, SWA_HEADS), 0.5),
        "swa_wo": nrm(ks[17], (N_ODD, SWA_HEADS * SWA_HEAD_DIM, D_MODEL), (SWA_HEADS * SWA_HEAD_DIM) ** -0.5),
        "ffn_w_up": nrm(ks[18], (DEPTH, D_MODEL, 2 * D_FF), D_MODEL ** -0.5),
        "ffn_conv_w": nrm(ks[19], (DEPTH, FFN_CONV_WIDTH, D_FF), FFN_CONV_WIDTH ** -0.5),
        "ffn_w_down": nrm(ks[20], (DEPTH, D_FF, D_MODEL), D_FF ** -0.5),
    }


def reference(x, meta_tokens, attn_norm_w, ffn_norm_w, mix_w_in, conv_a_w, dn_conv_w, dn_a_log,
              dn_dt_bias, dn_norm_w, mix_w_out, swa_wq, swa_wk, swa_wv, swa_q_norm_w, swa_k_norm_w,
              swa_sinks, swa_wo, ffn_w_up, ffn_conv_w, ffn_w_down):
    b = x.shape[0]
    meta = jnp.broadcast_to(meta_tokens[None].astype(x.dtype), (b, N_META, D_MODEL))
    h = jnp.concatenate([meta, x], 1)
    for layer in range(DEPTH):
        i = layer // 2
        hn = rms_norm(h, attn_norm_w[layer])
        if layer % 2 == 0:
            mix = even_mixer(hn, mix_w_in[i], conv_a_w[i], dn_conv_w[i], dn_a_log[i],
                             dn_dt_bias[i], dn_norm_w[i], mix_w_out[i])
        else:
            mix = swa_mixer(hn, swa_wq[i], swa_wk[i], swa_wv[i], swa_q_norm_w[i],
                            swa_k_norm_w[i], swa_sinks[i], swa_wo[i])
        h = h + mix.astype(h.dtype)
        ff = conv_ffn(rms_norm(h, ffn_norm_w[layer]), ffn_w_up[layer], ffn_conv_w[layer], ffn_w_down[layer])
        h = h + ff.astype(h.dtype)
    return h[:, N_META:]
```

```python
import contextlib
import numpy as np
import concourse.bass as bass
import concourse.mybir as mybir
from concourse.bass_utils import run_bass_kernel_spmd

F32 = mybir.dt.float32
BF16 = mybir.dt.bfloat16
AF = mybir.ActivationFunctionType
ALU = mybir.AluOpType
AX = mybir.AxisListType

EPS = 1e-6
GROUPS_PER_TILE = [5, 4, 4, 4, 4, 4, 4, 4]
CMAX = 640
NSLOT = 6
MASK_NAMES = ["ID", "ONES", "NEGONES", "BLK64", "L", "U", "MC0", "MC1", "NEGSL", "POSSU",
              "LV0", "LV1", "LV2", "LV3", "LV4", "LV5", "UV0", "UV1", "UV2", "UV3", "UV4", "UV5",
              "SWAPREV", "SWACUR", "SWAMETA", "METAK"]
MI = {n: i for i, n in enumerate(MASK_NAMES)}
PP_ANW, PP_FNW, PP_CA, PP_DC, PP_FC, PP_QN, PP_KN, NPP = 0, 16, 32, 44, 92, 224, 225, 226
RP_DTB, RP_ALOG, RP_DNW, RP_SINK, NRP = 0, 4, 8, 136, 152


class Tok:
    __slots__ = ("w", "r", "x")

    def __init__(self, x=False):
        self.w = None
        self.r = []
        self.x = x


class Op:
    __slots__ = ("eng", "fn", "dma", "waits", "need_inc", "pos", "snap", "sem", "semval", "fs")


class Prog:
    ENGS = ("pe", "act", "dve", "pool", "sp")
    NDMASEM = 12

    def __init__(self, nc):
        self.nc = nc
        self.stack = contextlib.ExitStack()
        self.e = {"pe": nc.tensor, "act": nc.scalar, "dve": nc.vector, "pool": nc.gpsimd, "sp": nc.sync}
        self.ops = []
        self.label = ""
        self.labels = []
        self.npos = {e: 0 for e in self.ENGS}
        self.known = {e: {f: -1 for f in self.ENGS} for e in self.ENGS}
        self.known_dma = {e: set() for e in self.ENGS}
        self.sem = {e: self.stack.enter_context(nc.semaphore("s_" + e)) for e in self.ENGS}
        self.use_scopes = False
        self.dsem, self.dsem_use, self.dsem_last, self.dsem_rr = {}, {}, {}, {}
        for e in ("sp", "pool"):
            self.dsem[e] = [self.stack.enter_context(nc.semaphore("d_%s%d" % (e, i))) for i in range(self.NDMASEM)]
            self.dsem_use[e] = [0] * self.NDMASEM
            self.dsem_last[e] = [None] * self.NDMASEM
            self.dsem_rr[e] = 0

    def sb(self, name, shape, dtype):
        return self.stack.enter_context(self.nc.sbuf_tensor(name, list(shape), dtype))

    def ps(self, name, shape, dtype):
        return self.stack.enter_context(self.nc.psum_tensor(name, list(shape), dtype))

    @staticmethod
    def tok(n=None):
        if n is None:
            return Tok()
        return [Tok() for _ in range(n)]

    def add(self, eng, fn, reads=(), writes=(), dma=False, track=True, fs=1 << 30):
        op = Op()
        opid = len(self.ops)
        op.eng, op.fn, op.dma = eng, fn, dma
        op.fs = fs
        op.need_inc = False
        op.sem = None
        op.semval = 0
        deps = set()
        xr = [t for t in reads if t.x]
        if xr:
            reads = [t for t in reads if not t.x]
            writes = list(writes) + xr
        for t in reads:
            if t.w is not None:
                deps.add(t.w)
        for t in writes:
            if t.w is not None:
                deps.add(t.w)
            deps.update(t.r)
        if dma:
            pool = self.dsem[eng]
            i = self.dsem_rr[eng]
            self.dsem_rr[eng] = (i + 1) % len(pool)
            if self.dsem_last[eng][i] is not None:
                deps.add(self.dsem_last[eng][i])
            self.dsem_use[eng][i] += 1
            op.sem = pool[i]
            op.semval = 16 * self.dsem_use[eng][i]
            self.dsem_last[eng][i] = opid
        known = self.known[eng]
        kd = self.known_dma[eng]
        cw = {}
        waits = []
        for d in deps:
            dop = self.ops[d]
            if dop.dma:
                if d in kd:
                    continue
                kd.add(d)
                waits.append(d)
            else:
                if dop.eng == eng:
                    if not dma and (eng == "pe" or (dop.fs >= 256 and fs >= 256)):
                        continue
                    key = "self_" + eng
                else:
                    key = dop.eng
                if known.get(key, -1) >= dop.pos:
                    continue
                if key not in cw or self.ops[cw[key]].pos < dop.pos:
                    cw[key] = d
        for key, d in cw.items():
            self.ops[d].need_inc = True
            waits.append(d)
        for d in waits:
            dop = self.ops[d]
            for f, v in dop.snap.items():
                if known.get(f, -1) < v:
                    known[f] = v
            if not dop.dma:
                key = dop.eng if dop.eng != eng else "self_" + eng
                if known.get(key, -1) < dop.pos:
                    known[key] = dop.pos
        op.waits = waits
        self.labels.append(self.label)
        op.pos = self.npos[eng]
        if fn is not None:
            self.npos[eng] += 1
        snap = {f: v for f, v in known.items() if not f.startswith("self_")}
        if not dma and fn is not None:
            snap[eng] = op.pos
        op.snap = snap
        self.ops.append(op)
        if not track:
            return opid
        for t in reads:
            t.r.append(opid)
        for t in writes:
            t.w = opid
            t.r = []
        return opid

    def dma(self, eng, out, in_, reads=(), writes=()):
        e = self.e[eng]
        return self.add(eng, lambda: e.dma_start(out=out, in_=in_), reads, writes, dma=True)

    def wait_all(self, eng, toks):
        return self.add(eng, None, reads=(), writes=toks, track=False)

    def emit(self):
        cnt = {e: 0 for e in self.ENGS}
        val = {}
        for i, op in enumerate(self.ops):
            if op.need_inc:
                cnt[op.eng] += 1
                val[i] = cnt[op.eng]
        nw = 0
        cur = None
        scope = None
        for i, op in enumerate(self.ops):
            e = self.e[op.eng]
            if self.use_scopes and self.labels[i] != cur:
                if scope is not None:
                    self.nc.leave_named_scope(cur, scope, False)
                cur = self.labels[i]
                scope = self.nc.enter_named_scope(cur, False)[0]
            for d in op.waits:
                dop = self.ops[d]
                if dop.dma:
                    e.wait_ge(dop.sem, dop.semval)
                else:
                    e.wait_ge(self.sem[dop.eng], val[d])
                nw += 1
            if op.fn is None:
                continue
            ins = op.fn()
            if op.dma:
                ins.then_inc(op.sem, 16)
            elif op.need_inc:
                ins.then_inc(self.sem[op.eng], 1)
        if scope is not None:
            self.nc.leave_named_scope(cur, scope, False)
        self.stats = dict(n_ops=len(self.ops), n_waits=nw, incs=dict(cnt), pos=dict(self.npos))
        return self.stats


class Arena:
    def __init__(self, handle, nwords):
        self.h = handle
        self.n = nwords
        self.off = 0
        self.live = []

    def reset(self):
        self.off = 0

    def alloc(self, free_shape, dtype, ntok=1, parts=128):
        nel = int(np.prod(free_shape))
        words = nel if dtype == F32 else (nel + 1) // 2
        assert self.off + words <= self.n, ("arena overflow", self.off, words, self.n)
        s, e = self.off, self.off + words
        self.off = e
        ap = self.h[:, s:e]
        if dtype != F32:
            ap = ap.bitcast(dtype)
        if len(free_shape) == 2:
            ap = ap.rearrange("p (a b) -> p a b", a=free_shape[0])
        elif len(free_shape) == 3:
            ap = ap.rearrange("p (a b c) -> p a b c", a=free_shape[0], b=free_shape[1])
        toks = [Tok() for _ in range(ntok)]
        inh = []
        keep = []
        for (os_, oe, otoks) in self.live:
            if os_ < e and s < oe:
                for t in otoks:
                    if t.w is not None:
                        inh.append(t.w)
                    inh.extend(t.r)
                if s <= os_ and oe <= e:
                    continue
            keep.append((os_, oe, otoks))
        inh = sorted(set(inh))
        for t in toks:
            t.r = list(inh)
        keep.append((s, e, toks))
        self.live = keep
        return ap, toks


def run_pipeline(gens, depth, lag=0):
    gens = list(gens)
    active = []
    nxt = 0
    while True:
        while len(active) < depth and nxt < len(gens) and (not active or active[-1][1] >= lag):
            active.append([gens[nxt], 0])
            nxt += 1
        if not active:
            break
        for a in list(active):
            try:
                next(a[0])
                a[1] += 1
            except StopIteration:
                active.remove(a)


class TempPool:
    def __init__(self, arena, free_shape, dtype, n=2):
        self.bufs = [arena.alloc(free_shape, dtype) for _ in range(n)]
        self.i = 0

    def get(self):
        b = self.bufs[self.i % len(self.bufs)]
        self.i += 1
        return b


def make_masks():
    i = np.arange(128)[:, None]
    j = np.arange(128)[None, :]
    same = (i // 64) == (j // 64)
    m = {}
    m["ID"] = (i == j)
    m["ONES"] = np.ones((128, 128), bool)
    m["NEGONES"] = -np.ones((128, 128), np.float32)
    m["BLK64"] = same
    m["L"] = same & (i <= j)
    m["U"] = same & (i > j)
    m["MC0"] = (i < 64) & (j >= 0)
    m["MC1"] = (i >= 64) & (j >= 0)
    m["NEGSL"] = np.where(same & (i > j), 0.0, -30000.0)
    m["POSSU"] = np.where(same & (j > i), 0.0, 30000.0)
    for l in range(6):
        b = 1 << l
        lv = ((i // (2 * b)) == (j // (2 * b))) & ((i % (2 * b)) >= b) & ((j % (2 * b)) < b)
        m["LV%d" % l] = lv
        m["UV%d" % l] = lv.T
    m["SWAPREV"] = (i > j)
    m["SWACUR"] = (i <= j)
    m["SWAMETA"] = (i >= 112) & (i <= j)
    m["METAK"] = (i >= 16) & (i < 32) & (j >= 0)
    out = np.zeros((128, len(MASK_NAMES), 128), np.float32)
    for n, k in MI.items():
        out[:, k, :] = m[n].astype(np.float32)
    return out.reshape(128, len(MASK_NAMES) * 128)


def build_program(cfg):
    nc = bass.Bass("TRN2", target_bir_lowering=False)
    P = Prog(nc)
    P.use_scopes = bool(cfg.get("scopes"))

    def dram(name, shape, kind="ExternalInput"):
        return nc.dram_tensor(name, list(shape), F32, kind=kind).ap()

    x_d = dram("x", [4096, 1024])
    meta_d = dram("meta", [16, 1024])
    win_d = dram("w_in", [1024, 3592])
    wout_d = dram("w_out", [1024, 1024])
    wq_d = dram("wq", [1024, 1024])
    wk_d = dram("wk", [1024, 256])
    wv_d = dram("wv", [1024, 256])
    wo_d = dram("wo", [1024, 1024])
    wup_d = dram("w_up", [2, 1024, 5632])
    wdn_d = dram("w_dn", [2, 2816, 1024])
    pp_d = dram("pp", [128, NPP])
    rp_d = dram("rp", [128, NRP])
    mk_d = dram("masks", [128, len(MASK_NAMES) * 128])
    out_d = dram("out", [4096, 1024], kind="ExternalOutput")
    dbg_d = dram("dbg", [128, 8192], kind="ExternalOutput") if cfg.get("dbg") else None
    dbg_state = dict(off=0, names=[])

    NM = len(MASK_NAMES)
    mk_h = P.sb("mk_sb", [128, NM, 128], F32)
    mk_t = P.tok()
    MK = lambda n: mk_h[:, MI[n], :]
    MK4 = lambda n: mk_h[:, MI[n]:MI[n] + 1, :].broadcast_to([128, 4, 128])
    cb_h = P.sb("cb", [128, 3, 128], BF16)
    cb_t = P.tok()
    ID_B, ONES_B, BLK_B = cb_h[:, 0, :], cb_h[:, 1, :], cb_h[:, 2, :]
    pp_h = P.sb("pp_sb", [128, NPP + 4], F32)
    pp_t = P.tok()
    rp_h = P.sb("rp_sb", [128, NRP + 8], F32)
    rp_t = P.tok()
    PPc = lambda c: pp_h[:, c:c + 1]
    hT_h = P.sb("hT", [128, 8, CMAX], F32)
    hT_t = P.tok(8)
    hn_h = P.sb("hn", [128, 8, CMAX], BF16)
    hn_t = P.tok(8)
    ring_h = [P.sb("ring%d" % i, [128, 4096], BF16) for i in range(NSLOT)]
    ring_t = P.tok(NSLOT)
    S_h = P.sb("S", [128, 4, 128], F32)
    S_t = P.tok()
    Sb_h = P.sb("Sb", [128, 4, 128], BF16)
    Sb_t = P.tok()
    haloA_h = P.sb("haloA", [128, 4, 2], F32)
    haloA_t = P.tok(4)
    haloQ_h = P.sb("haloQ", [128, 12, 3], F32)
    haloQ_t = P.tok(12)
    haloF_h = P.sb("haloF", [128, 2, 22, 2], F32)
    haloF_t = [P.tok(22), P.tok(22)]
    kZ_h = P.sb("kZ", [128, 8, 128 + CMAX], BF16)
    kZ_t = P.tok(8)
    kZm_h = P.sb("kZm", [128, 8, 32], BF16)
    kZm_t = P.tok()
    Vb_h = P.sb("Vb", [128, 6, 4, 65], BF16)
    Vb_t = P.tok(6)
    Vm_h = P.sb("Vm", [32, 4, 65], BF16)
    Vm_t = P.tok()
    arena = Arena(P.sb("arena", [128, 23800], F32), 23800)
    bank_h = [P.ps("bank%d" % i, [128, 512], F32) for i in range(8)]
    bank_t = [Tok(x=True) for _ in range(8)]
    bank_rr = [0]

    def psum(n, dtype=F32, shape=None):
        i = bank_rr[0]
        bank_rr[0] = (i + 1) % 8
        ap = bank_h[i][:, 0:n]
        if dtype != F32:
            ap = ap.bitcast(dtype)
        if shape is not None and len(shape) == 2:
            ap = ap.rearrange("p (a b) -> p a b", a=shape[0])
        return ap, bank_t[i]

    def mm(out, lhsT, rhs, start, stop, reads, writes):
        P.add("pe", lambda: nc.tensor.matmul(out, lhsT, rhs, start=start, stop=stop), reads, writes)

    def tr(out, in_, ident, reads, writes):
        P.add("pe", lambda: nc.tensor.transpose(out, in_, ident), reads, writes)

    def fsz(ap):
        return int(np.prod(ap.shape[1:]))

    def act(out, in_, func, reads, writes, bias=0.0, scale=1.0):
        P.add("act", lambda: nc.scalar.activation(out=out, in_=in_, func=func, bias=bias, scale=scale), reads, writes,
              fs=fsz(out))

    def tt(eng, out, in0, in1, op, reads, writes):
        e = P.e[eng]
        P.add(eng, lambda: e.tensor_tensor(out, in0, in1, op), reads, writes, fs=fsz(out))

    def ts(eng, out, in0, s1, s2, op0, op1, reads, writes):
        e = P.e[eng]
        if op1 is None:
            P.add(eng, lambda: e.tensor_scalar(out, in0, s1, s2, op0), reads, writes, fs=fsz(out))
        else:
            P.add(eng, lambda: e.tensor_scalar(out, in0, s1, s2, op0, op1), reads, writes, fs=fsz(out))

    def stt(eng, out, in0, scalar, in1, op0, op1, reads, writes):
        e = P.e[eng]
        P.add(eng, lambda: e.scalar_tensor_tensor(out, in0, scalar, in1, op0, op1), reads, writes, fs=fsz(out))

    def cp(eng, out, in_, reads, writes):
        e = P.e[eng]
        if eng == "act":
            P.add(eng, lambda: e.copy(out, in_), reads, writes, fs=fsz(out))
        else:
            P.add(eng, lambda: e.tensor_copy(out, in_), reads, writes, fs=fsz(out))

    def memset(eng, ap, v, writes):
        e = P.e[eng]
        P.add(eng, lambda: e.memset(ap, v), (), writes, fs=fsz(ap))

    def recip(out, in_, reads, writes):
        P.add("dve", lambda: nc.vector.reciprocal(out, in_), reads, writes, fs=fsz(out))

    def rsum(out, in_, reads, writes):
        P.add("dve", lambda: nc.vector.reduce_sum(out, in_, AX.X), reads, writes, fs=fsz(out))

    dbg_stage = P.sb("dbg_stage", [128, 1024], F32) if cfg.get("dbg") else None
    dbg_stage_t = P.tok()
    dbg_out_t = P.tok()

    def dump(name, ap2d, toks, parts=128):
        if not cfg.get("dbg") or name not in cfg["dbg"]:
            return
        n = ap2d.shape[1]
        off = dbg_state["off"]
        dbg_state["off"] += n
        dbg_state["names"].append((name, off, n, parts))
        cp("dve", dbg_stage[0:parts, 0:n], ap2d, toks, [dbg_stage_t])
        P.dma("sp", dbg_d[0:parts, off:off + n], dbg_stage[0:parts, 0:n], reads=[dbg_stage_t], writes=[dbg_out_t])

    P.dma("sp", mk_h[:, :, :], mk_d.rearrange("p (m j) -> p m j", m=NM), writes=[mk_t])
    P.dma("sp", pp_h[:, 0:NPP], pp_d[:, :], writes=[pp_t])
    P.dma("sp", rp_h[:, 0:NRP], rp_d[:, :], writes=[rp_t])
    for i, n in enumerate(["ID", "ONES", "BLK64"]):
        P.dma("pool", cb_h[:, i, :], mk_d[:, MI[n] * 128:(MI[n] + 1) * 128], writes=[cb_t])
    QSC, KSC = NPP, NPP + 1
    ts("dve", pp_h[:, QSC:QSC + 1], pp_h[:, PP_QN:PP_QN + 1], 0.125, None, ALU.mult, None, [pp_t], [pp_t])
    cp("dve", pp_h[:, KSC:KSC + 1], pp_h[:, PP_KN:PP_KN + 1], [pp_t], [pp_t])
    NEXPA = NRP
    act(rp_h[:, NEXPA:NEXPA + 4], rp_h[:, RP_ALOG:RP_ALOG + 4], AF.Exp, [rp_t], [rp_t])
    ts("dve", rp_h[:, NEXPA:NEXPA + 4], rp_h[:, NEXPA:NEXPA + 4], -1.0, None, ALU.mult, None, [rp_t], [rp_t])
    act(rp_h[:, RP_SINK:RP_SINK + 16], rp_h[:, RP_SINK:RP_SINK + 16], AF.Exp, [rp_t], [rp_t])
    memset("dve", S_h[:, :, :], 0.0, [S_t])
    memset("dve", Sb_h[:, :, :], 0.0, [Sb_t])
    memset("dve", haloA_h[:, :, :], 0.0, haloA_t)
    memset("dve", haloQ_h[:, :, :], 0.0, haloQ_t)
    memset("dve", haloF_h[:, :, :, :], 0.0, haloF_t[0] + haloF_t[1])
    memset("dve", Vb_h[:, :, :, :], 1.0, Vb_t)
    memset("dve", kZ_h[:, :, :], 0.0, kZ_t)

    def wsrc(name):
        kind = name[0]
        if kind == "in":
            b = name[1]
            if b == "ab":
                return [((8, 8), win_d[:, 3584:3592].rearrange("(k p) n -> p k n", p=128), 0)]
            return [((8, 512), win_d[:, b * 512:(b + 1) * 512].rearrange("(k p) n -> p k n", p=128), 0)]
        if kind in ("out", "q", "o"):
            src = {"out": wout_d, "q": wq_d, "o": wo_d}[kind]
            b = name[1]
            return [((8, 512), src[:, b * 512:(b + 1) * 512].rearrange("(k p) n -> p k n", p=128), 0)]
        if kind == "kv":
            return [((8, 256), wk_d[:, :].rearrange("(k p) n -> p k n", p=128), 0),
                    ((8, 256), wv_d[:, :].rearrange("(k p) n -> p k n", p=128), 2048)]
        if kind == "up":
            l, half, jb = name[1], name[2], name[3]
            n = 512 if jb < 5 else 256
            c0 = half * 2816 + jb * 512
            return [((8, n), wup_d[l, :, c0:c0 + n].rearrange("(k p) n -> p k n", p=128), 0)]
        if kind == "dn":
            l, mp, jh = name[1], name[2], name[3]
            return [((11, 256), wdn_d[l, jh * 1408:(jh + 1) * 1408, mp * 256:(mp + 1) * 256].rearrange("(j p) n -> p j n", p=128), 0)]
        raise ValueError(name)

    def tile_blocks():
        seq = []
        if cfg["mix0"]:
            seq += [("in", b) for b in range(7)] + [("in", "ab"), ("out", 0), ("out", 1)]
        if cfg["ffn0"]:
            for jb in range(6):
                seq += [("up", 0, 0, jb), ("up", 0, 1, jb)]
            seq += [("dn", 0, mp, jh) for mp in range(4) for jh in range(2)]
        if cfg["mix1"]:
            seq += [("q", 0), ("q", 1), ("kv",), ("o", 0), ("o", 1)]
        if cfg["ffn1"]:
            for jb in range(6):
                seq += [("up", 1, 0, jb), ("up", 1, 1, jb)]
            seq += [("dn", 1, mp, jh) for mp in range(4) for jh in range(2)]
        return seq

    wseq = []
    for _ in GROUPS_PER_TILE:
        wseq += tile_blocks()
    wstate = dict(issued=0, used=0, released=0)

    def w_pump():
        while wstate["issued"] < len(wseq) and wstate["issued"] - NSLOT < wstate["released"]:
            i = wstate["issued"]
            s = i % NSLOT
            for (shape, src, off) in wsrc(wseq[i]):
                n = shape[0] * shape[1]
                dst = ring_h[s][:, off:off + n].rearrange("p (k n) -> p k n", k=shape[0])
                P.dma("pool", dst, src, writes=[ring_t[s]])
            wstate["issued"] += 1

    def wnext(name):
        i = wstate["used"]
        assert wseq[i] == name, (wseq[i], name)
        w_pump()
        assert wstate["issued"] > i, "weight ring over-subscribed"
        wstate["used"] += 1
        s = i % NSLOT
        return ring_h[s], ring_t[s]

    def wrel(n=1):
        wstate["released"] += n
        assert wstate["released"] <= wstate["used"]
        w_pump()

    def wview(slot, k, n, off=0):
        return slot[:, off:off + k * n].rearrange("p (k n) -> p k n", k=k)

    def colgroups(C):
        return [(0, 512), (512, C - 512)] if C > 512 else [(0, C)]

    def norm(C, wcol):
        P.label = "norm"
        sqp = TempPool(arena, [C], BF16, 4)
        rs, rst = arena.alloc([C], F32)
        cgs = colgroups(C)
        prs = [psum(n) for (c0, n) in cgs]
        for c in range(8):
            sq, sqt = sqp.get()
            if c % 2 == 1:
                tt("pool", sq, hT_h[:, c, 0:C], hT_h[:, c, 0:C], ALU.mult, [hT_t[c]], sqt)
            else:
                act(sq, hT_h[:, c, 0:C], AF.Square, [hT_t[c]], sqt)
            for (pr, prt), (c0, n) in zip(prs, cgs):
                mm(pr, ONES_B, sq[:, c0:c0 + n], c == 0, c == 7, [cb_t] + sqt, [prt])
        for (pr, prt), (c0, n) in zip(prs, cgs):
            act(rs[:, c0:c0 + n], pr, AF.Ln, [prt], rst, bias=EPS, scale=1.0 / 1024)
        act(rs, rs, AF.Exp, rst, rst, scale=-0.5)
        for c in range(8):
            stt("dve", hn_h[:, c, 0:C], hT_h[:, c, 0:C], PPc(wcol + c), rs, ALU.mult, ALU.mult,
                [hT_t[c], pp_t] + rst, [hn_t[c]])

    def proj(slot, slot_t, kview, c, C, wcols=128):
        res = []
        for (c0, n) in colgroups(C):
            pr, prt = psum(n)
            for k in range(8):
                mm(pr, kview[:, k, c * 128:c * 128 + wcols], hn_h[:, k, c0:c0 + n], k == 0, k == 7,
                   [slot_t, hn_t[k]], [prt])
            res.append((pr, prt, c0, n))
        return res

    def conv_taps(xe, xet, C, K, wcol0, accp):
        acc, acct = accp.get()
        act(acc, xe[:, 0:C], AF.Copy, xet + [pp_t], acct, scale=PPc(wcol0))
        for j in range(1, K):
            stt("dve", acc, xe[:, j:j + C], PPc(wcol0 + j), acc, ALU.mult, ALU.add, xet + [pp_t] + acct, acct)
        return acc, acct

    def load_tile(ti, g0, G):
        C = 128 * G
        P.label = "load"
        xp = TempPool(arena, [1024], F32, 5)
        for g in range(G):
            gg = g0 + g
            xin, xint = xp.get()
            if gg == 0:
                memset("dve", xin, 0.0, xint)
                P.dma("sp", xin[112:128, :], meta_d[:, :], writes=xint)
            else:
                P.dma("sp", xin, x_d[(gg - 1) * 128:gg * 128, :], writes=xint)
            for half in range(2):
                pt, ptt = psum(512, F32, [4, 128])
                for cc in range(4):
                    c = half * 4 + cc
                    tr(pt[:, cc, :], xin[:, c * 128:(c + 1) * 128], MK("ID"), xint + [mk_t], [ptt])
                cp("act" if half == 0 else "dve", hT_h[:, half * 4:half * 4 + 4, g * 128:(g + 1) * 128], pt,
                   [ptt], hT_t[half * 4:half * 4 + 4])

    def store_tile(ti, g0, G):
        P.label = "store"
        xp = TempPool(arena, [1024], F32, 2)
        for g in range(G):
            gg = g0 + g
            if gg == 0:
                continue
            xo, xot = xp.get()
            for half in range(2):
                pt, ptt = psum(512, F32, [4, 128])
                for cc in range(4):
                    c = half * 4 + cc
                    tr(pt[:, cc, :], hT_h[:, c, g * 128:(g + 1) * 128], MK("ID"), [hT_t[c], mk_t], [ptt])
                cp("act" if half == 0 else "dve", xo[:, half * 512:(half + 1) * 512], pt.rearrange("p a b -> p (a b)"),
                   [ptt], xot)
            P.dma("sp", out_d[(gg - 1) * 128:gg * 128, :], xo, reads=xot, writes=[out_t])

    def out_proj(names, rhs_h, rhs_t, C):
        P.label = "outproj"
        slots = [wnext(n) for n in names]
        for m in range(8):
            slot, slot_t = slots[m // 4]
            wv_ = wview(slot, 8, 512)
            mc = m % 4
            for (c0, n) in colgroups(C):
                pr, prt = psum(n)
                for k in range(8):
                    mm(pr, wv_[:, k, mc * 128:(mc + 1) * 128], rhs_h[:, k, c0:c0 + n], k == 0, k == 7,
                       [slot_t, rhs_t[k]], [prt])
                tt("dve", hT_h[:, m, c0:c0 + n], hT_h[:, m, c0:c0 + n], pr, ALU.add, [hT_t[m], prt], [hT_t[m]])
            if m % 4 == 3:
                wrel()

    def ffn(l, C):
        arena.reset()
        norm(C, PP_FNW + l * 8)
        P.label = "ffn_up"
        actb, actt = arena.alloc([22, C], BF16, ntok=22)
        xep = TempPool(arena, [C + 2], F32, 3)
        accp = TempPool(arena, [C], F32, 3)
        slots = {}

        def ffn_chunk(j):
            jb, jc = j // 4, j % 4
            ncol = 512 if jb < 5 else 256
            last = (jc == ncol // 128 - 1)
            P.label = "ffn_up"
            if jc == 0:
                slots[("g", jb)] = wnext(("up", l, 0, jb))
                slots[("v", jb)] = wnext(("up", l, 1, jb))
            sg, sgt = slots[("g", jb)]
            sv, svt = slots[("v", jb)]
            pg = proj(sg, sgt, wview(sg, 8, ncol), jc, C)
            if last:
                wrel()
            xe, xet = xep.get()
            cp("dve", xe[:, 0:2], haloF_h[:, l, j, :], [haloF_t[l][j]], xet)
            for (pr, prt, c0, n) in pg:
                cp("act", xe[:, 2 + c0:2 + c0 + n], pr, [prt], xet)
            yield
            P.label = "ffn_up"
            cp("dve", haloF_h[:, l, j, :], xe[:, C:C + 2], xet, [haloF_t[l][j]])
            acc, acct = conv_taps(xe, xet, C, 3, PP_FC + l * 66 + j * 3, accp)
            pv = proj(sv, svt, wview(sv, 8, ncol), jc, C)
            if last:
                wrel()
            act(acc, acc, AF.Silu, acct, acct)
            yield
            P.label = "ffn_up"
            for (pr, prt, c0, n) in pv:
                tt("dve", actb[:, j, c0:c0 + n], pr, acc[:, c0:c0 + n], ALU.mult, [prt] + acct, [actt[j]])

        run_pipeline([ffn_chunk(j) for j in range(22)], 3, lag=1)
        P.label = "ffn_dn"
        cgs = colgroups(C)
        for mp in range(4):
            regs = {}
            for mi in range(2):
                for ci, (c0, n) in enumerate(cgs):
                    regs[(mi, ci)] = psum(n)
            for jh in range(2):
                sd, sdt = wnext(("dn", l, mp, jh))
                vd = wview(sd, 11, 256)
                for mi in range(2):
                    for ci, (c0, n) in enumerate(cgs):
                        pr, prt = regs[(mi, ci)]
                        for jj in range(11):
                            j = jh * 11 + jj
                            mm(pr, vd[:, jj, mi * 128:(mi + 1) * 128], actb[:, j, c0:c0 + n], j == 0, j == 21,
                               [sdt, actt[j]], [prt])
                wrel()
            for mi in range(2):
                m = mp * 2 + mi
                for ci, (c0, n) in enumerate(cgs):
                    pr, prt = regs[(mi, ci)]
                    tt("dve", hT_h[:, m, c0:c0 + n], hT_h[:, m, c0:c0 + n], pr, ALU.add, [hT_t[m], prt], [hT_t[m]])

    def even_mixer(ti, G):
        C = 128 * G
        arena.reset()
        yab, yabt = arena.alloc([8, C], BF16, ntok=8)
        qkvT, qkvt = arena.alloc([12, C], BF16, ntok=12)
        sz, szt = arena.alloc([G, 512], BF16, ntok=G)
        bg, bgt = arena.alloc([G, 16], F32, ntok=G)
        mark = arena.off
        norm(C, PP_ANW + 0)
        gip = TempPool(arena, [C], F32, 2)
        xep = TempPool(arena, [C + 3], F32, 4)
        accp = TempPool(arena, [C], F32, 4)
        sqp = TempPool(arena, [C], BF16, 3)
        slots = {}

        def conva_chunk(c):
            P.label = "convA"
            if c == 0:
                slots["gi"] = wnext(("in", 0))
                slots["go"] = wnext(("in", 1))
                slots["ah"] = wnext(("in", 2))
            s_gi, t_gi = slots["gi"]
            s_go, t_go = slots["go"]
            s_ah, t_ah = slots["ah"]
            p_gi = proj(s_gi, t_gi, wview(s_gi, 8, 512), c, C)
            p_ah = proj(s_ah, t_ah, wview(s_ah, 8, 512), c, C)
            gi, git = gip.get()
            xe, xet = xep.get()
            cp("dve", xe[:, 0:2], haloA_h[:, c, :], [haloA_t[c]], xet)
            for (pr, prt, c0, n) in p_gi:
                cp("act", gi[:, c0:c0 + n], pr, [prt], git)
            yield
            P.label = "convA"
            for (pr, prt, c0, n) in p_ah:
                tt("dve", xe[:, 2 + c0:2 + c0 + n], pr, gi[:, c0:c0 + n], ALU.mult, [prt] + git, xet)
            cp("dve", haloA_h[:, c, :], xe[:, C:C + 2], xet, [haloA_t[c]])
            p_go = proj(s_go, t_go, wview(s_go, 8, 512), c, C)
            if c == 3:
                wrel(3)
            acc, acct = conv_taps(xe, xet, C, 3, PP_CA + c * 3, accp)
            yield
            P.label = "convA"
            for (pr, prt, c0, n) in p_go:
                tt("dve", yab[:, c, c0:c0 + n], pr, acc[:, c0:c0 + n], ALU.mult, [prt] + acct, [yabt[c]])

        def qkv_chunk(cc):
            b, c = cc // 4, cc % 4
            P.label = "qkv"
            if c == 0:
                slots[("qkv", b)] = wnext(("in", 3 + b))
            sw, swt = slots[("qkv", b)]
            pq = proj(sw, swt, wview(sw, 8, 512), c, C)
            if c == 3:
                wrel()
            xe, xet = xep.get()
            cp("dve", xe[:, 0:3], haloQ_h[:, cc, :], [haloQ_t[cc]], xet)
            for (pr, prt, c0, n) in pq:
                cp("act", xe[:, 3 + c0:3 + c0 + n], pr, [prt], xet)
            yield
            P.label = "qkv"
            cp("dve", haloQ_h[:, cc, :], xe[:, C:C + 3], xet, [haloQ_t[cc]])
            acc, acct = conv_taps(xe, xet, C, 4, PP_DC + cc * 4, accp)
            if b == 2:
                act(qkvT[:, cc, :], acc, AF.Silu, acct, [qkvt[cc]])
                return
            act(qkvT[:, cc, :], acc, AF.Silu, acct, [qkvt[cc]])
            sq, sqt = sqp.get()
            act(sq, qkvT[:, cc, :], AF.Square, [qkvt[cc]], sqt)
            yield
            P.label = "qkv"
            for (c0, n) in colgroups(C):
                pr, prt = psum(n)
                mm(pr, ONES_B, sq[:, c0:c0 + n], True, True, [cb_t] + sqt, [prt])
                cp("act", ssum[:, cc, c0:c0 + n], pr, [prt], [ssumt[cc]])

        ssum, ssumt = arena.alloc([8, C], F32, ntok=8)
        run_pipeline([conva_chunk(c) for c in range(4)] + [qkv_chunk(cc) for cc in range(12)], 4, lag=1)
        P.label = "qkv"
        for half in range(2):
            hsl = slice(4 * half, 4 * half + 4)
            act(ssum[:, hsl, :], ssum[:, hsl, :], AF.Ln, ssumt[4 * half:4 * half + 4], ssumt[4 * half:4 * half + 4], bias=EPS)
            act(ssum[:, hsl, :], ssum[:, hsl, :], AF.Exp, ssumt[4 * half:4 * half + 4], ssumt[4 * half:4 * half + 4], scale=-0.5)
        for cc in range(8):
            stt("dve", qkvT[:, cc, :], qkvT[:, cc, :], (128 ** -0.5) if cc < 4 else 1.0, ssum[:, cc, :],
                ALU.mult, ALU.mult, [qkvt[cc], ssumt[cc]], [qkvt[cc]])
        P.label = "ztok"
        s_z, t_z = wnext(("in", 6))
        s_ab, t_ab = wnext(("in", "ab"))
        vz, vab = wview(s_z, 8, 512), wview(s_ab, 8, 8)
        for g in range(G):
            gc = slice(g * 128, (g + 1) * 128)
            pz, pzt = psum(512)
            for k in range(8):
                mm(pz, hn_h[:, k, gc], vz[:, k, :], k == 0, k == 7, [t_z, hn_t[k]], [pzt])
            act(sz[:, g, :], pz, AF.Silu, [pzt], [szt[g]])
        for g in range(G):
            gc = slice(g * 128, (g + 1) * 128)
            pab, pabt = psum(8)
            for k in range(8):
                mm(pab, hn_h[:, k, gc], vab[:, k, :], k == 0, k == 7, [t_ab, hn_t[k]], [pabt])
            act(bg[:, g, 0:4], pab[:, 0:4], AF.Sigmoid, [pabt], [bgt[g]])
            tt("dve", bg[:, g, 12:16], pab[:, 4:8], rp_h[:, RP_DTB:RP_DTB + 4], ALU.add, [pabt, rp_t], [bgt[g]])
        ts("dve", bg[:, :, 4:8], bg[:, :, 0:4], -1.0, None, ALU.mult, None, bgt, bgt)
        act(bg[:, :, 12:16], bg[:, :, 12:16], AF.Exp, bgt, bgt)
        act(bg[:, :, 12:16], bg[:, :, 12:16], AF.Ln, bgt, bgt, bias=1.0)
        tt("dve", bg[:, :, 8:12], bg[:, :, 12:16], rp_h[:, NEXPA:NEXPA + 4].unsqueeze(1).broadcast_to([128, G, 4]),
           ALU.mult, bgt + [rp_t], bgt)
        wrel(2)
        arena.off = mark
        bl = lambda ap: ap.unsqueeze(2).broadcast_to([128, 4, 128])

        class DnBufs:
            pass

        def dn_bufs():
            B = DnBufs()
            A4 = lambda dt: arena.alloc([4, 128], dt)
            for nm in ("nkb", "kbd", "kdec", "vb", "nkbT", "qdT", "Ybf", "qkT", "nwT", "vn", "ybt",
                       "Nm", "Mm", "Nl", "Ml", "X", "Y", "Pm", "Qm"):
                setattr(B, nm, A4(BF16))
            for nm in ("gL", "Dl", "Du", "dmTi", "edB"):
                setattr(B, nm, A4(F32))
            B.u, B.o, B.sqo, B.nwz = B.gL, B.Dl, B.Du, B.dmTi
            B.e16 = arena.alloc([16], F32)
            B.bed = arena.alloc([4], F32)
            B.ss = arena.alloc([4], F32)
            return B

        def dn_group(g, B):
            gc = slice(g * 128, (g + 1) * 128)
            beta, nbeta, gg_ = bg[:, g, 0:4], bg[:, g, 4:8], bg[:, g, 8:12]
            nkb, nkbt = B.nkb; kbd, kbdt = B.kbd; kdec, kdect = B.kdec; vb, vbt = B.vb
            nkbT, nkbTt = B.nkbT; qdT, qdTt = B.qdT; Ybf, Ybft = B.Ybf; qkT, qkTt = B.qkT
            nwT, nwTt = B.nwT; vn, vnt = B.vn; ybt, ybtt = B.ybt
            Nm, Nmt = B.Nm; Mm, Mmt = B.Mm; Nl, Nlt = B.Nl; Ml, Mlt = B.Ml
            X, Xt = B.X; Y, Yt = B.Y; Pm, Pmt = B.Pm; Qm, Qmt = B.Qm
            gL, gLt = B.gL; Dl, Dlt = B.Dl; Du, Dut = B.Du; dmTi, dmTit = B.dmTi; edB, edBt = B.edB
            u, ut = B.u; o, ot = B.o; sqo, sqot = B.sqo; nwz, nwzt = B.nwz
            e16, e16t = B.e16; bed, bedt = B.bed; ss, sst = B.ss
            P.label = "dn_prep"
            pst, pstt = psum(512, BF16, [8, 128])
            for hh in range(4):
                tr(pst[:, hh, :], qkvT[:, 4 + hh, gc], ID_B, [qkvt[4 + hh], cb_t], [pstt])
                tr(pst[:, 4 + hh, :], qkvT[:, 8 + hh, gc], ID_B, [qkvt[8 + hh], cb_t], [pstt])
            pse, pset = psum(16)
            mm(pse[:, 0:4], MK("L"), gg_, True, True, [mk_t, bgt[g]], [pset])
            mm(pse[:, 4:8], MK("U"), gg_, True, True, [mk_t, bgt[g]], [pset])
            mm(pse[:, 8:12], MK("MC0"), gg_, True, True, [mk_t, bgt[g]], [pset])
            mm(pse[:, 12:16], MK("MC1"), gg_, True, True, [mk_t, bgt[g]], [pset])
            tt("dve", gL, MK4("L"), bl(gg_), ALU.mult, [mk_t, bgt[g]], gLt)
            yield
            P.label = "dn_prep"
            act(e16, pse, AF.Exp, [pset], e16t)
            tt("dve", bed, beta, e16[:, 0:4], ALU.mult, [bgt[g]] + e16t, bedt)
            tt("dve", nkb, pst[:, 0:4, :], bl(nbeta), ALU.mult, [pstt, bgt[g]], nkbt)
            psD, psDt = psum(512, F32, [4, 128])
            for hh in range(4):
                mm(psD[:, hh, :], gL[:, hh, :], MK("ONES"), True, False, gLt + [mk_t], [psDt])
                mm(psD[:, hh, :], MK("NEGONES"), gL[:, hh, :], False, True, gLt + [mk_t], [psDt])
            psB, psBt = psum(512, F32, [4, 128])
            for hh in range(4):
                mm(psB[:, hh, :], MK("ONES"), gL[:, hh, :], True, True, gLt + [mk_t], [psBt])
            psn, psnt = psum(256, BF16, [4, 128])
            for hh in range(4):
                tr(psn[:, hh, :], nkb[:, hh, :], ID_B, nkbt + [cb_t], [psnt])
            tt("dve", kbd, pst[:, 0:4, :], bl(bed), ALU.mult, [pstt] + bedt, kbdt)
            tt("dve", kdec, pst[:, 0:4, :], bl(e16[:, 4:8]), ALU.mult, [pstt] + e16t, kdect)
            tt("dve", vb, pst[:, 4:8, :], bl(beta), ALU.mult, [pstt, bgt[g]], vbt)
            yield
            P.label = "dn_prep"
            cp("act", nkbT, psn, [psnt], nkbTt)
            tt("dve", Dl, psD, MK4("NEGSL"), ALU.add, [psDt, mk_t], Dlt)
            tt("dve", Du, psD, MK4("POSSU"), ALU.add, [psDt, mk_t], Dut)
            act(Dl, Dl, AF.Exp, Dlt, Dlt)
            act(Du, Du, AF.Exp, Dut, Dut, scale=-1.0)
            act(edB, psB, AF.Exp, [psBt], edBt)
            psN, psNt = psum(512, F32, [4, 128])
            psM, psMt = psum(512, F32, [4, 128])
            psQ, psQt = psum(512, F32, [4, 128])
            for hh in range(4):
                mm(psN[:, hh, :], nkbT[:, hh, :], qkvT[:, 4 + hh, gc], True, True, nkbTt + [qkvt[4 + hh]], [psNt])
            for hh in range(4):
                mm(psM[:, hh, :], qkvT[:, 4 + hh, gc], nkbT[:, hh, :], True, True, nkbTt + [qkvt[4 + hh]], [psMt])
            for hh in range(4):
                mm(psQ[:, hh, :], qkvT[:, 4 + hh, gc], qkvT[:, hh, gc], True, True, [qkvt[4 + hh], qkvt[hh]], [psQt])
            yield
            P.label = "dn_prep"
            tt("dve", dmTi, Du, MK4("ID"), ALU.add, Dut + [mk_t], dmTit)
            tt("dve", qdT, qkvT[:, 0:4, gc], edB, ALU.mult, qkvt[0:4] + edBt, qdTt)
            tt("dve", Nm, psN, Dl, ALU.mult, [psNt] + Dlt, Nmt)
            tt("dve", Mm, psM, Du, ALU.mult, [psMt] + Dut, Mmt)
            tt("dve", qkT, psQ, dmTi, ALU.mult, [psQt] + dmTit, qkTt)
            P.label = "dn_chain"
            tt("dve", Nl, Nm, MK4("LV0"), ALU.mult, Nmt + [mk_t], Nlt)
            tt("dve", X, Nl, MK4("ID"), ALU.add, Nlt + [mk_t], Xt)
            tt("dve", Ml, Mm, MK4("UV0"), ALU.mult, Mmt + [mk_t], Mlt)
            tt("dve", Y, Ml, MK4("ID"), ALU.add, Mlt + [mk_t], Yt)
            for l in range(1, 6):
                last = (l == 5)
                P.label = "dn_chain"
                tt("dve", Nl, Nm, MK4("LV%d" % l), ALU.mult, Nmt + [mk_t], Nlt)
                if not last:
                    tt("dve", Ml, Mm, MK4("UV%d" % l), ALU.mult, Mmt + [mk_t], Mlt)
                pQ, pQt = psum(512, F32, [4, 128])
                for hh in range(4):
                    mm(pQ[:, hh, :], Nl[:, hh, :], Y[:, hh, :], True, True, Nlt + Yt, [pQt])
                if not last:
                    pP, pPt = psum(512, F32, [4, 128])
                    for hh in range(4):
                        mm(pP[:, hh, :], Ml[:, hh, :], X[:, hh, :], True, True, Mlt + Xt, [pPt])
                yield
                P.label = "dn_chain"
                cp("act", Qm, pQ, [pQt], Qmt)
                if not last:
                    cp("act", Pm, pP, [pPt], Pmt)
                pY, pYt = psum(512, F32, [4, 128])
                for hh in range(4):
                    mm(pY[:, hh, :], X[:, hh, :], Qm[:, hh, :], True, True, Xt + Qmt, [pYt])
                if not last:
                    pX, pXt = psum(512, F32, [4, 128])
                    for hh in range(4):
                        mm(pX[:, hh, :], Y[:, hh, :], Pm[:, hh, :], True, True, Yt + Pmt, [pXt])
                yield
                P.label = "dn_chain"
                if not last:
                    tt("dve", Y, Y, pY, ALU.add, Yt + [pYt], Yt)
                    tt("dve", X, X, pX, ALU.add, Xt + [pXt], Xt)
                else:
                    tt("dve", Ybf, Y, pY, ALU.add, Yt + [pYt], Ybft)
            P.label = "dn_uw"
            pU, pUt = psum(512, F32, [4, 128])
            for hh in range(4):
                mm(pU[:, hh, :], Ybf[:, hh, :], vb[:, hh, :], True, True, Ybft + vbt, [pUt])
            pW, pWt = psum(512, F32, [4, 128])
            for hh in range(4):
                mm(pW[:, hh, :], kbd[:, hh, :], Ybf[:, hh, :], True, True, kbdt + Ybft, [pWt])
            yield
            P.label = "dn_uw"
            cp("act", u, pU, [pUt], ut)
            act(nwT, pW, AF.Copy, [pWt], nwTt, scale=-1.0)
            tt("dve", nwz, sz[:, g, :].rearrange("p (a b) -> p a b", a=4),
               rp_h[:, RP_DNW:RP_DNW + 128].unsqueeze(1).broadcast_to([128, 4, 128]), ALU.mult, [szt[g], rp_t], nwzt)
            while scan_turn[0] != g:
                yield
            for cc in range(2):
                P.label = "dn_scan"
                r = slice(64 * cc, 64 * cc + 64)
                pV, pVt = psum(512, F32, [4, 128])
                for hh in range(4):
                    mm(pV[r, hh, :], nwT[:, hh, r], Sb_h[:, hh, :], True, True, nwTt + [Sb_t], [pVt])
                yield
                P.label = "dn_scan"
                tt("dve", vn[r, :, :], pV[r, :, :], u[r, :, :], ALU.add, [pVt] + ut, vnt)
                tt("dve", S_h[:, :, :], S_h[:, :, :], bl(e16[:, 8 + 4 * cc:12 + 4 * cc]), ALU.mult, [S_t] + e16t, [S_t])
                pO, pOt = psum(512, F32, [4, 128])
                for hh in range(4):
                    mm(pO[r, hh, :], qdT[:, hh, r], Sb_h[:, hh, :], True, False, qdTt + [Sb_t], [pOt])
                    mm(pO[r, hh, :], qkT[r, hh, r], vn[r, hh, :], False, True, qkTt + vnt, [pOt])
                pS, pSt = psum(512, F32, [4, 128])
                for hh in range(4):
                    mm(pS[:, hh, :], kdec[r, hh, :], vn[r, hh, :], True, True, kdect + vnt, [pSt])
                yield
                P.label = "dn_scan"
                tt("dve", S_h[:, :, :], S_h[:, :, :], pS, ALU.add, [S_t, pSt], [S_t])
                cp("act", Sb_h[:, :, :], S_h[:, :, :], [S_t], [Sb_t])
                cp("act", o[r, :, :], pO[r, :, :], [pOt], ot)
            scan_turn[0] = g + 1
            P.label = "dn_out"
            act(sqo, o, AF.Square, ot, sqot)
            rsum(ss, sqo, sqot, sst)
            act(ss, ss, AF.Ln, sst, sst, bias=EPS, scale=1.0 / 128)
            act(ss, ss, AF.Exp, sst, sst, scale=-0.5)
            yield
            P.label = "dn_out"
            tt("dve", sqo, o, bl(ss), ALU.mult, ot + sst, sqot)
            tt("dve", ybt, sqo, nwz, ALU.mult, sqot + nwzt, ybtt)
            pT, pTt = psum(256, BF16, [4, 128])
            for hh in range(4):
                tr(pT[:, hh, :], ybt[:, hh, :], ID_B, ybtt + [cb_t], [pTt])
            yield
            P.label = "dn_out"
            cp("act", yab[:, 4:8, gc], pT, [pTt], yabt[4:8])

        sets = [dn_bufs(), dn_bufs()]
        scan_turn = [0]
        oslots = {}

        def outproj_cg(ci, first, lastcg):
            (c0, n) = colgroups(C)[ci]
            P.label = "outproj"
            if first:
                oslots[0] = wnext(("out", 0))
                oslots[1] = wnext(("out", 1))
            for m in range(8):
                P.label = "outproj"
                slot, slot_t = oslots[m // 4]
                wv_ = wview(slot, 8, 512)
                mc = m % 4
                pr, prt = psum(n)
                for k in range(8):
                    mm(pr, wv_[:, k, mc * 128:(mc + 1) * 128], yab[:, k, c0:c0 + n], k == 0, k == 7,
                       [slot_t, yabt[k]], [prt])
                tt("dve", hT_h[:, m, c0:c0 + n], hT_h[:, m, c0:c0 + n], pr, ALU.add, [hT_t[m], prt], [hT_t[m]])
                if lastcg and m % 4 == 3:
                    wrel()
                yield

        gens = [dn_group(g, sets[g % 2]) for g in range(G)]
        if G == 5:
            gens.append(outproj_cg(0, True, False))
        run_pipeline(gens, cfg.get("dn_depth", 2), lag=8)
        if G == 5:
            for _ in outproj_cg(1, False, True):
                pass
        else:
            for _ in outproj_cg(0, True, True):
                pass
        return
        out_proj([("out", 0), ("out", 1)], yab, yabt, C)

    def swa_mixer(ti, G):
        C = 128 * G
        arena.reset()
        norm(C, PP_ANW + 8)
        qT, qTt = arena.alloc([8, C], BF16, ntok=8)
        OT, OTt = arena.alloc([8, C], BF16, ntok=8)
        qrp = TempPool(arena, [C], F32, 4)
        sqp = TempPool(arena, [C], BF16, 4)
        rsp = TempPool(arena, [C], F32, 4)
        Otp = TempPool(arena, [16, 64], BF16, 2)
        Ecp = TempPool(arena, [4, 128], BF16, 5)
        Emp = TempPool(arena, [4, 128], BF16, 5)
        Epp = TempPool(arena, [4, 128], BF16, 5)
        denp = TempPool(arena, [4], F32, 6)
        kTt_, kTtt = arena.alloc([2, C], BF16, ntok=2)
        P.label = "swa_proj"
        sq_slots = [wnext(("q", 0)), wnext(("q", 1))]
        s_kv, t_kv = wnext(("kv",))
        vk = wview(s_kv, 8, 256)

        def head_chunk(kind, c):
            P.label = "swa_proj"
            if kind == "q":
                slot, slot_t = sq_slots[c // 4]
                pq = proj(slot, slot_t, wview(slot, 8, 512), c % 4, C)
                if c % 4 == 3:
                    wrel()
                sccol = QSC
            else:
                pq = proj(s_kv, t_kv, vk, c, C)
                sccol = KSC
            qraw, qrawt = qrp.get()
            sq, sqt = sqp.get()
            for (pr, prt, c0, n) in pq:
                cp("act", qraw[:, c0:c0 + n], pr, [prt], qrawt)
            act(sq, qraw, AF.Square, qrawt, sqt)
            yield
            P.label = "swa_proj"
            rs, rst = rsp.get()
            for (c0, n) in colgroups(C):
                pr2, pr2t = psum(n)
                mm(pr2, BLK_B, sq[:, c0:c0 + n], True, True, [cb_t] + sqt, [pr2t])
                act(rs[:, c0:c0 + n], pr2, AF.Ln, [pr2t], rst, bias=EPS, scale=1.0 / 64)
            act(rs, rs, AF.Exp, rst, rst, scale=-0.5)
            yield
            P.label = "swa_proj"
            if kind == "q":
                dst, dstt = qT[:, c, :], [qTt[c]]
            else:
                dst, dstt = kTt_[:, c, :], [kTtt[c]]
            stt("dve", dst, qraw, PPc(sccol), rs, ALU.mult, ALU.mult, qrawt + rst + [pp_t], dstt)
            if kind == "k":
                ck = c
                for hk in range(2):
                    kh = 2 * ck + hk
                    rows = slice(64 * hk, 64 * hk + 64)
                    orows = slice(64 * (1 - hk), 64 * (1 - hk) + 64)
                    cp("act", kZ_h[rows, hk * 4 + kh, 128:128 + C], kTt_[rows, ck, :], [kTtt[ck]], [kZ_t[hk * 4 + kh]])
                    P.dma("sp", kZ_h[orows, (1 - hk) * 4 + kh, 128:128 + C], kTt_[rows, ck, :], reads=[kTtt[ck]],
                          writes=[kZ_t[(1 - hk) * 4 + kh]])

        run_pipeline([head_chunk("k", 0), head_chunk("k", 1)] + [head_chunk("q", c) for c in range(8)], 4, lag=1)
        vvw = wview(s_kv, 8, 256, off=2048)
        for g in range(G):
            gc = slice(g * 128, (g + 1) * 128)
            pv, pvt = psum(256, F32, [4, 64])
            for k in range(8):
                mm(pv, hn_h[:, k, gc], vvw[:, k, :], k == 0, k == 7, [t_kv, hn_t[k]], [pvt])
            cp("act", Vb_h[:, 1 + g, :, 0:64], pv, [pvt], [Vb_t[1 + g]])
        wrel()
        if ti == 0:
            cp("dve", kZm_h[:, :, :], kZ_h[:, :, 128 + 96:128 + 128], kZ_t, [kZm_t])
            P.dma("sp", Vm_h[0:32, :, :], Vb_h[96:128, 1, :, :], reads=[Vb_t[1]], writes=[Vm_t])
        otoks = {}

        def attn_unit(g, kh):
            gc = slice(g * 128, (g + 1) * 128)
            is_meta = (ti == 0 and g == 0)
            has_prev = not (ti == 0 and g <= 1)
            P.label = "swa_attn"
            if kh == 0:
                otoks[g] = Otp.get()
            Otok, Otokt = otoks[g]
            if not is_meta:
                pM, pMt = psum(512, F32, [4, 128])
                if has_prev:
                    pP, pPt = psum(512, F32, [4, 128])
            pC, pCt = psum(512, F32, [4, 128])
            for hq in range(2):
                zi = hq * 4 + kh
                q_ap = qT[:, 2 * kh:2 * kh + 2, gc]
                qtk = [qTt[2 * kh], qTt[2 * kh + 1]]
                if not is_meta:
                    mm(pM[0:32, hq:4:2, :], kZm_h[:, zi, :], q_ap, True, True, [kZm_t] + qtk, [pMt])
                    if has_prev:
                        mm(pP[:, hq:4:2, :], kZ_h[:, zi, 128 * g:128 * g + 128], q_ap, True, True,
                           [kZ_t[zi]] + qtk, [pPt])
                mm(pC[:, hq:4:2, :], kZ_h[:, zi, 128 * (g + 1):128 * (g + 1) + 128], q_ap, True, True,
                   [kZ_t[zi]] + qtk, [pCt])
            yield
            P.label = "swa_attn"
            Ec, Ect = Ecp.get()
            act(Ec, pC, AF.Exp, [pCt], Ect)
            tt("dve", Ec, Ec, MK4("SWAMETA" if is_meta else "SWACUR"), ALU.mult, Ect + [mk_t], Ect)
            if not is_meta:
                Em, Emt = Emp.get()
                act(Em[0:32, :, :], pM[0:32, :, :], AF.Exp, [pMt], Emt)
                tt("dve", Em[0:32, :, :], Em[0:32, :, :],
                   mk_h[0:32, MI["METAK"]:MI["METAK"] + 1, :].broadcast_to([32, 4, 128]),
                   ALU.mult, Emt + [mk_t], Emt)
                if has_prev:
                    Ep, Ept = Epp.get()
                    act(Ep, pP, AF.Exp, [pPt], Ept)
                    tt("dve", Ep, Ep, MK4("SWAPREV"), ALU.mult, Ept + [mk_t], Ept)
            yield
            P.label = "swa_attn"
            pO, pOt = psum(512, F32, [4, 128])
            for gq in range(4):
                first = True
                if not is_meta:
                    mm(pO[:, gq, 0:65], Em[0:32, gq, :], Vm_h[0:32, kh, :], True, False, Emt + [Vm_t], [pOt])
                    first = False
                    if has_prev:
                        mm(pO[:, gq, 0:65], Ep[:, gq, :], Vb_h[:, g, kh, :], False, False, Ept + [Vb_t[g]], [pOt])
                mm(pO[:, gq, 0:65], Ec[:, gq, :], Vb_h[:, g + 1, kh, :], first, True, Ect + [Vb_t[g + 1]], [pOt])
            yield
            P.label = "swa_attn"
            den, dent = denp.get()
            tt("dve", den, pO[:, :, 64], rp_h[:, RP_SINK + 4 * kh:RP_SINK + 4 * kh + 4], ALU.add, [pOt, rp_t], dent)
            recip(den, den, dent, dent)
            tt("dve", Otok[:, 4 * kh:4 * kh + 4, :], pO[:, :, 0:64], den.unsqueeze(2).broadcast_to([128, 4, 64]),
               ALU.mult, [pOt] + dent, Otokt)
            if kh == 3:
                yield
                P.label = "swa_attn"
                pT, pTt = psum(512, BF16, [8, 128])
                Of = Otok.rearrange("p a b -> p (a b)")
                for c in range(8):
                    tr(pT[:, c, :], Of[:, c * 128:(c + 1) * 128], ID_B, Otokt + [cb_t], [pTt])
                yield
                P.label = "swa_attn"
                cp("act", OT[:, :, gc], pT, [pTt], OTt)

        run_pipeline([attn_unit(g, kh) for g in range(G) for kh in range(4)], 4, lag=1)
        if cfg.get("swa_dbg") == "noout":
            wnext(("o", 0)); wnext(("o", 1)); wrel(2)
        else:
            if ti == 0:
                dump("hpre", hT_h[:, 0, 128:256], hT_t)
            out_proj([("o", 0), ("o", 1)], OT, OTt, C)
            if ti == 0:
                dump("hpost", hT_h[:, 0, 128:256], hT_t)
        cp("dve", kZ_h[:, :, 0:128], kZ_h[:, :, C:C + 128], kZ_t, kZ_t)
        cp("dve", Vb_h[:, 0, :, 0:64], Vb_h[:, G, :, 0:64], [Vb_t[G]], [Vb_t[0]])

    out_t = P.tok()
    g0 = 0
    for ti, G in enumerate(GROUPS_PER_TILE):
        if ti >= cfg.get("ntiles", 99):
            break
        C = 128 * G
        arena.reset()
        load_tile(ti, g0, G)
        if cfg["mix0"]:
            even_mixer(ti, G)
        if cfg["ffn0"]:
            ffn(0, C)
        if cfg["mix1"]:
            swa_mixer(ti, G)
        if cfg["ffn1"]:
            ffn(1, C)
        arena.reset()
        store_tile(ti, g0, G)
        g0 += G
    P.wait_all("sp", [out_t, dbg_out_t])
    stats = P.emit()
    stats["dbg"] = dbg_state["names"]
    build_program.last_P = P
    return nc, stats


FULL_CFG = dict(mix0=True, ffn0=True, mix1=True, ffn1=True)
_CACHE = {}


def host_params(inp):
    f = lambda a: np.asarray(a, np.float32)
    pp = np.zeros((128, NPP), np.float32)
    for l in range(2):
        pp[:, PP_ANW + l * 8:PP_ANW + l * 8 + 8] = f(inp["attn_norm_w"])[l].reshape(8, 128).T
        pp[:, PP_FNW + l * 8:PP_FNW + l * 8 + 8] = f(inp["ffn_norm_w"])[l].reshape(8, 128).T
        fc = f(inp["ffn_conv_w"])[l].reshape(3, 22, 128)
        pp[:, PP_FC + l * 66:PP_FC + l * 66 + 66] = fc.transpose(2, 1, 0).reshape(128, 66)
    ca = f(inp["conv_a_w"])[0].reshape(3, 4, 128)
    pp[:, PP_CA:PP_CA + 12] = ca.transpose(2, 1, 0).reshape(128, 12)
    dc = f(inp["dn_conv_w"])[0].reshape(4, 12, 128)
    pp[:, PP_DC:PP_DC + 48] = dc.transpose(2, 1, 0).reshape(128, 48)
    pp[:, PP_QN] = np.tile(f(inp["swa_q_norm_w"])[0], 2)
    pp[:, PP_KN] = np.tile(f(inp["swa_k_norm_w"])[0], 2)
    rp = np.zeros((128, NRP), np.float32)
    rp[:, RP_DTB:RP_DTB + 4] = f(inp["dn_dt_bias"])[0][None, :]
    rp[:, RP_ALOG:RP_ALOG + 4] = f(inp["dn_a_log"])[0][None, :]
    rp[:, RP_DNW:RP_DNW + 128] = f(inp["dn_norm_w"])[0][None, :]
    rp[:, RP_SINK:RP_SINK + 16] = f(inp["swa_sinks"])[0][None, :]
    return pp, rp


def run(inp, cfg, built=None):
    if built is None:
        key = tuple(sorted((k, str(v)) for k, v in cfg.items()))
        if key not in _CACHE:
            _CACHE[key] = build_program(cfg)
        built = _CACHE[key]
    nc, stats = built
    f = lambda a: np.ascontiguousarray(np.asarray(a, np.float32))
    pp, rp = host_params(inp)
    masks = make_masks()
    shared = dict(meta=f(inp["meta_tokens"]), w_in=f(inp["mix_w_in"])[0], w_out=f(inp["mix_w_out"])[0],
                  wq=f(inp["swa_wq"])[0], wk=f(inp["swa_wk"])[0], wv=f(inp["swa_wv"])[0], wo=f(inp["swa_wo"])[0],
                  w_up=f(inp["ffn_w_up"]), w_dn=f(inp["ffn_w_down"]), pp=pp, rp=rp, masks=masks)
    x = f(inp["x"])
    in_maps = [dict(shared, x=x[b]) for b in range(8)]
    res = run_bass_kernel_spmd(nc, in_maps, core_ids=list(range(8)))
    if cfg.get("dbg"):
        run.dbg = [np.asarray(r["dbg"]) for r in res.results]
    return np.stack([np.asarray(r["out"], np.float32) for r in res.results], 0)


def kernel(**inputs):
    return run(inputs, FULL_CFG)
```

```python
import contextlib
import numpy as np
import concourse.bass as bass
import concourse.mybir as mybir
from concourse.bass_utils import run_bass_kernel_spmd

F32 = mybir.dt.float32
BF16 = mybir.dt.bfloat16
AF = mybir.ActivationFunctionType
ALU = mybir.AluOpType
AX = mybir.AxisListType

EPS = 1e-6
GROUPS_PER_TILE = [5, 5, 5, 5, 5, 4, 4]
CMAX = 640
NSLOT = 6
MASK_NAMES = ["ID", "ONES", "NEGONES", "BLK64", "L", "U", "MC0", "MC1", "NEGSL", "POSSU",
              "LV0", "LV1", "LV2", "LV3", "LV4", "LV5", "UV0", "UV1", "UV2", "UV3", "UV4", "UV5",
              "SWAPREV", "SWACUR", "SWAMETA", "METAK"]
MI = {n: i for i, n in enumerate(MASK_NAMES)}
PP_ANW, PP_FNW, PP_CA, PP_DC, PP_FC, PP_QN, PP_KN, NPP = 0, 16, 32, 44, 92, 224, 225, 226
RP_DTB, RP_ALOG, RP_DNW, RP_SINK, NRP = 0, 4, 8, 136, 152


class Tok:
    __slots__ = ("w", "r", "x")

    def __init__(self, x=False):
        self.w = None
        self.r = []
        self.x = x


class Op:
    __slots__ = ("eng", "fn", "dma", "waits", "need_inc", "pos", "snap", "sem", "semval", "fs")


class Prog:
    ENGS = ("pe", "act", "dve", "pool", "sp")
    NDMASEM = 12

    def __init__(self, nc):
        self.nc = nc
        self.stack = contextlib.ExitStack()
        self.e = {"pe": nc.tensor, "act": nc.scalar, "dve": nc.vector, "pool": nc.gpsimd, "sp": nc.sync}
        self.ops = []
        self.label = ""
        self.labels = []
        self.npos = {e: 0 for e in self.ENGS}
        self.known = {e: {f: -1 for f in self.ENGS} for e in self.ENGS}
        self.known_dma = {e: set() for e in self.ENGS}
        self.sem = {e: self.stack.enter_context(nc.semaphore("s_" + e)) for e in self.ENGS}
        self.use_scopes = False
        self.dsem, self.dsem_use, self.dsem_last, self.dsem_rr = {}, {}, {}, {}
        for e in ("sp", "pool"):
            self.dsem[e] = [self.stack.enter_context(nc.semaphore("d_%s%d" % (e, i))) for i in range(self.NDMASEM)]
            self.dsem_use[e] = [0] * self.NDMASEM
            self.dsem_last[e] = [None] * self.NDMASEM
            self.dsem_rr[e] = 0

    def sb(self, name, shape, dtype):
        return self.stack.enter_context(self.nc.sbuf_tensor(name, list(shape), dtype))

    def ps(self, name, shape, dtype):
        return self.stack.enter_context(self.nc.psum_tensor(name, list(shape), dtype))

    @staticmethod
    def tok(n=None):
        if n is None:
            return Tok()
        return [Tok() for _ in range(n)]

    def add(self, eng, fn, reads=(), writes=(), dma=False, track=True, fs=1 << 30):
        op = Op()
        opid = len(self.ops)
        op.eng, op.fn, op.dma = eng, fn, dma
        op.fs = fs
        op.need_inc = False
        op.sem = None
        op.semval = 0
        deps = set()
        xr = [t for t in reads if t.x]
        if xr:
            reads = [t for t in reads if not t.x]
            writes = list(writes) + xr
        for t in reads:
            if t.w is not None:
                deps.add(t.w)
        for t in writes:
            if t.w is not None:
                deps.add(t.w)
            deps.update(t.r)
        if dma:
            pool = self.dsem[eng]
            i = self.dsem_rr[eng]
            self.dsem_rr[eng] = (i + 1) % len(pool)
            if self.dsem_last[eng][i] is not None:
                deps.add(self.dsem_last[eng][i])
            self.dsem_use[eng][i] += 1
            op.sem = pool[i]
            op.semval = 16 * self.dsem_use[eng][i]
            self.dsem_last[eng][i] = opid
        known = self.known[eng]
        kd = self.known_dma[eng]
        cw = {}
        waits = []
        for d in deps:
            dop = self.ops[d]
            if dop.dma:
                if d in kd:
                    continue
                kd.add(d)
                waits.append(d)
            else:
                if dop.eng == eng:
                    if not dma and (eng == "pe" or (dop.fs >= 256 and fs >= 256)):
                        continue
                    key = "self_" + eng
                else:
                    key = dop.eng
                if known.get(key, -1) >= dop.pos:
                    continue
                if key not in cw or self.ops[cw[key]].pos < dop.pos:
                    cw[key] = d
        for key, d in cw.items():
            self.ops[d].need_inc = True
            waits.append(d)
        for d in waits:
            dop = self.ops[d]
            for f, v in dop.snap.items():
                if known.get(f, -1) < v:
                    known[f] = v
            if not dop.dma:
                key = dop.eng if dop.eng != eng else "self_" + eng
                if known.get(key, -1) < dop.pos:
                    known[key] = dop.pos
        op.waits = waits
        self.labels.append(self.label)
        op.pos = self.npos[eng]
        if fn is not None:
            self.npos[eng] += 1
        snap = {f: v for f, v in known.items() if not f.startswith("self_")}
        if not dma and fn is not None:
            snap[eng] = op.pos
        op.snap = snap
        self.ops.append(op)
        if not track:
            return opid
        for t in reads:
            t.r.append(opid)
        for t in writes:
            t.w = opid
            t.r = []
        return opid

    def dma(self, eng, out, in_, reads=(), writes=()):
        e = self.e[eng]
        return self.add(eng, lambda: e.dma_start(out=out, in_=in_), reads, writes, dma=True)

    def wait_all(self, eng, toks):
        return self.add(eng, None, reads=(), writes=toks, track=False)

    def emit(self):
        cnt = {e: 0 for e in self.ENGS}
        val = {}
        for i, op in enumerate(self.ops):
            if op.need_inc:
                cnt[op.eng] += 1
                val[i] = cnt[op.eng]
        nw = 0
        cur = None
        scope = None
        for i, op in enumerate(self.ops):
            e = self.e[op.eng]
            if self.use_scopes and self.labels[i] != cur:
                if scope is not None:
                    self.nc.leave_named_scope(cur, scope, False)
                cur = self.labels[i]
                scope = self.nc.enter_named_scope(cur, False)[0]
            for d in op.waits:
                dop = self.ops[d]
                if dop.dma:
                    e.wait_ge(dop.sem, dop.semval)
                else:
                    e.wait_ge(self.sem[dop.eng], val[d])
                nw += 1
            if op.fn is None:
                continue
            ins = op.fn()
            if op.dma:
                ins.then_inc(op.sem, 16)
            elif op.need_inc:
                ins.then_inc(self.sem[op.eng], 1)
        if scope is not None:
            self.nc.leave_named_scope(cur, scope, False)
        self.stats = dict(n_ops=len(self.ops), n_waits=nw, incs=dict(cnt), pos=dict(self.npos))
        return self.stats


class Arena:
    def __init__(self, handle, nwords):
        self.h = handle
        self.n = nwords
        self.off = 0
        self.live = []

    def reset(self):
        self.off = 0

    def alloc(self, free_shape, dtype, ntok=1, parts=128):
        nel = int(np.prod(free_shape))
        words = nel if dtype == F32 else (nel + 1) // 2
        assert self.off + words <= self.n, ("arena overflow", self.off, words, self.n)
        s, e = self.off, self.off + words
        self.off = e
        ap = self.h[:, s:e]
        if dtype != F32:
            ap = ap.bitcast(dtype)
        if len(free_shape) == 2:
            ap = ap.rearrange("p (a b) -> p a b", a=free_shape[0])
        elif len(free_shape) == 3:
            ap = ap.rearrange("p (a b c) -> p a b c", a=free_shape[0], b=free_shape[1])
        toks = [Tok() for _ in range(ntok)]
        inh = []
        keep = []
        for (os_, oe, otoks) in self.live:
            if os_ < e and s < oe:
                for t in otoks:
                    if t.w is not None:
                        inh.append(t.w)
                    inh.extend(t.r)
                if s <= os_ and oe <= e:
                    continue
            keep.append((os_, oe, otoks))
        inh = sorted(set(inh))
        for t in toks:
            t.r = list(inh)
        keep.append((s, e, toks))
        self.live = keep
        return ap, toks


def run_pipeline(gens, depth, lag=0):
    gens = list(gens)
    active = []
    nxt = 0
    while True:
        while len(active) < depth and nxt < len(gens) and (not active or active[-1][1] >= lag):
            active.append([gens[nxt], 0])
            nxt += 1
        if not active:
            break
        for a in list(active):
            try:
                next(a[0])
                a[1] += 1
            except StopIteration:
                active.remove(a)


class TempPool:
    def __init__(self, arena, free_shape, dtype, n=2):
        self.bufs = [arena.alloc(free_shape, dtype) for _ in range(n)]
        self.i = 0

    def get(self):
        b = self.bufs[self.i % len(self.bufs)]
        self.i += 1
        return b


def make_masks():
    i = np.arange(128)[:, None]
    j = np.arange(128)[None, :]
    same = (i // 64) == (j // 64)
    m = {}
    m["ID"] = (i == j)
    m["ONES"] = np.ones((128, 128), bool)
    m["NEGONES"] = -np.ones((128, 128), np.float32)
    m["BLK64"] = same
    m["L"] = same & (i <= j)
    m["U"] = same & (i > j)
    m["MC0"] = (i < 64) & (j >= 0)
    m["MC1"] = (i >= 64) & (j >= 0)
    m["NEGSL"] = np.where(same & (i > j), 0.0, -30000.0)
    m["POSSU"] = np.where(same & (j > i), 0.0, 30000.0)
    for l in range(6):
        b = 1 << l
        lv = ((i // (2 * b)) == (j // (2 * b))) & ((i % (2 * b)) >= b) & ((j % (2 * b)) < b)
        m["LV%d" % l] = lv
        m["UV%d" % l] = lv.T
    m["SWAPREV"] = (i > j)
    m["SWACUR"] = (i <= j)
    m["SWAMETA"] = (i >= 112) & (i <= j)
    m["METAK"] = (i >= 16) & (i < 32) & (j >= 0)
    out = np.zeros((128, len(MASK_NAMES), 128), np.float32)
    for n, k in MI.items():
        out[:, k, :] = m[n].astype(np.float32)
    return out.reshape(128, len(MASK_NAMES) * 128)


def build_program(cfg):
    nc = bass.Bass("TRN2", target_bir_lowering=False)
    P = Prog(nc)
    P.use_scopes = bool(cfg.get("scopes"))

    def dram(name, shape, kind="ExternalInput"):
        return nc.dram_tensor(name, list(shape), F32, kind=kind).ap()

    x_d = dram("x", [4096, 1024])
    meta_d = dram("meta", [16, 1024])
    win_d = dram("w_in", [1024, 3592])
    wout_d = dram("w_out", [1024, 1024])
    wq_d = dram("wq", [1024, 1024])
    wk_d = dram("wk", [1024, 256])
    wv_d = dram("wv", [1024, 256])
    wo_d = dram("wo", [1024, 1024])
    wup_d = dram("w_up", [2, 1024, 5632])
    wdn_d = dram("w_dn", [2, 2816, 1024])
    pp_d = dram("pp", [128, NPP])
    rp_d = dram("rp", [128, NRP])
    mk_d = dram("masks", [128, len(MASK_NAMES) * 128])
    out_d = dram("out", [4096, 1024], kind="ExternalOutput")
    dbg_d = dram("dbg", [128, 8192], kind="ExternalOutput") if cfg.get("dbg") else None
    dbg_state = dict(off=0, names=[])

    NM = len(MASK_NAMES)
    mk_h = P.sb("mk_sb", [128, NM, 128], F32)
    mk_t = P.tok()
    MK = lambda n: mk_h[:, MI[n], :]
    MK4 = lambda n: mk_h[:, MI[n]:MI[n] + 1, :].broadcast_to([128, 4, 128])
    cb_h = P.sb("cb", [128, 3, 128], BF16)
    cb_t = P.tok()
    ID_B, ONES_B, BLK_B = cb_h[:, 0, :], cb_h[:, 1, :], cb_h[:, 2, :]
    pp_h = P.sb("pp_sb", [128, NPP + 4], F32)
    pp_t = P.tok()
    rp_h = P.sb("rp_sb", [128, NRP + 8], F32)
    rp_t = P.tok()
    PPc = lambda c: pp_h[:, c:c + 1]
    hT_h = P.sb("hT", [128, 8, CMAX], F32)
    hT_t = P.tok(8)
    hn_h = P.sb("hn", [128, 8, CMAX], BF16)
    hn_t = P.tok(8)
    ring_h = [P.sb("ring%d" % i, [128, 4096], BF16) for i in range(NSLOT)]
    ring_t = P.tok(NSLOT)
    S_h = P.sb("S", [128, 4, 128], F32)
    S_t = P.tok()
    Sb_h = P.sb("Sb", [128, 4, 128], BF16)
    Sb_t = P.tok()
    haloA_h = P.sb("haloA", [128, 4, 2], F32)
    haloA_t = P.tok(4)
    haloQ_h = P.sb("haloQ", [128, 12, 3], F32)
    haloQ_t = P.tok(12)
    haloF_h = P.sb("haloF", [128, 2, 22, 2], F32)
    haloF_t = [P.tok(22), P.tok(22)]
    kZ_h = P.sb("kZ", [128, 8, 128 + CMAX], BF16)
    kZ_t = P.tok(8)
    kZm_h = P.sb("kZm", [128, 8, 32], BF16)
    kZm_t = P.tok()
    Vb_h = P.sb("Vb", [128, 6, 4, 65], BF16)
    Vb_t = P.tok(6)
    Vm_h = P.sb("Vm", [32, 4, 65], BF16)
    Vm_t = P.tok()
    arena = Arena(P.sb("arena", [128, 23800], F32), 23800)
    bank_h = [P.ps("bank%d" % i, [128, 512], F32) for i in range(8)]
    bank_t = [Tok(x=True) for _ in range(8)]
    bank_rr = [0]

    def psum(n, dtype=F32, shape=None):
        i = bank_rr[0]
        bank_rr[0] = (i + 1) % 8
        ap = bank_h[i][:, 0:n]
        if dtype != F32:
            ap = ap.bitcast(dtype)
        if shape is not None and len(shape) == 2:
            ap = ap.rearrange("p (a b) -> p a b", a=shape[0])
        return ap, bank_t[i]

    def mm(out, lhsT, rhs, start, stop, reads, writes):
        P.add("pe", lambda: nc.tensor.matmul(out, lhsT, rhs, start=start, stop=stop), reads, writes)

    def tr(out, in_, ident, reads, writes):
        P.add("pe", lambda: nc.tensor.transpose(out, in_, ident), reads, writes)

    def fsz(ap):
        return int(np.prod(ap.shape[1:]))

    def act(out, in_, func, reads, writes, bias=0.0, scale=1.0):
        P.add("act", lambda: nc.scalar.activation(out=out, in_=in_, func=func, bias=bias, scale=scale), reads, writes,
              fs=fsz(out))

    def tt(eng, out, in0, in1, op, reads, writes):
        e = P.e[eng]
        P.add(eng, lambda: e.tensor_tensor(out, in0, in1, op), reads, writes, fs=fsz(out))

    def ts(eng, out, in0, s1, s2, op0, op1, reads, writes):
        e = P.e[eng]
        if op1 is None:
            P.add(eng, lambda: e.tensor_scalar(out, in0, s1, s2, op0), reads, writes, fs=fsz(out))
        else:
            P.add(eng, lambda: e.tensor_scalar(out, in0, s1, s2, op0, op1), reads, writes, fs=fsz(out))

    def stt(eng, out, in0, scalar, in1, op0, op1, reads, writes):
        e = P.e[eng]
        P.add(eng, lambda: e.scalar_tensor_tensor(out, in0, scalar, in1, op0, op1), reads, writes, fs=fsz(out))

    def cp(eng, out, in_, reads, writes):
        e = P.e[eng]
        if eng == "act":
            P.add(eng, lambda: e.copy(out, in_), reads, writes, fs=fsz(out))
        else:
            P.add(eng, lambda: e.tensor_copy(out, in_), reads, writes, fs=fsz(out))

    def memset(eng, ap, v, writes):
        e = P.e[eng]
        P.add(eng, lambda: e.memset(ap, v), (), writes, fs=fsz(ap))

    def recip(out, in_, reads, writes):
        P.add("dve", lambda: nc.vector.reciprocal(out, in_), reads, writes, fs=fsz(out))

    def rsum(out, in_, reads, writes):
        P.add("dve", lambda: nc.vector.reduce_sum(out, in_, AX.X), reads, writes, fs=fsz(out))

    dbg_stage = P.sb("dbg_stage", [128, 1024], F32) if cfg.get("dbg") else None
    dbg_stage_t = P.tok()
    dbg_out_t = P.tok()

    def dump(name, ap2d, toks, parts=128):
        if not cfg.get("dbg") or name not in cfg["dbg"]:
            return
        n = ap2d.shape[1]
        off = dbg_state["off"]
        dbg_state["off"] += n
        dbg_state["names"].append((name, off, n, parts))
        cp("dve", dbg_stage[0:parts, 0:n], ap2d, toks, [dbg_stage_t])
        P.dma("sp", dbg_d[0:parts, off:off + n], dbg_stage[0:parts, 0:n], reads=[dbg_stage_t], writes=[dbg_out_t])

    P.dma("sp", mk_h[:, :, :], mk_d.rearrange("p (m j) -> p m j", m=NM), writes=[mk_t])
    P.dma("sp", pp_h[:, 0:NPP], pp_d[:, :], writes=[pp_t])
    P.dma("sp", rp_h[:, 0:NRP], rp_d[:, :], writes=[rp_t])
    for i, n in enumerate(["ID", "ONES", "BLK64"]):
        P.dma("pool", cb_h[:, i, :], mk_d[:, MI[n] * 128:(MI[n] + 1) * 128], writes=[cb_t])
    QSC, KSC = NPP, NPP + 1
    ts("dve", pp_h[:, QSC:QSC + 1], pp_h[:, PP_QN:PP_QN + 1], 0.125, None, ALU.mult, None, [pp_t], [pp_t])
    cp("dve", pp_h[:, KSC:KSC + 1], pp_h[:, PP_KN:PP_KN + 1], [pp_t], [pp_t])
    NEXPA = NRP
    act(rp_h[:, NEXPA:NEXPA + 4], rp_h[:, RP_ALOG:RP_ALOG + 4], AF.Exp, [rp_t], [rp_t])
    ts("dve", rp_h[:, NEXPA:NEXPA + 4], rp_h[:, NEXPA:NEXPA + 4], -1.0, None, ALU.mult, None, [rp_t], [rp_t])
    act(rp_h[:, RP_SINK:RP_SINK + 16], rp_h[:, RP_SINK:RP_SINK + 16], AF.Exp, [rp_t], [rp_t])
    memset("dve", S_h[:, :, :], 0.0, [S_t])
    memset("dve", Sb_h[:, :, :], 0.0, [Sb_t])
    memset("dve", haloA_h[:, :, :], 0.0, haloA_t)
    memset("dve", haloQ_h[:, :, :], 0.0, haloQ_t)
    memset("dve", haloF_h[:, :, :, :], 0.0, haloF_t[0] + haloF_t[1])
    memset("dve", Vb_h[:, :, :, :], 1.0, Vb_t)
    memset("dve", kZ_h[:, :, :], 0.0, kZ_t)

    def wsrc(name):
        kind = name[0]
        if kind == "in":
            b = name[1]
            if b == "ab":
                return [((8, 8), win_d[:, 3584:3592].rearrange("(k p) n -> p k n", p=128), 0)]
            return [((8, 512), win_d[:, b * 512:(b + 1) * 512].rearrange("(k p) n -> p k n", p=128), 0)]
        if kind in ("out", "q", "o"):
            src = {"out": wout_d, "q": wq_d, "o": wo_d}[kind]
            b = name[1]
            return [((8, 512), src[:, b * 512:(b + 1) * 512].rearrange("(k p) n -> p k n", p=128), 0)]
        if kind == "kv":
            return [((8, 256), wk_d[:, :].rearrange("(k p) n -> p k n", p=128), 0),
                    ((8, 256), wv_d[:, :].rearrange("(k p) n -> p k n", p=128), 2048)]
        if kind == "up":
            l, half, jb = name[1], name[2], name[3]
            n = 512 if jb < 5 else 256
            c0 = half * 2816 + jb * 512
            return [((8, n), wup_d[l, :, c0:c0 + n].rearrange("(k p) n -> p k n", p=128), 0)]
        if kind == "dn":
            l, mp, jh = name[1], name[2], name[3]
            return [((11, 256), wdn_d[l, jh * 1408:(jh + 1) * 1408, mp * 256:(mp + 1) * 256].rearrange("(j p) n -> p j n", p=128), 0)]
        raise ValueError(name)

    def tile_blocks():
        seq = []
        if cfg["mix0"]:
            seq += [("in", b) for b in range(7)] + [("in", "ab"), ("out", 0), ("out", 1)]
        if cfg["ffn0"]:
            for jb in range(6):
                seq += [("up", 0, 0, jb), ("up", 0, 1, jb)]
            seq += [("dn", 0, mp, jh) for mp in range(4) for jh in range(2)]
        if cfg["mix1"]:
            seq += [("q", 0), ("q", 1), ("kv",), ("o", 0), ("o", 1)]
        if cfg["ffn1"]:
            for jb in range(6):
                seq += [("up", 1, 0, jb), ("up", 1, 1, jb)]
            seq += [("dn", 1, mp, jh) for mp in range(4) for jh in range(2)]
        return seq

    wseq = []
    for _ in GROUPS_PER_TILE:
        wseq += tile_blocks()
    wstate = dict(issued=0, used=0, released=0)

    def w_pump():
        while wstate["issued"] < len(wseq) and wstate["issued"] - NSLOT < wstate["released"]:
            i = wstate["issued"]
            s = i % NSLOT
            for (shape, src, off) in wsrc(wseq[i]):
                n = shape[0] * shape[1]
                dst = ring_h[s][:, off:off + n].rearrange("p (k n) -> p k n", k=shape[0])
                P.dma("pool", dst, src, writes=[ring_t[s]])
            wstate["issued"] += 1

    def wnext(name):
        i = wstate["used"]
        assert wseq[i] == name, (wseq[i], name)
        w_pump()
        assert wstate["issued"] > i, "weight ring over-subscribed"
        wstate["used"] += 1
        s = i % NSLOT
        return ring_h[s], ring_t[s]

    def wrel(n=1):
        wstate["released"] += n
        assert wstate["released"] <= wstate["used"]
        w_pump()

    def wview(slot, k, n, off=0):
        return slot[:, off:off + k * n].rearrange("p (k n) -> p k n", k=k)

    def colgroups(C):
        return [(0, 512), (512, C - 512)] if C > 512 else [(0, C)]

    def norm(C, wcol):
        P.label = "norm"
        sqp = TempPool(arena, [C], BF16, 4)
        rs, rst = arena.alloc([C], F32)
        cgs = colgroups(C)
        prs = [psum(n) for (c0, n) in cgs]
        for c in range(8):
            sq, sqt = sqp.get()
            if c % 2 == 1:
                tt("pool", sq, hT_h[:, c, 0:C], hT_h[:, c, 0:C], ALU.mult, [hT_t[c]], sqt)
            else:
                act(sq, hT_h[:, c, 0:C], AF.Square, [hT_t[c]], sqt)
            for (pr, prt), (c0, n) in zip(prs, cgs):
                mm(pr, ONES_B, sq[:, c0:c0 + n], c == 0, c == 7, [cb_t] + sqt, [prt])
        for (pr, prt), (c0, n) in zip(prs, cgs):
            act(rs[:, c0:c0 + n], pr, AF.Ln, [prt], rst, bias=EPS, scale=1.0 / 1024)
        act(rs, rs, AF.Exp, rst, rst, scale=-0.5)
        for c in range(8):
            stt("dve", hn_h[:, c, 0:C], hT_h[:, c, 0:C], PPc(wcol + c), rs, ALU.mult, ALU.mult,
                [hT_t[c], pp_t] + rst, [hn_t[c]])

    def proj(slot, slot_t, kview, c, C, wcols=128):
        res = []
        for (c0, n) in colgroups(C):
            pr, prt = psum(n)
            for k in range(8):
                mm(pr, kview[:, k, c * 128:c * 128 + wcols], hn_h[:, k, c0:c0 + n], k == 0, k == 7,
                   [slot_t, hn_t[k]], [prt])
            res.append((pr, prt, c0, n))
        return res

    def conv_taps(xe, xet, C, K, wcol0, accp):
        acc, acct = accp.get()
        act(acc, xe[:, 0:C], AF.Copy, xet + [pp_t], acct, scale=PPc(wcol0))
        for j in range(1, K):
            stt("dve", acc, xe[:, j:j + C], PPc(wcol0 + j), acc, ALU.mult, ALU.add, xet + [pp_t] + acct, acct)
        return acc, acct

    def load_tile(ti, g0, G):
        C = 128 * G
        P.label = "load"
        xp = TempPool(arena, [1024], F32, 5)
        for g in range(G):
            gg = g0 + g
            xin, xint = xp.get()
            if gg == 0:
                memset("dve", xin, 0.0, xint)
                P.dma("sp", xin[112:128, :], meta_d[:, :], writes=xint)
            else:
                P.dma("sp", xin, x_d[(gg - 1) * 128:gg * 128, :], writes=xint)
            for half in range(2):
                pt, ptt = psum(512, F32, [4, 128])
                for cc in range(4):
                    c = half * 4 + cc
                    tr(pt[:, cc, :], xin[:, c * 128:(c + 1) * 128], MK("ID"), xint + [mk_t], [ptt])
                cp("act" if half == 0 else "dve", hT_h[:, half * 4:half * 4 + 4, g * 128:(g + 1) * 128], pt,
                   [ptt], hT_t[half * 4:half * 4 + 4])

    def store_tile(ti, g0, G):
        P.label = "store"
        xp = TempPool(arena, [1024], F32, 2)
        for g in range(G):
            gg = g0 + g
            if gg == 0:
                continue
            xo, xot = xp.get()
            for half in range(2):
                pt, ptt = psum(512, F32, [4, 128])
                for cc in range(4):
                    c = half * 4 + cc
                    tr(pt[:, cc, :], hT_h[:, c, g * 128:(g + 1) * 128], MK("ID"), [hT_t[c], mk_t], [ptt])
                cp("act" if half == 0 else "dve", xo[:, half * 512:(half + 1) * 512], pt.rearrange("p a b -> p (a b)"),
                   [ptt], xot)
            P.dma("sp", out_d[(gg - 1) * 128:gg * 128, :], xo, reads=xot, writes=[out_t])

    def out_proj(names, rhs_h, rhs_t, C):
        P.label = "outproj"
        slots = [wnext(n) for n in names]
        for m in range(8):
            slot, slot_t = slots[m // 4]
            wv_ = wview(slot, 8, 512)
            mc = m % 4
            for (c0, n) in colgroups(C):
                pr, prt = psum(n)
                for k in range(8):
                    mm(pr, wv_[:, k, mc * 128:(mc + 1) * 128], rhs_h[:, k, c0:c0 + n], k == 0, k == 7,
                       [slot_t, rhs_t[k]], [prt])
                tt("dve", hT_h[:, m, c0:c0 + n], hT_h[:, m, c0:c0 + n], pr, ALU.add, [hT_t[m], prt], [hT_t[m]])
            if m % 4 == 3:
                wrel()

    def ffn(l, C):
        arena.reset()
        norm(C, PP_FNW + l * 8)
        P.label = "ffn_up"
        actb, actt = arena.alloc([22, C], BF16, ntok=22)
        xep = TempPool(arena, [C + 2], F32, 3)
        accp = TempPool(arena, [C], F32, 3)
        slots = {}

        def ffn_chunk(j):
            jb, jc = j // 4, j % 4
            ncol = 512 if jb < 5 else 256
            last = (jc == ncol // 128 - 1)
            P.label = "ffn_up"
            if jc == 0:
                slots[("g", jb)] = wnext(("up", l, 0, jb))
                slots[("v", jb)] = wnext(("up", l, 1, jb))
            sg, sgt = slots[("g", jb)]
            sv, svt = slots[("v", jb)]
            pg = proj(sg, sgt, wview(sg, 8, ncol), jc, C)
            if last:
                wrel()
            xe, xet = xep.get()
            cp("dve", xe[:, 0:2], haloF_h[:, l, j, :], [haloF_t[l][j]], xet)
            for (pr, prt, c0, n) in pg:
                cp("act", xe[:, 2 + c0:2 + c0 + n], pr, [prt], xet)
            yield
            P.label = "ffn_up"
            cp("dve", haloF_h[:, l, j, :], xe[:, C:C + 2], xet, [haloF_t[l][j]])
            acc, acct = conv_taps(xe, xet, C, 3, PP_FC + l * 66 + j * 3, accp)
            pv = proj(sv, svt, wview(sv, 8, ncol), jc, C)
            if last:
                wrel()
            act(acc, acc, AF.Silu, acct, acct)
            yield
            P.label = "ffn_up"
            for (pr, prt, c0, n) in pv:
                tt("dve", actb[:, j, c0:c0 + n], pr, acc[:, c0:c0 + n], ALU.mult, [prt] + acct, [actt[j]])

        run_pipeline([ffn_chunk(j) for j in range(22)], 3, lag=1)
        P.label = "ffn_dn"
        cgs = colgroups(C)
        for mp in range(4):
            regs = {}
            for mi in range(2):
                for ci, (c0, n) in enumerate(cgs):
                    regs[(mi, ci)] = psum(n)
            for jh in range(2):
                sd, sdt = wnext(("dn", l, mp, jh))
                vd = wview(sd, 11, 256)
                for mi in range(2):
                    for ci, (c0, n) in enumerate(cgs):
                        pr, prt = regs[(mi, ci)]
                        for jj in range(11):
                            j = jh * 11 + jj
                            mm(pr, vd[:, jj, mi * 128:(mi + 1) * 128], actb[:, j, c0:c0 + n], j == 0, j == 21,
                               [sdt, actt[j]], [prt])
                wrel()
            for mi in range(2):
                m = mp * 2 + mi
                for ci, (c0, n) in enumerate(cgs):
                    pr, prt = regs[(mi, ci)]
                    tt("dve", hT_h[:, m, c0:c0 + n], hT_h[:, m, c0:c0 + n], pr, ALU.add, [hT_t[m], prt], [hT_t[m]])

    def even_mixer(ti, G):
        C = 128 * G
        arena.reset()
        yab, yabt = arena.alloc([8, C], BF16, ntok=8)
        qkvT, qkvt = arena.alloc([12, C], BF16, ntok=12)
        sz, szt = arena.alloc([G, 512], BF16, ntok=G)
        bg, bgt = arena.alloc([G, 16], F32, ntok=G)
        mark = arena.off
        norm(C, PP_ANW + 0)
        gip = TempPool(arena, [C], F32, 2)
        xep = TempPool(arena, [C + 3], F32, 4)
        accp = TempPool(arena, [C], F32, 4)
        sqp = TempPool(arena, [C], BF16, 3)
        slots = {}

        def conva_chunk(c):
            P.label = "convA"
            if c == 0:
                slots["gi"] = wnext(("in", 0))
                slots["go"] = wnext(("in", 1))
                slots["ah"] = wnext(("in", 2))
            s_gi, t_gi = slots["gi"]
            s_go, t_go = slots["go"]
            s_ah, t_ah = slots["ah"]
            p_gi = proj(s_gi, t_gi, wview(s_gi, 8, 512), c, C)
            p_ah = proj(s_ah, t_ah, wview(s_ah, 8, 512), c, C)
            gi, git = gip.get()
            xe, xet = xep.get()
            cp("dve", xe[:, 0:2], haloA_h[:, c, :], [haloA_t[c]], xet)
            for (pr, prt, c0, n) in p_gi:
                cp("act", gi[:, c0:c0 + n], pr, [prt], git)
            yield
            P.label = "convA"
            for (pr, prt, c0, n) in p_ah:
                tt("dve", xe[:, 2 + c0:2 + c0 + n], pr, gi[:, c0:c0 + n], ALU.mult, [prt] + git, xet)
            cp("dve", haloA_h[:, c, :], xe[:, C:C + 2], xet, [haloA_t[c]])
            p_go = proj(s_go, t_go, wview(s_go, 8, 512), c, C)
            if c == 3:
                wrel(3)
            acc, acct = conv_taps(xe, xet, C, 3, PP_CA + c * 3, accp)
            yield
            P.label = "convA"
            for (pr, prt, c0, n) in p_go:
                tt("dve", yab[:, c, c0:c0 + n], pr, acc[:, c0:c0 + n], ALU.mult, [prt] + acct, [yabt[c]])

        def qkv_chunk(cc):
            b, c = cc // 4, cc % 4
            P.label = "qkv"
            if c == 0:
                slots[("qkv", b)] = wnext(("in", 3 + b))
            sw, swt = slots[("qkv", b)]
            pq = proj(sw, swt, wview(sw, 8, 512), c, C)
            if c == 3:
                wrel()
            xe, xet = xep.get()
            cp("dve", xe[:, 0:3], haloQ_h[:, cc, :], [haloQ_t[cc]], xet)
            for (pr, prt, c0, n) in pq:
                cp("act", xe[:, 3 + c0:3 + c0 + n], pr, [prt], xet)
            yield
            P.label = "qkv"
            cp("dve", haloQ_h[:, cc, :], xe[:, C:C + 3], xet, [haloQ_t[cc]])
            acc, acct = conv_taps(xe, xet, C, 4, PP_DC + cc * 4, accp)
            if b == 2:
                act(qkvT[:, cc, :], acc, AF.Silu, acct, [qkvt[cc]])
                return
            act(qkvT[:, cc, :], acc, AF.Silu, acct, [qkvt[cc]])
            sq, sqt = sqp.get()
            act(sq, qkvT[:, cc, :], AF.Square, [qkvt[cc]], sqt)
            yield
            P.label = "qkv"
            for (c0, n) in colgroups(C):
                pr, prt = psum(n)
                mm(pr, ONES_B, sq[:, c0:c0 + n], True, True, [cb_t] + sqt, [prt])
                cp("dve", ssum[:, cc, c0:c0 + n], pr, [prt], [ssumt[cc]])

        ssum, ssumt = arena.alloc([8, C], F32, ntok=8)
        run_pipeline([conva_chunk(c) for c in range(4)] + [qkv_chunk(cc) for cc in range(12)], 4, lag=1)
        P.label = "qkv"
        for half in range(2):
            hsl = slice(4 * half, 4 * half + 4)
            act(ssum[:, hsl, :], ssum[:, hsl, :], AF.Ln, ssumt[4 * half:4 * half + 4], ssumt[4 * half:4 * half + 4], bias=EPS)
            act(ssum[:, hsl, :], ssum[:, hsl, :], AF.Exp, ssumt[4 * half:4 * half + 4], ssumt[4 * half:4 * half + 4], scale=-0.5)
        for cc in range(8):
            stt("dve", qkvT[:, cc, :], qkvT[:, cc, :], (128 ** -0.5) if cc < 4 else 1.0, ssum[:, cc, :],
                ALU.mult, ALU.mult, [qkvt[cc], ssumt[cc]], [qkvt[cc]])
        P.label = "ztok"
        s_z, t_z = wnext(("in", 6))
        s_ab, t_ab = wnext(("in", "ab"))
        vz, vab = wview(s_z, 8, 512), wview(s_ab, 8, 8)
        for g in range(G):
            gc = slice(g * 128, (g + 1) * 128)
            pz, pzt = psum(512)
            for k in range(8):
                mm(pz, hn_h[:, k, gc], vz[:, k, :], k == 0, k == 7, [t_z, hn_t[k]], [pzt])
            act(sz[:, g, :], pz, AF.Silu, [pzt], [szt[g]])
        for g in range(G):
            gc = slice(g * 128, (g + 1) * 128)
            pab, pabt = psum(8)
            for k in range(8):
                mm(pab, hn_h[:, k, gc], vab[:, k, :], k == 0, k == 7, [t_ab, hn_t[k]], [pabt])
            act(bg[:, g, 0:4], pab[:, 0:4], AF.Sigmoid, [pabt], [bgt[g]])
            tt("dve", bg[:, g, 12:16], pab[:, 4:8], rp_h[:, RP_DTB:RP_DTB + 4], ALU.add, [pabt, rp_t], [bgt[g]])
        ts("dve", bg[:, :, 4:8], bg[:, :, 0:4], -1.0, None, ALU.mult, None, bgt, bgt)
        act(bg[:, :, 12:16], bg[:, :, 12:16], AF.Exp, bgt, bgt)
        act(bg[:, :, 12:16], bg[:, :, 12:16], AF.Ln, bgt, bgt, bias=1.0)
        tt("dve", bg[:, :, 8:12], bg[:, :, 12:16], rp_h[:, NEXPA:NEXPA + 4].unsqueeze(1).broadcast_to([128, G, 4]),
           ALU.mult, bgt + [rp_t], bgt)
        wrel(2)
        arena.off = mark
        bl = lambda ap: ap.unsqueeze(2).broadcast_to([128, 4, 128])

        class DnBufs:
            pass

        def dn_bufs():
            B = DnBufs()
            A4 = lambda dt: arena.alloc([4, 128], dt)
            for nm in ("nkb", "kbd", "kdec", "vb", "nkbT", "qdT", "Ybf", "qkT", "nwT", "vn", "ybt",
                       "Nm", "Mm", "Nl", "Ml", "X", "Y", "Pm", "Qm"):
                setattr(B, nm, A4(BF16))
            for nm in ("gL", "Dl", "Du", "dmTi", "edB"):
                setattr(B, nm, A4(F32))
            B.u, B.o, B.sqo, B.nwz = B.gL, B.Dl, B.Du, B.dmTi
            B.e16 = arena.alloc([16], F32)
            B.bed = arena.alloc([4], F32)
            B.ss = arena.alloc([4], F32)
            return B

        def dn_group(g, B):
            gc = slice(g * 128, (g + 1) * 128)
            beta, nbeta, gg_ = bg[:, g, 0:4], bg[:, g, 4:8], bg[:, g, 8:12]
            nkb, nkbt = B.nkb; kbd, kbdt = B.kbd; kdec, kdect = B.kdec; vb, vbt = B.vb
            nkbT, nkbTt = B.nkbT; qdT, qdTt = B.qdT; Ybf, Ybft = B.Ybf; qkT, qkTt = B.qkT
            nwT, nwTt = B.nwT; vn, vnt = B.vn; ybt, ybtt = B.ybt
            Nm, Nmt = B.Nm; Mm, Mmt = B.Mm; Nl, Nlt = B.Nl; Ml, Mlt = B.Ml
            X, Xt = B.X; Y, Yt = B.Y; Pm, Pmt = B.Pm; Qm, Qmt = B.Qm
            gL, gLt = B.gL; Dl, Dlt = B.Dl; Du, Dut = B.Du; dmTi, dmTit = B.dmTi; edB, edBt = B.edB
            u, ut = B.u; o, ot = B.o; sqo, sqot = B.sqo; nwz, nwzt = B.nwz
            e16, e16t = B.e16; bed, bedt = B.bed; ss, sst = B.ss
            P.label = "dn_prep"
            pst, pstt = psum(512, BF16, [8, 128])
            for hh in range(4):
                tr(pst[:, hh, :], qkvT[:, 4 + hh, gc], ID_B, [qkvt[4 + hh], cb_t], [pstt])
                tr(pst[:, 4 + hh, :], qkvT[:, 8 + hh, gc], ID_B, [qkvt[8 + hh], cb_t], [pstt])
            pse, pset = psum(16)
            mm(pse[:, 0:4], MK("L"), gg_, True, True, [mk_t, bgt[g]], [pset])
            mm(pse[:, 4:8], MK("U"), gg_, True, True, [mk_t, bgt[g]], [pset])
            mm(pse[:, 8:12], MK("MC0"), gg_, True, True, [mk_t, bgt[g]], [pset])
            mm(pse[:, 12:16], MK("MC1"), gg_, True, True, [mk_t, bgt[g]], [pset])
            tt("dve", gL, MK4("L"), bl(gg_), ALU.mult, [mk_t, bgt[g]], gLt)
            yield
            P.label = "dn_prep"
            act(e16, pse, AF.Exp, [pset], e16t)
            tt("dve", bed, beta, e16[:, 0:4], ALU.mult, [bgt[g]] + e16t, bedt)
            tt("dve", nkb, pst[:, 0:4, :], bl(nbeta), ALU.mult, [pstt, bgt[g]], nkbt)
            psD, psDt = psum(512, F32, [4, 128])
            for hh in range(4):
                mm(psD[:, hh, :], gL[:, hh, :], MK("ONES"), True, False, gLt + [mk_t], [psDt])
                mm(psD[:, hh, :], MK("NEGONES"), gL[:, hh, :], False, True, gLt + [mk_t], [psDt])
            psB, psBt = psum(512, F32, [4, 128])
            for hh in range(4):
                mm(psB[:, hh, :], MK("ONES"), gL[:, hh, :], True, True, gLt + [mk_t], [psBt])
            psn, psnt = psum(256, BF16, [4, 128])
            for hh in range(4):
                tr(psn[:, hh, :], nkb[:, hh, :], ID_B, nkbt + [cb_t], [psnt])
            tt("dve", kbd, pst[:, 0:4, :], bl(bed), ALU.mult, [pstt] + bedt, kbdt)
            tt("dve", kdec, pst[:, 0:4, :], bl(e16[:, 4:8]), ALU.mult, [pstt] + e16t, kdect)
            tt("dve", vb, pst[:, 4:8, :], bl(beta), ALU.mult, [pstt, bgt[g]], vbt)
            yield
            P.label = "dn_prep"
            cp("act", nkbT, psn, [psnt], nkbTt)
            tt("dve", Dl, psD, MK4("NEGSL"), ALU.add, [psDt, mk_t], Dlt)
            tt("dve", Du, psD, MK4("POSSU"), ALU.add, [psDt, mk_t], Dut)
            act(Dl, Dl, AF.Exp, Dlt, Dlt)
            act(Du, Du, AF.Exp, Dut, Dut, scale=-1.0)
            act(edB, psB, AF.Exp, [psBt], edBt)
            psN, psNt = psum(512, F32, [4, 128])
            psM, psMt = psum(512, F32, [4, 128])
            psQ, psQt = psum(512, F32, [4, 128])
            for hh in range(4):
                mm(psN[:, hh, :], nkbT[:, hh, :], qkvT[:, 4 + hh, gc], True, True, nkbTt + [qkvt[4 + hh]], [psNt])
            for hh in range(4):
                mm(psM[:, hh, :], qkvT[:, 4 + hh, gc], nkbT[:, hh, :], True, True, nkbTt + [qkvt[4 + hh]], [psMt])
            for hh in range(4):
                mm(psQ[:, hh, :], qkvT[:, 4 + hh, gc], qkvT[:, hh, gc], True, True, [qkvt[4 + hh], qkvt[hh]], [psQt])
            yield
            P.label = "dn_prep"
            tt("dve", dmTi, Du, MK4("ID"), ALU.add, Dut + [mk_t], dmTit)
            tt("dve", qdT, qkvT[:, 0:4, gc], edB, ALU.mult, qkvt[0:4] + edBt, qdTt)
            tt("dve", Nm, psN, Dl, ALU.mult, [psNt] + Dlt, Nmt)
            tt("dve", Mm, psM, Du, ALU.mult, [psMt] + Dut, Mmt)
            tt("dve", qkT, psQ, dmTi, ALU.mult, [psQt] + dmTit, qkTt)
            P.label = "dn_chain"
            tt("dve", Nl, Nm, MK4("LV0"), ALU.mult, Nmt + [mk_t], Nlt)
            tt("dve", X, Nl, MK4("ID"), ALU.add, Nlt + [mk_t], Xt)
            tt("dve", Ml, Mm, MK4("UV0"), ALU.mult, Mmt + [mk_t], Mlt)
            tt("dve", Y, Ml, MK4("ID"), ALU.add, Mlt + [mk_t], Yt)
            for l in range(1, 6):
                last = (l == 5)
                P.label = "dn_chain"
                tt("dve", Nl, Nm, MK4("LV%d" % l), ALU.mult, Nmt + [mk_t], Nlt)
                if not last:
                    tt("dve", Ml, Mm, MK4("UV%d" % l), ALU.mult, Mmt + [mk_t], Mlt)
                pQ, pQt = psum(512, F32, [4, 128])
                for hh in range(4):
                    mm(pQ[:, hh, :], Nl[:, hh, :], Y[:, hh, :], True, True, Nlt + Yt, [pQt])
                if not last:
                    pP, pPt = psum(512, F32, [4, 128])
                    for hh in range(4):
                        mm(pP[:, hh, :], Ml[:, hh, :], X[:, hh, :], True, True, Mlt + Xt, [pPt])
                yield
                P.label = "dn_chain"
                cp("act", Qm, pQ, [pQt], Qmt)
                if not last:
                    cp("act", Pm, pP, [pPt], Pmt)
                pY, pYt = psum(512, F32, [4, 128])
                for hh in range(4):
                    mm(pY[:, hh, :], X[:, hh, :], Qm[:, hh, :], True, True, Xt + Qmt, [pYt])
                if not last:
                    pX, pXt = psum(512, F32, [4, 128])
                    for hh in range(4):
                        mm(pX[:, hh, :], Y[:, hh, :], Pm[:, hh, :], True, True, Yt + Pmt, [pXt])
                yield
                P.label = "dn_chain"
                if not last:
                    tt("dve", Y, Y, pY, ALU.add, Yt + [pYt], Yt)
                    tt("dve", X, X, pX, ALU.add, Xt + [pXt], Xt)
                else:
                    tt("dve", Ybf, Y, pY, ALU.add, Yt + [pYt], Ybft)
            P.label = "dn_uw"
            pU, pUt = psum(512, F32, [4, 128])
            for hh in range(4):
                mm(pU[:, hh, :], Ybf[:, hh, :], vb[:, hh, :], True, True, Ybft + vbt, [pUt])
            pW, pWt = psum(512, F32, [4, 128])
            for hh in range(4):
                mm(pW[:, hh, :], kbd[:, hh, :], Ybf[:, hh, :], True, True, kbdt + Ybft, [pWt])
            yield
            P.label = "dn_uw"
            cp("act", u, pU, [pUt], ut)
            act(nwT, pW, AF.Copy, [pWt], nwTt, scale=-1.0)
            tt("dve", nwz, sz[:, g, :].rearrange("p (a b) -> p a b", a=4),
               rp_h[:, RP_DNW:RP_DNW + 128].unsqueeze(1).broadcast_to([128, 4, 128]), ALU.mult, [szt[g], rp_t], nwzt)
            while scan_turn[0] != g:
                yield
            for cc in range(2):
                P.label = "dn_scan"
                r = slice(64 * cc, 64 * cc + 64)
                pV, pVt = psum(512, F32, [4, 128])
                for hh in range(4):
                    mm(pV[r, hh, :], nwT[:, hh, r], Sb_h[:, hh, :], True, True, nwTt + [Sb_t], [pVt])
                yield
                P.label = "dn_scan"
                tt("dve", vn[r, :, :], pV[r, :, :], u[r, :, :], ALU.add, [pVt] + ut, vnt)
                tt("dve", S_h[:, :, :], S_h[:, :, :], bl(e16[:, 8 + 4 * cc:12 + 4 * cc]), ALU.mult, [S_t] + e16t, [S_t])
                pO, pOt = psum(512, F32, [4, 128])
                for hh in range(4):
                    mm(pO[r, hh, :], qdT[:, hh, r], Sb_h[:, hh, :], True, False, qdTt + [Sb_t], [pOt])
                    mm(pO[r, hh, :], qkT[r, hh, r], vn[r, hh, :], False, True, qkTt + vnt, [pOt])
                pS, pSt = psum(512, F32, [4, 128])
                for hh in range(4):
                    mm(pS[:, hh, :], kdec[r, hh, :], vn[r, hh, :], True, True, kdect + vnt, [pSt])
                yield
                P.label = "dn_scan"
                tt("dve", S_h[:, :, :], S_h[:, :, :], pS, ALU.add, [S_t, pSt], [S_t])
                cp("act", Sb_h[:, :, :], S_h[:, :, :], [S_t], [Sb_t])
                cp("act", o[r, :, :], pO[r, :, :], [pOt], ot)
            scan_turn[0] = g + 1
            P.label = "dn_out"
            act(sqo, o, AF.Square, ot, sqot)
            rsum(ss, sqo, sqot, sst)
            act(ss, ss, AF.Ln, sst, sst, bias=EPS, scale=1.0 / 128)
            act(ss, ss, AF.Exp, sst, sst, scale=-0.5)
            yield
            P.label = "dn_out"
            tt("dve", sqo, o, bl(ss), ALU.mult, ot + sst, sqot)
            tt("dve", ybt, sqo, nwz, ALU.mult, sqot + nwzt, ybtt)
            pT, pTt = psum(256, BF16, [4, 128])
            for hh in range(4):
                tr(pT[:, hh, :], ybt[:, hh, :], ID_B, ybtt + [cb_t], [pTt])
            yield
            P.label = "dn_out"
            cp("act", yab[:, 4:8, gc], pT, [pTt], yabt[4:8])

        sets = [dn_bufs(), dn_bufs()]
        scan_turn = [0]
        oslots = {}

        def outproj_cg(ci, first, lastcg):
            (c0, n) = colgroups(C)[ci]
            P.label = "outproj"
            if first:
                oslots[0] = wnext(("out", 0))
                oslots[1] = wnext(("out", 1))
            for m in range(8):
                P.label = "outproj"
                slot, slot_t = oslots[m // 4]
                wv_ = wview(slot, 8, 512)
                mc = m % 4
                pr, prt = psum(n)
                for k in range(8):
                    mm(pr, wv_[:, k, mc * 128:(mc + 1) * 128], yab[:, k, c0:c0 + n], k == 0, k == 7,
                       [slot_t, yabt[k]], [prt])
                tt("dve", hT_h[:, m, c0:c0 + n], hT_h[:, m, c0:c0 + n], pr, ALU.add, [hT_t[m], prt], [hT_t[m]])
                if lastcg and m % 4 == 3:
                    wrel()
                yield

        gens = [dn_group(g, sets[g % 2]) for g in range(G)]
        if G == 5:
            gens.append(outproj_cg(0, True, False))
        run_pipeline(gens, cfg.get("dn_depth", 2), lag=8)
        if G == 5:
            for _ in outproj_cg(1, False, True):
                pass
        else:
            for _ in outproj_cg(0, True, True):
                pass
        return
        out_proj([("out", 0), ("out", 1)], yab, yabt, C)

    def swa_mixer(ti, G):
        C = 128 * G
        arena.reset()
        norm(C, PP_ANW + 8)
        qT, qTt = arena.alloc([8, C], BF16, ntok=8)
        OT, OTt = arena.alloc([8, C], BF16, ntok=8)
        qrp = TempPool(arena, [C], F32, 4)
        sqp = TempPool(arena, [C], BF16, 4)
        rsp = TempPool(arena, [C], F32, 4)
        Otp = TempPool(arena, [16, 64], BF16, 2)
        Ecp = TempPool(arena, [4, 128], BF16, 5)
        Emp = TempPool(arena, [4, 128], BF16, 5)
        Epp = TempPool(arena, [4, 128], BF16, 5)
        denp = TempPool(arena, [4], F32, 6)
        kTt_, kTtt = arena.alloc([2, C], BF16, ntok=2)
        P.label = "swa_proj"
        sq_slots = [wnext(("q", 0)), wnext(("q", 1))]
        s_kv, t_kv = wnext(("kv",))
        vk = wview(s_kv, 8, 256)

        def head_chunk(kind, c):
            P.label = "swa_proj"
            if kind == "q":
                slot, slot_t = sq_slots[c // 4]
                pq = proj(slot, slot_t, wview(slot, 8, 512), c % 4, C)
                if c % 4 == 3:
                    wrel()
                sccol = QSC
            else:
                pq = proj(s_kv, t_kv, vk, c, C)
                sccol = KSC
            qraw, qrawt = qrp.get()
            sq, sqt = sqp.get()
            for (pr, prt, c0, n) in pq:
                cp("act", qraw[:, c0:c0 + n], pr, [prt], qrawt)
            act(sq, qraw, AF.Square, qrawt, sqt)
            yield
            P.label = "swa_proj"
            rs, rst = rsp.get()
            for (c0, n) in colgroups(C):
                pr2, pr2t = psum(n)
                mm(pr2, BLK_B, sq[:, c0:c0 + n], True, True, [cb_t] + sqt, [pr2t])
                act(rs[:, c0:c0 + n], pr2, AF.Ln, [pr2t], rst, bias=EPS, scale=1.0 / 64)
            act(rs, rs, AF.Exp, rst, rst, scale=-0.5)
            yield
            P.label = "swa_proj"
            if kind == "q":
                dst, dstt = qT[:, c, :], [qTt[c]]
            else:
                dst, dstt = kTt_[:, c, :], [kTtt[c]]
            stt("dve", dst, qraw, PPc(sccol), rs, ALU.mult, ALU.mult, qrawt + rst + [pp_t], dstt)
            if kind == "k":
                ck = c
                for hk in range(2):
                    kh = 2 * ck + hk
                    rows = slice(64 * hk, 64 * hk + 64)
                    orows = slice(64 * (1 - hk), 64 * (1 - hk) + 64)
                    cp("act", kZ_h[rows, hk * 4 + kh, 128:128 + C], kTt_[rows, ck, :], [kTtt[ck]], [kZ_t[hk * 4 + kh]])
                    P.dma("sp", kZ_h[orows, (1 - hk) * 4 + kh, 128:128 + C], kTt_[rows, ck, :], reads=[kTtt[ck]],
                          writes=[kZ_t[(1 - hk) * 4 + kh]])

        run_pipeline([head_chunk("k", 0), head_chunk("k", 1)] + [head_chunk("q", c) for c in range(8)], 4, lag=1)
        vvw = wview(s_kv, 8, 256, off=2048)
        for g in range(G):
            gc = slice(g * 128, (g + 1) * 128)
            pv, pvt = psum(256, F32, [4, 64])
            for k in range(8):
                mm(pv, hn_h[:, k, gc], vvw[:, k, :], k == 0, k == 7, [t_kv, hn_t[k]], [pvt])
            cp("act", Vb_h[:, 1 + g, :, 0:64], pv, [pvt], [Vb_t[1 + g]])
        wrel()
        if ti == 0:
            cp("dve", kZm_h[:, :, :], kZ_h[:, :, 128 + 96:128 + 128], kZ_t, [kZm_t])
            P.dma("sp", Vm_h[0:32, :, :], Vb_h[96:128, 1, :, :], reads=[Vb_t[1]], writes=[Vm_t])
        otoks = {}

        def attn_unit(g, kh):
            gc = slice(g * 128, (g + 1) * 128)
            is_meta = (ti == 0 and g == 0)
            has_prev = not (ti == 0 and g <= 1)
            P.label = "swa_attn"
            if kh == 0:
                otoks[g] = Otp.get()
            Otok, Otokt = otoks[g]
            if not is_meta:
                pM, pMt = psum(512, F32, [4, 128])
                if has_prev:
                    pP, pPt = psum(512, F32, [4, 128])
            pC, pCt = psum(512, F32, [4, 128])
            for hq in range(2):
                zi = hq * 4 + kh
                q_ap = qT[:, 2 * kh:2 * kh + 2, gc]
                qtk = [qTt[2 * kh], qTt[2 * kh + 1]]
                if not is_meta:
                    mm(pM[0:32, hq:4:2, :], kZm_h[:, zi, :], q_ap, True, True, [kZm_t] + qtk, [pMt])
                    if has_prev:
                        mm(pP[:, hq:4:2, :], kZ_h[:, zi, 128 * g:128 * g + 128], q_ap, True, True,
                           [kZ_t[zi]] + qtk, [pPt])
                mm(pC[:, hq:4:2, :], kZ_h[:, zi, 128 * (g + 1):128 * (g + 1) + 128], q_ap, True, True,
                   [kZ_t[zi]] + qtk, [pCt])
            yield
            P.label = "swa_attn"
            Ec, Ect = Ecp.get()
            act(Ec, pC, AF.Exp, [pCt], Ect)
            tt("dve", Ec, Ec, MK4("SWAMETA" if is_meta else "SWACUR"), ALU.mult, Ect + [mk_t], Ect)
            if not is_meta:
                Em, Emt = Emp.get()
                act(Em[0:32, :, :], pM[0:32, :, :], AF.Exp, [pMt], Emt)
                tt("dve", Em[0:32, :, :], Em[0:32, :, :],
                   mk_h[0:32, MI["METAK"]:MI["METAK"] + 1, :].broadcast_to([32, 4, 128]),
                   ALU.mult, Emt + [mk_t], Emt)
                if has_prev:
                    Ep, Ept = Epp.get()
                    act(Ep, pP, AF.Exp, [pPt], Ept)
                    tt("dve", Ep, Ep, MK4("SWAPREV"), ALU.mult, Ept + [mk_t], Ept)
            yield
            P.label = "swa_attn"
            pO, pOt = psum(512, F32, [4, 128])
            for gq in range(4):
                first = True
                if not is_meta:
                    mm(pO[:, gq, 0:65], Em[0:32, gq, :], Vm_h[0:32, kh, :], True, False, Emt + [Vm_t], [pOt])
                    first = False
                    if has_prev:
                        mm(pO[:, gq, 0:65], Ep[:, gq, :], Vb_h[:, g, kh, :], False, False, Ept + [Vb_t[g]], [pOt])
                mm(pO[:, gq, 0:65], Ec[:, gq, :], Vb_h[:, g + 1, kh, :], first, True, Ect + [Vb_t[g + 1]], [pOt])
            yield
            P.label = "swa_attn"
            den, dent = denp.get()
            tt("dve", den, pO[:, :, 64], rp_h[:, RP_SINK + 4 * kh:RP_SINK + 4 * kh + 4], ALU.add, [pOt, rp_t], dent)
            recip(den, den, dent, dent)
            tt("dve", Otok[:, 4 * kh:4 * kh + 4, :], pO[:, :, 0:64], den.unsqueeze(2).broadcast_to([128, 4, 64]),
               ALU.mult, [pOt] + dent, Otokt)
            if kh == 3:
                yield
                P.label = "swa_attn"
                pT, pTt = psum(512, BF16, [8, 128])
                Of = Otok.rearrange("p a b -> p (a b)")
                for c in range(8):
                    tr(pT[:, c, :], Of[:, c * 128:(c + 1) * 128], ID_B, Otokt + [cb_t], [pTt])
                yield
                P.label = "swa_attn"
                cp("act", OT[:, :, gc], pT, [pTt], OTt)

        run_pipeline([attn_unit(g, kh) for g in range(G) for kh in range(4)], 4, lag=1)
        if cfg.get("swa_dbg") == "noout":
            wnext(("o", 0)); wnext(("o", 1)); wrel(2)
        else:
            if ti == 0:
                dump("hpre", hT_h[:, 0, 128:256], hT_t)
            out_proj([("o", 0), ("o", 1)], OT, OTt, C)
            if ti == 0:
                dump("hpost", hT_h[:, 0, 128:256], hT_t)
        cp("dve", kZ_h[:, :, 0:128], kZ_h[:, :, C:C + 128], kZ_t, kZ_t)
        cp("dve", Vb_h[:, 0, :, 0:64], Vb_h[:, G, :, 0:64], [Vb_t[G]], [Vb_t[0]])

    out_t = P.tok()
    g0 = 0
    for ti, G in enumerate(GROUPS_PER_TILE):
        if ti >= cfg.get("ntiles", 99):
            break
        C = 128 * G
        arena.reset()
        load_tile(ti, g0, G)
        if cfg["mix0"]:
            even_mixer(ti, G)
        if cfg["ffn0"]:
            ffn(0, C)
        if cfg["mix1"]:
            swa_mixer(ti, G)
        if cfg["ffn1"]:
            ffn(1, C)
        arena.reset()
        store_tile(ti, g0, G)
        g0 += G
    P.wait_all("sp", [out_t, dbg_out_t])
    stats = P.emit()
    stats["dbg"] = dbg_state["names"]
    build_program.last_P = P
    return nc, stats


FULL_CFG = dict(mix0=True, ffn0=True, mix1=True, ffn1=True)
_CACHE = {}


def host_params(inp):
    f = lambda a: np.asarray(a, np.float32)
    pp = np.zeros((128, NPP), np.float32)
    for l in range(2):
        pp[:, PP_ANW + l * 8:PP_ANW + l * 8 + 8] = f(inp["attn_norm_w"])[l].reshape(8, 128).T
        pp[:, PP_FNW + l * 8:PP_FNW + l * 8 + 8] = f(inp["ffn_norm_w"])[l].reshape(8, 128).T
        fc = f(inp["ffn_conv_w"])[l].reshape(3, 22, 128)
        pp[:, PP_FC + l * 66:PP_FC + l * 66 + 66] = fc.transpose(2, 1, 0).reshape(128, 66)
    ca = f(inp["conv_a_w"])[0].reshape(3, 4, 128)
    pp[:, PP_CA:PP_CA + 12] = ca.transpose(2, 1, 0).reshape(128, 12)
    dc = f(inp["dn_conv_w"])[0].reshape(4, 12, 128)
    pp[:, PP_DC:PP_DC + 48] = dc.transpose(2, 1, 0).reshape(128, 48)
    pp[:, PP_QN] = np.tile(f(inp["swa_q_norm_w"])[0], 2)
    pp[:, PP_KN] = np.tile(f(inp["swa_k_norm_w"])[0], 2)
    rp = np.zeros((128, NRP), np.float32)
    rp[:, RP_DTB:RP_DTB + 4] = f(inp["dn_dt_bias"])[0][None, :]
    rp[:, RP_ALOG:RP_ALOG + 4] = f(inp["dn_a_log"])[0][None, :]
    rp[:, RP_DNW:RP_DNW + 128] = f(inp["dn_norm_w"])[0][None, :]
    rp[:, RP_SINK:RP_SINK + 16] = f(inp["swa_sinks"])[0][None, :]
    return pp, rp


def run(inp, cfg, built=None):
    if built is None:
        key = tuple(sorted((k, str(v)) for k, v in cfg.items()))
        if key not in _CACHE:
            _CACHE[key] = build_program(cfg)
        built = _CACHE[key]
    nc, stats = built
    f = lambda a: np.ascontiguousarray(np.asarray(a, np.float32))
    pp, rp = host_params(inp)
    masks = make_masks()
    shared = dict(meta=f(inp["meta_tokens"]), w_in=f(inp["mix_w_in"])[0], w_out=f(inp["mix_w_out"])[0],
                  wq=f(inp["swa_wq"])[0], wk=f(inp["swa_wk"])[0], wv=f(inp["swa_wv"])[0], wo=f(inp["swa_wo"])[0],
                  w_up=f(inp["ffn_w_up"]), w_dn=f(inp["ffn_w_down"]), pp=pp, rp=rp, masks=masks)
    x = f(inp["x"])
    in_maps = [dict(shared, x=x[b]) for b in range(8)]
    res = run_bass_kernel_spmd(nc, in_maps, core_ids=list(range(8)))
    if cfg.get("dbg"):
        run.dbg = [np.asarray(r["dbg"]) for r in res.results]
    return np.stack([np.asarray(r["out"], np.float32) for r in res.results], 0)


def kernel(**inputs):
    return run(inputs, FULL_CFG)
```

```python
import contextlib
import numpy as np
import concourse.bass as bass
import concourse.mybir as mybir
from concourse.bass_utils import run_bass_kernel_spmd

F32 = mybir.dt.float32
BF16 = mybir.dt.bfloat16
AF = mybir.ActivationFunctionType
ALU = mybir.AluOpType
AX = mybir.AxisListType

EPS = 1e-6
GROUPS_PER_TILE = [5, 5, 5, 5, 5, 4, 4]
CMAX = 640
NSLOT = 6
MASK_NAMES = ["ID", "ONES", "NEGONES", "BLK64", "L", "U", "MC0", "MC1", "NEGSL", "POSSU",
              "LV0", "LV1", "LV2", "LV3", "LV4", "LV5", "UV0", "UV1", "UV2", "UV3", "UV4", "UV5",
              "SWAPREV", "SWACUR", "SWAMETA", "METAK"]
MI = {n: i for i, n in enumerate(MASK_NAMES)}
PP_ANW, PP_FNW, PP_CA, PP_DC, PP_FC, PP_QN, PP_KN, NPP = 0, 16, 32, 44, 92, 224, 225, 226
RP_DTB, RP_ALOG, RP_DNW, RP_SINK, NRP = 0, 4, 8, 136, 152


class Tok:
    __slots__ = ("w", "r", "x")

    def __init__(self, x=False):
        self.w = None
        self.r = []
        self.x = x


class Op:
    __slots__ = ("eng", "fn", "dma", "waits", "need_inc", "pos", "snap", "sem", "semval", "fs")


class Prog:
    ENGS = ("pe", "act", "dve", "pool", "sp")
    NDMASEM = 12

    def __init__(self, nc):
        self.nc = nc
        self.stack = contextlib.ExitStack()
        self.e = {"pe": nc.tensor, "act": nc.scalar, "dve": nc.vector, "pool": nc.gpsimd, "sp": nc.sync}
        self.ops = []
        self.label = ""
        self.labels = []
        self.npos = {e: 0 for e in self.ENGS}
        self.known = {e: {f: -1 for f in self.ENGS} for e in self.ENGS}
        self.known_dma = {e: set() for e in self.ENGS}
        self.sem = {e: self.stack.enter_context(nc.semaphore("s_" + e)) for e in self.ENGS}
        self.use_scopes = False
        self.dsem, self.dsem_use, self.dsem_last, self.dsem_rr = {}, {}, {}, {}
        for e in ("sp", "pool"):
            self.dsem[e] = [self.stack.enter_context(nc.semaphore("d_%s%d" % (e, i))) for i in range(self.NDMASEM)]
            self.dsem_use[e] = [0] * self.NDMASEM
            self.dsem_last[e] = [None] * self.NDMASEM
            self.dsem_rr[e] = 0

    def sb(self, name, shape, dtype):
        return self.stack.enter_context(self.nc.sbuf_tensor(name, list(shape), dtype))

    def ps(self, name, shape, dtype):
        return self.stack.enter_context(self.nc.psum_tensor(name, list(shape), dtype))

    @staticmethod
    def tok(n=None):
        if n is None:
            return Tok()
        return [Tok() for _ in range(n)]

    def add(self, eng, fn, reads=(), writes=(), dma=False, track=True, fs=1 << 30):
        op = Op()
        opid = len(self.ops)
        op.eng, op.fn, op.dma = eng, fn, dma
        op.fs = fs
        op.need_inc = False
        op.sem = None
        op.semval = 0
        deps = set()
        xr = [t for t in reads if t.x]
        if xr:
            reads = [t for t in reads if not t.x]
            writes = list(writes) + xr
        for t in reads:
            if t.w is not None:
                deps.add(t.w)
        for t in writes:
            if t.w is not None:
                deps.add(t.w)
            deps.update(t.r)
        if dma:
            pool = self.dsem[eng]
            i = self.dsem_rr[eng]
            self.dsem_rr[eng] = (i + 1) % len(pool)
            if self.dsem_last[eng][i] is not None:
                deps.add(self.dsem_last[eng][i])
            self.dsem_use[eng][i] += 1
            op.sem = pool[i]
            op.semval = 16 * self.dsem_use[eng][i]
            self.dsem_last[eng][i] = opid
        known = self.known[eng]
        kd = self.known_dma[eng]
        cw = {}
        waits = []
        for d in deps:
            dop = self.ops[d]
            if dop.dma:
                if d in kd:
                    continue
                kd.add(d)
                waits.append(d)
            else:
                if dop.eng == eng:
                    if not dma and (eng == "pe" or (dop.fs >= 256 and fs >= 256)):
                        continue
                    key = "self_" + eng
                else:
                    key = dop.eng
                if known.get(key, -1) >= dop.pos:
                    continue
                if key not in cw or self.ops[cw[key]].pos < dop.pos:
                    cw[key] = d
        for key, d in cw.items():
            self.ops[d].need_inc = True
            waits.append(d)
        for d in waits:
            dop = self.ops[d]
            for f, v in dop.snap.items():
                if known.get(f, -1) < v:
                    known[f] = v
            if not dop.dma:
                key = dop.eng if dop.eng != eng else "self_" + eng
                if known.get(key, -1) < dop.pos:
                    known[key] = dop.pos
        op.waits = waits
        self.labels.append(self.label)
        op.pos = self.npos[eng]
        if fn is not None:
            self.npos[eng] += 1
        snap = {f: v for f, v in known.items() if not f.startswith("self_")}
        if not dma and fn is not None:
            snap[eng] = op.pos
        op.snap = snap
        self.ops.append(op)
        if not track:
            return opid
        for t in reads:
            t.r.append(opid)
        for t in writes:
            t.w = opid
            t.r = []
        return opid

    def dma(self, eng, out, in_, reads=(), writes=()):
        e = self.e[eng]
        return self.add(eng, lambda: e.dma_start(out=out, in_=in_), reads, writes, dma=True)

    def wait_all(self, eng, toks):
        return self.add(eng, None, reads=(), writes=toks, track=False)

    def emit(self):
        cnt = {e: 0 for e in self.ENGS}
        val = {}
        for i, op in enumerate(self.ops):
            if op.need_inc:
                cnt[op.eng] += 1
                val[i] = cnt[op.eng]
        nw = 0
        cur = None
        scope = None
        for i, op in enumerate(self.ops):
            e = self.e[op.eng]
            if self.use_scopes and self.labels[i] != cur:
                if scope is not None:
                    self.nc.leave_named_scope(cur, scope, False)
                cur = self.labels[i]
                scope = self.nc.enter_named_scope(cur, False)[0]
            for d in op.waits:
                dop = self.ops[d]
                if dop.dma:
                    e.wait_ge(dop.sem, dop.semval)
                else:
                    e.wait_ge(self.sem[dop.eng], val[d])
                nw += 1
            if op.fn is None:
                continue
            ins = op.fn()
            if op.dma:
                ins.then_inc(op.sem, 16)
            elif op.need_inc:
                ins.then_inc(self.sem[op.eng], 1)
        if scope is not None:
            self.nc.leave_named_scope(cur, scope, False)
        self.stats = dict(n_ops=len(self.ops), n_waits=nw, incs=dict(cnt), pos=dict(self.npos))
        return self.stats


class Arena:
    def __init__(self, handle, nwords):
        self.h = handle
        self.n = nwords
        self.off = 0
        self.live = []

    def reset(self):
        self.off = 0

    def alloc(self, free_shape, dtype, ntok=1, parts=128):
        nel = int(np.prod(free_shape))
        words = nel if dtype == F32 else (nel + 1) // 2
        assert self.off + words <= self.n, ("arena overflow", self.off, words, self.n)
        s, e = self.off, self.off + words
        self.off = e
        ap = self.h[:, s:e]
        if dtype != F32:
            ap = ap.bitcast(dtype)
        if len(free_shape) == 2:
            ap = ap.rearrange("p (a b) -> p a b", a=free_shape[0])
        elif len(free_shape) == 3:
            ap = ap.rearrange("p (a b c) -> p a b c", a=free_shape[0], b=free_shape[1])
        toks = [Tok() for _ in range(ntok)]
        inh = []
        keep = []
        for (os_, oe, otoks) in self.live:
            if os_ < e and s < oe:
                for t in otoks:
                    if t.w is not None:
                        inh.append(t.w)
                    inh.extend(t.r)
                if s <= os_ and oe <= e:
                    continue
            keep.append((os_, oe, otoks))
        inh = sorted(set(inh))
        for t in toks:
            t.r = list(inh)
        keep.append((s, e, toks))
        self.live = keep
        return ap, toks


def run_pipeline(gens, depth, lag=0):
    gens = list(gens)
    active = []
    nxt = 0
    while True:
        while len(active) < depth and nxt < len(gens) and (not active or active[-1][1] >= lag):
            active.append([gens[nxt], 0])
            nxt += 1
        if not active:
            break
        for a in list(active):
            try:
                next(a[0])
                a[1] += 1
            except StopIteration:
                active.remove(a)


class TempPool:
    def __init__(self, arena, free_shape, dtype, n=2):
        self.bufs = [arena.alloc(free_shape, dtype) for _ in range(n)]
        self.i = 0

    def get(self):
        b = self.bufs[self.i % len(self.bufs)]
        self.i += 1
        return b


def make_masks():
    i = np.arange(128)[:, None]
    j = np.arange(128)[None, :]
    same = (i // 64) == (j // 64)
    m = {}
    m["ID"] = (i == j)
    m["ONES"] = np.ones((128, 128), bool)
    m["NEGONES"] = -np.ones((128, 128), np.float32)
    m["BLK64"] = same
    m["L"] = same & (i <= j)
    m["U"] = same & (i > j)
    m["MC0"] = (i < 64) & (j >= 0)
    m["MC1"] = (i >= 64) & (j >= 0)
    m["NEGSL"] = np.where(same & (i > j), 0.0, -30000.0)
    m["POSSU"] = np.where(same & (j > i), 0.0, 30000.0)
    for l in range(6):
        b = 1 << l
        lv = ((i // (2 * b)) == (j // (2 * b))) & ((i % (2 * b)) >= b) & ((j % (2 * b)) < b)
        m["LV%d" % l] = lv
        m["UV%d" % l] = lv.T
    m["SWAPREV"] = (i > j)
    m["SWACUR"] = (i <= j)
    m["SWAMETA"] = (i >= 112) & (i <= j)
    m["METAK"] = (i >= 16) & (i < 32) & (j >= 0)
    out = np.zeros((128, len(MASK_NAMES), 128), np.float32)
    for n, k in MI.items():
        out[:, k, :] = m[n].astype(np.float32)
    return out.reshape(128, len(MASK_NAMES) * 128)


def build_program(cfg):
    nc = bass.Bass("TRN2", target_bir_lowering=False)
    P = Prog(nc)
    P.use_scopes = bool(cfg.get("scopes"))

    def dram(name, shape, kind="ExternalInput"):
        return nc.dram_tensor(name, list(shape), F32, kind=kind).ap()

    x_d = dram("x", [4096, 1024])
    meta_d = dram("meta", [16, 1024])
    win_d = dram("w_in", [1024, 3592])
    wout_d = dram("w_out", [1024, 1024])
    wq_d = dram("wq", [1024, 1024])
    wk_d = dram("wk", [1024, 256])
    wv_d = dram("wv", [1024, 256])
    wo_d = dram("wo", [1024, 1024])
    wup_d = dram("w_up", [2, 1024, 5632])
    wdn_d = dram("w_dn", [2, 2816, 1024])
    pp_d = dram("pp", [128, NPP])
    rp_d = dram("rp", [128, NRP])
    mk_d = dram("masks", [128, len(MASK_NAMES) * 128])
    out_d = dram("out", [4096, 1024], kind="ExternalOutput")
    dbg_d = dram("dbg", [128, 8192], kind="ExternalOutput") if cfg.get("dbg") else None
    dbg_state = dict(off=0, names=[])

    NM = len(MASK_NAMES)
    mk_h = P.sb("mk_sb", [128, NM, 128], F32)
    mk_t = P.tok()
    MK = lambda n: mk_h[:, MI[n], :]
    MK4 = lambda n: mk_h[:, MI[n]:MI[n] + 1, :].broadcast_to([128, 4, 128])
    cb_h = P.sb("cb", [128, 3, 128], BF16)
    cb_t = P.tok()
    ID_B, ONES_B, BLK_B = cb_h[:, 0, :], cb_h[:, 1, :], cb_h[:, 2, :]
    pp_h = P.sb("pp_sb", [128, NPP + 4], F32)
    pp_t = P.tok()
    rp_h = P.sb("rp_sb", [128, NRP + 8], F32)
    rp_t = P.tok()
    PPc = lambda c: pp_h[:, c:c + 1]
    hT_h = P.sb("hT", [128, 8, CMAX], F32)
    hT_t = P.tok(8)
    hn_h = P.sb("hn", [128, 8, CMAX], BF16)
    hn_t = P.tok(8)
    ring_h = [P.sb("ring%d" % i, [128, 4096], BF16) for i in range(NSLOT)]
    ring_t = P.tok(NSLOT)
    S_h = P.sb("S", [128, 4, 128], F32)
    S_t = P.tok()
    Sb_h = P.sb("Sb", [128, 4, 128], BF16)
    Sb_t = P.tok()
    haloA_h = P.sb("haloA", [128, 4, 2], F32)
    haloA_t = P.tok(4)
    haloQ_h = P.sb("haloQ", [128, 12, 3], F32)
    haloQ_t = P.tok(12)
    haloF_h = P.sb("haloF", [128, 2, 22, 2], F32)
    haloF_t = [P.tok(22), P.tok(22)]
    kZ_h = P.sb("kZ", [128, 8, 128 + CMAX], BF16)
    kZ_t = P.tok(8)
    kZm_h = P.sb("kZm", [128, 8, 32], BF16)
    kZm_t = P.tok()
    Vb_h = P.sb("Vb", [128, 6, 4, 65], BF16)
    Vb_t = P.tok(6)
    Vm_h = P.sb("Vm", [32, 4, 65], BF16)
    Vm_t = P.tok()
    arena = Arena(P.sb("arena", [128, 23800], F32), 23800)
    bank_h = [P.ps("bank%d" % i, [128, 512], F32) for i in range(8)]
    bank_t = [Tok(x=True) for _ in range(8)]
    bank_rr = [0]

    def psum(n, dtype=F32, shape=None):
        i = bank_rr[0]
        bank_rr[0] = (i + 1) % 8
        ap = bank_h[i][:, 0:n]
        if dtype != F32:
            ap = ap.bitcast(dtype)
        if shape is not None and len(shape) == 2:
            ap = ap.rearrange("p (a b) -> p a b", a=shape[0])
        return ap, bank_t[i]

    def mm(out, lhsT, rhs, start, stop, reads, writes):
        P.add("pe", lambda: nc.tensor.matmul(out, lhsT, rhs, start=start, stop=stop), reads, writes)

    def tr(out, in_, ident, reads, writes):
        P.add("pe", lambda: nc.tensor.transpose(out, in_, ident), reads, writes)

    def fsz(ap):
        return int(np.prod(ap.shape[1:]))

    def act(out, in_, func, reads, writes, bias=0.0, scale=1.0):
        P.add("act", lambda: nc.scalar.activation(out=out, in_=in_, func=func, bias=bias, scale=scale), reads, writes,
              fs=fsz(out))

    def tt(eng, out, in0, in1, op, reads, writes):
        e = P.e[eng]
        P.add(eng, lambda: e.tensor_tensor(out, in0, in1, op), reads, writes, fs=fsz(out))

    def ts(eng, out, in0, s1, s2, op0, op1, reads, writes):
        e = P.e[eng]
        if op1 is None:
            P.add(eng, lambda: e.tensor_scalar(out, in0, s1, s2, op0), reads, writes, fs=fsz(out))
        else:
            P.add(eng, lambda: e.tensor_scalar(out, in0, s1, s2, op0, op1), reads, writes, fs=fsz(out))

    def stt(eng, out, in0, scalar, in1, op0, op1, reads, writes):
        e = P.e[eng]
        P.add(eng, lambda: e.scalar_tensor_tensor(out, in0, scalar, in1, op0, op1), reads, writes, fs=fsz(out))

    def cp(eng, out, in_, reads, writes):
        e = P.e[eng]
        if eng == "act":
            P.add(eng, lambda: e.copy(out, in_), reads, writes, fs=fsz(out))
        else:
            P.add(eng, lambda: e.tensor_copy(out, in_), reads, writes, fs=fsz(out))

    def memset(eng, ap, v, writes):
        e = P.e[eng]
        P.add(eng, lambda: e.memset(ap, v), (), writes, fs=fsz(ap))

    def recip(out, in_, reads, writes):
        P.add("dve", lambda: nc.vector.reciprocal(out, in_), reads, writes, fs=fsz(out))

    def rsum(out, in_, reads, writes):
        P.add("dve", lambda: nc.vector.reduce_sum(out, in_, AX.X), reads, writes, fs=fsz(out))

    dbg_stage = P.sb("dbg_stage", [128, 1024], F32) if cfg.get("dbg") else None
    dbg_stage_t = P.tok()
    dbg_out_t = P.tok()

    def dump(name, ap2d, toks, parts=128):
        if not cfg.get("dbg") or name not in cfg["dbg"]:
            return
        n = ap2d.shape[1]
        off = dbg_state["off"]
        dbg_state["off"] += n
        dbg_state["names"].append((name, off, n, parts))
        cp("dve", dbg_stage[0:parts, 0:n], ap2d, toks, [dbg_stage_t])
        P.dma("sp", dbg_d[0:parts, off:off + n], dbg_stage[0:parts, 0:n], reads=[dbg_stage_t], writes=[dbg_out_t])

    P.dma("sp", mk_h[:, :, :], mk_d.rearrange("p (m j) -> p m j", m=NM), writes=[mk_t])
    P.dma("sp", pp_h[:, 0:NPP], pp_d[:, :], writes=[pp_t])
    P.dma("sp", rp_h[:, 0:NRP], rp_d[:, :], writes=[rp_t])
    for i, n in enumerate(["ID", "ONES", "BLK64"]):
        P.dma("pool", cb_h[:, i, :], mk_d[:, MI[n] * 128:(MI[n] + 1) * 128], writes=[cb_t])
    QSC, KSC = NPP, NPP + 1
    ts("dve", pp_h[:, QSC:QSC + 1], pp_h[:, PP_QN:PP_QN + 1], 0.125, None, ALU.mult, None, [pp_t], [pp_t])
    cp("dve", pp_h[:, KSC:KSC + 1], pp_h[:, PP_KN:PP_KN + 1], [pp_t], [pp_t])
    NEXPA = NRP
    act(rp_h[:, NEXPA:NEXPA + 4], rp_h[:, RP_ALOG:RP_ALOG + 4], AF.Exp, [rp_t], [rp_t])
    ts("dve", rp_h[:, NEXPA:NEXPA + 4], rp_h[:, NEXPA:NEXPA + 4], -1.0, None, ALU.mult, None, [rp_t], [rp_t])
    act(rp_h[:, RP_SINK:RP_SINK + 16], rp_h[:, RP_SINK:RP_SINK + 16], AF.Exp, [rp_t], [rp_t])
    memset("dve", S_h[:, :, :], 0.0, [S_t])
    memset("dve", Sb_h[:, :, :], 0.0, [Sb_t])
    memset("dve", haloA_h[:, :, :], 0.0, haloA_t)
    memset("dve", haloQ_h[:, :, :], 0.0, haloQ_t)
    memset("dve", haloF_h[:, :, :, :], 0.0, haloF_t[0] + haloF_t[1])
    memset("dve", Vb_h[:, :, :, :], 1.0, Vb_t)
    memset("dve", kZ_h[:, :, :], 0.0, kZ_t)

    def wsrc(name):
        kind = name[0]
        if kind == "in":
            b = name[1]
            if b == "ab":
                return [((8, 8), win_d[:, 3584:3592].rearrange("(k p) n -> p k n", p=128), 0)]
            return [((8, 512), win_d[:, b * 512:(b + 1) * 512].rearrange("(k p) n -> p k n", p=128), 0)]
        if kind in ("out", "q", "o"):
            src = {"out": wout_d, "q": wq_d, "o": wo_d}[kind]
            b = name[1]
            return [((8, 512), src[:, b * 512:(b + 1) * 512].rearrange("(k p) n -> p k n", p=128), 0)]
        if kind == "kv":
            return [((8, 256), wk_d[:, :].rearrange("(k p) n -> p k n", p=128), 0),
                    ((8, 256), wv_d[:, :].rearrange("(k p) n -> p k n", p=128), 2048)]
        if kind == "up":
            l, half, jb = name[1], name[2], name[3]
            n = 512 if jb < 5 else 256
            c0 = half * 2816 + jb * 512
            return [((8, n), wup_d[l, :, c0:c0 + n].rearrange("(k p) n -> p k n", p=128), 0)]
        if kind == "dn":
            l, mp, jh = name[1], name[2], name[3]
            return [((11, 256), wdn_d[l, jh * 1408:(jh + 1) * 1408, mp * 256:(mp + 1) * 256].rearrange("(j p) n -> p j n", p=128), 0)]
        raise ValueError(name)

    def tile_blocks():
        seq = []
        if cfg["mix0"]:
            seq += [("in", b) for b in range(7)] + [("in", "ab"), ("out", 0), ("out", 1)]
        if cfg["ffn0"]:
            for jb in range(6):
                seq += [("up", 0, 0, jb), ("up", 0, 1, jb)]
            seq += [("dn", 0, mp, jh) for mp in range(4) for jh in range(2)]
        if cfg["mix1"]:
            seq += [("q", 0), ("q", 1), ("kv",), ("o", 0), ("o", 1)]
        if cfg["ffn1"]:
            for jb in range(6):
                seq += [("up", 1, 0, jb), ("up", 1, 1, jb)]
            seq += [("dn", 1, mp, jh) for mp in range(4) for jh in range(2)]
        return seq

    wseq = []
    for _ in GROUPS_PER_TILE:
        wseq += tile_blocks()
    wstate = dict(issued=0, used=0, released=0)

    def w_pump():
        while wstate["issued"] < len(wseq) and wstate["issued"] - NSLOT < wstate["released"]:
            i = wstate["issued"]
            s = i % NSLOT
            for (shape, src, off) in wsrc(wseq[i]):
                n = shape[0] * shape[1]
                dst = ring_h[s][:, off:off + n].rearrange("p (k n) -> p k n", k=shape[0])
                P.dma("pool", dst, src, writes=[ring_t[s]])
            wstate["issued"] += 1

    def wnext(name):
        i = wstate["used"]
        assert wseq[i] == name, (wseq[i], name)
        w_pump()
        assert wstate["issued"] > i, "weight ring over-subscribed"
        wstate["used"] += 1
        s = i % NSLOT
        return ring_h[s], ring_t[s]

    def wrel(n=1):
        wstate["released"] += n
        assert wstate["released"] <= wstate["used"]
        w_pump()

    def wview(slot, k, n, off=0):
        return slot[:, off:off + k * n].rearrange("p (k n) -> p k n", k=k)

    def colgroups(C):
        return [(0, 512), (512, C - 512)] if C > 512 else [(0, C)]

    def norm(C, wcol):
        P.label = "norm"
        sqp = TempPool(arena, [C], BF16, 4)
        rs, rst = arena.alloc([C], F32)
        cgs = colgroups(C)
        prs = [psum(n) for (c0, n) in cgs]
        for c in range(8):
            sq, sqt = sqp.get()
            if c % 2 == 1:
                tt("pool", sq, hT_h[:, c, 0:C], hT_h[:, c, 0:C], ALU.mult, [hT_t[c]], sqt)
            else:
                act(sq, hT_h[:, c, 0:C], AF.Square, [hT_t[c]], sqt)
            for (pr, prt), (c0, n) in zip(prs, cgs):
                mm(pr, ONES_B, sq[:, c0:c0 + n], c == 0, c == 7, [cb_t] + sqt, [prt])
        for (pr, prt), (c0, n) in zip(prs, cgs):
            act(rs[:, c0:c0 + n], pr, AF.Ln, [prt], rst, bias=EPS, scale=1.0 / 1024)
        act(rs, rs, AF.Exp, rst, rst, scale=-0.5)
        for c in range(8):
            stt("dve", hn_h[:, c, 0:C], hT_h[:, c, 0:C], PPc(wcol + c), rs, ALU.mult, ALU.mult,
                [hT_t[c], pp_t] + rst, [hn_t[c]])

    def proj(slot, slot_t, kview, c, C, wcols=128):
        res = []
        for (c0, n) in colgroups(C):
            pr, prt = psum(n)
            for k in range(8):
                mm(pr, kview[:, k, c * 128:c * 128 + wcols], hn_h[:, k, c0:c0 + n], k == 0, k == 7,
                   [slot_t, hn_t[k]], [prt])
            res.append((pr, prt, c0, n))
        return res

    def conv_taps(xe, xet, C, K, wcol0, accp):
        acc, acct = accp.get()
        act(acc, xe[:, 0:C], AF.Copy, xet + [pp_t], acct, scale=PPc(wcol0))
        for j in range(1, K):
            stt("dve", acc, xe[:, j:j + C], PPc(wcol0 + j), acc, ALU.mult, ALU.add, xet + [pp_t] + acct, acct)
        return acc, acct

    def load_tile(ti, g0, G):
        C = 128 * G
        P.label = "load"
        xp = TempPool(arena, [1024], F32, 5)
        for g in range(G):
            gg = g0 + g
            xin, xint = xp.get()
            if gg == 0:
                memset("dve", xin, 0.0, xint)
                P.dma("sp", xin[112:128, :], meta_d[:, :], writes=xint)
            else:
                P.dma("sp", xin, x_d[(gg - 1) * 128:gg * 128, :], writes=xint)
            for half in range(2):
                pt, ptt = psum(512, F32, [4, 128])
                for cc in range(4):
                    c = half * 4 + cc
                    tr(pt[:, cc, :], xin[:, c * 128:(c + 1) * 128], MK("ID"), xint + [mk_t], [ptt])
                cp("act" if half == 0 else "dve", hT_h[:, half * 4:half * 4 + 4, g * 128:(g + 1) * 128], pt,
                   [ptt], hT_t[half * 4:half * 4 + 4])

    def store_tile(ti, g0, G):
        P.label = "store"
        xp = TempPool(arena, [1024], F32, 2)
        for g in range(G):
            gg = g0 + g
            if gg == 0:
                continue
            xo, xot = xp.get()
            for half in range(2):
                pt, ptt = psum(512, F32, [4, 128])
                for cc in range(4):
                    c = half * 4 + cc
                    tr(pt[:, cc, :], hT_h[:, c, g * 128:(g + 1) * 128], MK("ID"), [hT_t[c], mk_t], [ptt])
                cp("act" if half == 0 else "dve", xo[:, half * 512:(half + 1) * 512], pt.rearrange("p a b -> p (a b)"),
                   [ptt], xot)
            P.dma("sp", out_d[(gg - 1) * 128:gg * 128, :], xo, reads=xot, writes=[out_t])

    def out_proj(names, rhs_h, rhs_t, C):
        P.label = "outproj"
        slots = [wnext(n) for n in names]
        for m in range(8):
            slot, slot_t = slots[m // 4]
            wv_ = wview(slot, 8, 512)
            mc = m % 4
            for (c0, n) in colgroups(C):
                pr, prt = psum(n)
                for k in range(8):
                    mm(pr, wv_[:, k, mc * 128:(mc + 1) * 128], rhs_h[:, k, c0:c0 + n], k == 0, k == 7,
                       [slot_t, rhs_t[k]], [prt])
                tt("dve", hT_h[:, m, c0:c0 + n], hT_h[:, m, c0:c0 + n], pr, ALU.add, [hT_t[m], prt], [hT_t[m]])
            if m % 4 == 3:
                wrel()

    def ffn(l, C):
        arena.reset()
        norm(C, PP_FNW + l * 8)
        P.label = "ffn_up"
        actb, actt = arena.alloc([22, C], BF16, ntok=22)
        xep = TempPool(arena, [C + 2], F32, 3)
        accp = TempPool(arena, [C], F32, 3)
        slots = {}

        def ffn_chunk(j):
            jb, jc = j // 4, j % 4
            ncol = 512 if jb < 5 else 256
            last = (jc == ncol // 128 - 1)
            P.label = "ffn_up"
            if jc == 0:
                slots[("g", jb)] = wnext(("up", l, 0, jb))
                slots[("v", jb)] = wnext(("up", l, 1, jb))
            sg, sgt = slots[("g", jb)]
            sv, svt = slots[("v", jb)]
            pg = proj(sg, sgt, wview(sg, 8, ncol), jc, C)
            if last:
                wrel()
            xe, xet = xep.get()
            cp("dve", xe[:, 0:2], haloF_h[:, l, j, :], [haloF_t[l][j]], xet)
            for (pr, prt, c0, n) in pg:
                cp("act", xe[:, 2 + c0:2 + c0 + n], pr, [prt], xet)
            yield
            P.label = "ffn_up"
            cp("dve", haloF_h[:, l, j, :], xe[:, C:C + 2], xet, [haloF_t[l][j]])
            acc, acct = conv_taps(xe, xet, C, 3, PP_FC + l * 66 + j * 3, accp)
            pv = proj(sv, svt, wview(sv, 8, ncol), jc, C)
            if last:
                wrel()
            act(acc, acc, AF.Silu, acct, acct)
            yield
            P.label = "ffn_up"
            for (pr, prt, c0, n) in pv:
                tt("dve", actb[:, j, c0:c0 + n], pr, acc[:, c0:c0 + n], ALU.mult, [prt] + acct, [actt[j]])

        run_pipeline([ffn_chunk(j) for j in range(22)], 3, lag=1)
        P.label = "ffn_dn"
        cgs = colgroups(C)
        for mp in range(4):
            regs = {}
            for mi in range(2):
                for ci, (c0, n) in enumerate(cgs):
                    regs[(mi, ci)] = psum(n)
            for jh in range(2):
                sd, sdt = wnext(("dn", l, mp, jh))
                vd = wview(sd, 11, 256)
                for mi in range(2):
                    for ci, (c0, n) in enumerate(cgs):
                        pr, prt = regs[(mi, ci)]
                        for jj in range(11):
                            j = jh * 11 + jj
                            mm(pr, vd[:, jj, mi * 128:(mi + 1) * 128], actb[:, j, c0:c0 + n], j == 0, j == 21,
                               [sdt, actt[j]], [prt])
                wrel()
            for mi in range(2):
                m = mp * 2 + mi
                for ci, (c0, n) in enumerate(cgs):
                    pr, prt = regs[(mi, ci)]
                    tt("dve", hT_h[:, m, c0:c0 + n], hT_h[:, m, c0:c0 + n], pr, ALU.add, [hT_t[m], prt], [hT_t[m]])

    def even_mixer(ti, G):
        C = 128 * G
        arena.reset()
        yab, yabt = arena.alloc([8, C], BF16, ntok=8)
        qkvT, qkvt = arena.alloc([12, C], BF16, ntok=12)
        sz, szt = arena.alloc([G, 512], BF16, ntok=G)
        bg, bgt = arena.alloc([G, 16], F32, ntok=G)
        mark = arena.off
        norm(C, PP_ANW + 0)
        gip = TempPool(arena, [C], F32, 2)
        xep = TempPool(arena, [C + 3], F32, 4)
        accp = TempPool(arena, [C], F32, 4)
        sqp = TempPool(arena, [C], BF16, 3)
        slots = {}

        def conva_chunk(c):
            P.label = "convA"
            if c == 0:
                slots["gi"] = wnext(("in", 0))
                slots["go"] = wnext(("in", 1))
                slots["ah"] = wnext(("in", 2))
            s_gi, t_gi = slots["gi"]
            s_go, t_go = slots["go"]
            s_ah, t_ah = slots["ah"]
            p_gi = proj(s_gi, t_gi, wview(s_gi, 8, 512), c, C)
            p_ah = proj(s_ah, t_ah, wview(s_ah, 8, 512), c, C)
            gi, git = gip.get()
            xe, xet = xep.get()
            cp("dve", xe[:, 0:2], haloA_h[:, c, :], [haloA_t[c]], xet)
            for (pr, prt, c0, n) in p_gi:
                cp("act", gi[:, c0:c0 + n], pr, [prt], git)
            yield
            P.label = "convA"
            for (pr, prt, c0, n) in p_ah:
                tt("dve", xe[:, 2 + c0:2 + c0 + n], pr, gi[:, c0:c0 + n], ALU.mult, [prt] + git, xet)
            cp("dve", haloA_h[:, c, :], xe[:, C:C + 2], xet, [haloA_t[c]])
            p_go = proj(s_go, t_go, wview(s_go, 8, 512), c, C)
            if c == 3:
                wrel(3)
            acc, acct = conv_taps(xe, xet, C, 3, PP_CA + c * 3, accp)
            yield
            P.label = "convA"
            for (pr, prt, c0, n) in p_go:
                tt("dve", yab[:, c, c0:c0 + n], pr, acc[:, c0:c0 + n], ALU.mult, [prt] + acct, [yabt[c]])

        def qkv_chunk(cc):
            b, c = cc // 4, cc % 4
            P.label = "qkv"
            if c == 0:
                slots[("qkv", b)] = wnext(("in", 3 + b))
            sw, swt = slots[("qkv", b)]
            pq = proj(sw, swt, wview(sw, 8, 512), c, C)
            if c == 3:
                wrel()
            xe, xet = xep.get()
            cp("dve", xe[:, 0:3], haloQ_h[:, cc, :], [haloQ_t[cc]], xet)
            for (pr, prt, c0, n) in pq:
                cp("act", xe[:, 3 + c0:3 + c0 + n], pr, [prt], xet)
            yield
            P.label = "qkv"
            cp("dve", haloQ_h[:, cc, :], xe[:, C:C + 3], xet, [haloQ_t[cc]])
            acc, acct = conv_taps(xe, xet, C, 4, PP_DC + cc * 4, accp)
            if b == 2:
                act(qkvT[:, cc, :], acc, AF.Silu, acct, [qkvt[cc]])
                return
            act(qkvT[:, cc, :], acc, AF.Silu, acct, [qkvt[cc]])
            sq, sqt = sqp.get()
            act(sq, qkvT[:, cc, :], AF.Square, [qkvt[cc]], sqt)
            yield
            P.label = "qkv"
            for (c0, n) in colgroups(C):
                pr, prt = psum(n)
                mm(pr, ONES_B, sq[:, c0:c0 + n], True, True, [cb_t] + sqt, [prt])
                cp("dve", ssum[:, cc, c0:c0 + n], pr, [prt], [ssumt[cc]])

        ssum, ssumt = arena.alloc([8, C], F32, ntok=8)
        run_pipeline([conva_chunk(c) for c in range(4)] + [qkv_chunk(cc) for cc in range(12)], 4, lag=1)
        P.label = "qkv"
        for half in range(2):
            hsl = slice(4 * half, 4 * half + 4)
            act(ssum[:, hsl, :], ssum[:, hsl, :], AF.Ln, ssumt[4 * half:4 * half + 4], ssumt[4 * half:4 * half + 4], bias=EPS)
            act(ssum[:, hsl, :], ssum[:, hsl, :], AF.Exp, ssumt[4 * half:4 * half + 4], ssumt[4 * half:4 * half + 4], scale=-0.5)
        for cc in range(8):
            stt("dve", qkvT[:, cc, :], qkvT[:, cc, :], (128 ** -0.5) if cc < 4 else 1.0, ssum[:, cc, :],
                ALU.mult, ALU.mult, [qkvt[cc], ssumt[cc]], [qkvt[cc]])
        P.label = "ztok"
        s_z, t_z = wnext(("in", 6))
        s_ab, t_ab = wnext(("in", "ab"))
        vz, vab = wview(s_z, 8, 512), wview(s_ab, 8, 8)
        for g in range(G):
            gc = slice(g * 128, (g + 1) * 128)
            pz, pzt = psum(512)
            for k in range(8):
                mm(pz, hn_h[:, k, gc], vz[:, k, :], k == 0, k == 7, [t_z, hn_t[k]], [pzt])
            act(sz[:, g, :], pz, AF.Silu, [pzt], [szt[g]])
        for g in range(G):
            gc = slice(g * 128, (g + 1) * 128)
            pab, pabt = psum(8)
            for k in range(8):
                mm(pab, hn_h[:, k, gc], vab[:, k, :], k == 0, k == 7, [t_ab, hn_t[k]], [pabt])
            act(bg[:, g, 0:4], pab[:, 0:4], AF.Sigmoid, [pabt], [bgt[g]])
            tt("dve", bg[:, g, 12:16], pab[:, 4:8], rp_h[:, RP_DTB:RP_DTB + 4], ALU.add, [pabt, rp_t], [bgt[g]])
        ts("dve", bg[:, :, 4:8], bg[:, :, 0:4], -1.0, None, ALU.mult, None, bgt, bgt)
        act(bg[:, :, 12:16], bg[:, :, 12:16], AF.Exp, bgt, bgt)
        act(bg[:, :, 12:16], bg[:, :, 12:16], AF.Ln, bgt, bgt, bias=1.0)
        tt("dve", bg[:, :, 8:12], bg[:, :, 12:16], rp_h[:, NEXPA:NEXPA + 4].unsqueeze(1).broadcast_to([128, G, 4]),
           ALU.mult, bgt + [rp_t], bgt)
        wrel(2)
        arena.off = mark
        bl = lambda ap: ap.unsqueeze(2).broadcast_to([128, 4, 128])

        class DnBufs:
            pass

        def dn_bufs():
            B = DnBufs()
            A4 = lambda dt: arena.alloc([4, 128], dt)
            for nm in ("nkb", "kbd", "kdec", "vb", "nkbT", "qdT", "Ybf", "qkT", "nwT", "vn", "ybt",
                       "Nm", "Mm", "Nl", "Ml", "X", "Y", "Pm", "Qm"):
                setattr(B, nm, A4(BF16))
            for nm in ("gL", "Dl", "Du", "dmTi", "edB"):
                setattr(B, nm, A4(F32))
            B.u, B.o, B.sqo, B.nwz = B.gL, B.Dl, B.Du, B.dmTi
            B.e16 = arena.alloc([16], F32)
            B.bed = arena.alloc([4], F32)
            B.ss = arena.alloc([4], F32)
            return B

        def dn_group(g, B):
            gc = slice(g * 128, (g + 1) * 128)
            beta, nbeta, gg_ = bg[:, g, 0:4], bg[:, g, 4:8], bg[:, g, 8:12]
            nkb, nkbt = B.nkb; kbd, kbdt = B.kbd; kdec, kdect = B.kdec; vb, vbt = B.vb
            nkbT, nkbTt = B.nkbT; qdT, qdTt = B.qdT; Ybf, Ybft = B.Ybf; qkT, qkTt = B.qkT
            nwT, nwTt = B.nwT; vn, vnt = B.vn; ybt, ybtt = B.ybt
            Nm, Nmt = B.Nm; Mm, Mmt = B.Mm; Nl, Nlt = B.Nl; Ml, Mlt = B.Ml
            X, Xt = B.X; Y, Yt = B.Y; Pm, Pmt = B.Pm; Qm, Qmt = B.Qm
            gL, gLt = B.gL; Dl, Dlt = B.Dl; Du, Dut = B.Du; dmTi, dmTit = B.dmTi; edB, edBt = B.edB
            u, ut = B.u; o, ot = B.o; sqo, sqot = B.sqo; nwz, nwzt = B.nwz
            e16, e16t = B.e16; bed, bedt = B.bed; ss, sst = B.ss
            P.label = "dn_prep"
            pst, pstt = psum(512, BF16, [8, 128])
            for hh in range(4):
                tr(pst[:, hh, :], qkvT[:, 4 + hh, gc], ID_B, [qkvt[4 + hh], cb_t], [pstt])
                tr(pst[:, 4 + hh, :], qkvT[:, 8 + hh, gc], ID_B, [qkvt[8 + hh], cb_t], [pstt])
            pse, pset = psum(16)
            mm(pse[:, 0:4], MK("L"), gg_, True, True, [mk_t, bgt[g]], [pset])
            mm(pse[:, 4:8], MK("U"), gg_, True, True, [mk_t, bgt[g]], [pset])
            mm(pse[:, 8:12], MK("MC0"), gg_, True, True, [mk_t, bgt[g]], [pset])
            mm(pse[:, 12:16], MK("MC1"), gg_, True, True, [mk_t, bgt[g]], [pset])
            tt("dve", gL, MK4("L"), bl(gg_), ALU.mult, [mk_t, bgt[g]], gLt)
            yield
            P.label = "dn_prep"
            act(e16, pse, AF.Exp, [pset], e16t)
            tt("dve", bed, beta, e16[:, 0:4], ALU.mult, [bgt[g]] + e16t, bedt)
            tt("dve", nkb, pst[:, 0:4, :], bl(nbeta), ALU.mult, [pstt, bgt[g]], nkbt)
            psD, psDt = psum(512, F32, [4, 128])
            for hh in range(4):
                mm(psD[:, hh, :], gL[:, hh, :], MK("ONES"), True, False, gLt + [mk_t], [psDt])
                mm(psD[:, hh, :], MK("NEGONES"), gL[:, hh, :], False, True, gLt + [mk_t], [psDt])
            psB, psBt = psum(512, F32, [4, 128])
            for hh in range(4):
                mm(psB[:, hh, :], MK("ONES"), gL[:, hh, :], True, True, gLt + [mk_t], [psBt])
            psn, psnt = psum(256, BF16, [4, 128])
            for hh in range(4):
                tr(psn[:, hh, :], nkb[:, hh, :], ID_B, nkbt + [cb_t], [psnt])
            tt("dve", kbd, pst[:, 0:4, :], bl(bed), ALU.mult, [pstt] + bedt, kbdt)
            tt("dve", kdec, pst[:, 0:4, :], bl(e16[:, 4:8]), ALU.mult, [pstt] + e16t, kdect)
            tt("dve", vb, pst[:, 4:8, :], bl(beta), ALU.mult, [pstt, bgt[g]], vbt)
            yield
            P.label = "dn_prep"
            cp("act", nkbT, psn, [psnt], nkbTt)
            tt("dve", Dl, psD, MK4("NEGSL"), ALU.add, [psDt, mk_t], Dlt)
            tt("dve", Du, psD, MK4("POSSU"), ALU.add, [psDt, mk_t], Dut)
            act(Dl, Dl, AF.Exp, Dlt, Dlt)
            act(Du, Du, AF.Exp, Dut, Dut, scale=-1.0)
            act(edB, psB, AF.Exp, [psBt], edBt)
            psN, psNt = psum(512, F32, [4, 128])
            psM, psMt = psum(512, F32, [4, 128])
            psQ, psQt = psum(512, F32, [4, 128])
            for hh in range(4):
                mm(psN[:, hh, :], nkbT[:, hh, :], qkvT[:, 4 + hh, gc], True, True, nkbTt + [qkvt[4 + hh]], [psNt])
            for hh in range(4):
                mm(psM[:, hh, :], qkvT[:, 4 + hh, gc], nkbT[:, hh, :], True, True, nkbTt + [qkvt[4 + hh]], [psMt])
            for hh in range(4):
                mm(psQ[:, hh, :], qkvT[:, 4 + hh, gc], qkvT[:, hh, gc], True, True, [qkvt[4 + hh], qkvt[hh]], [psQt])
            yield
            P.label = "dn_prep"
            tt("dve", dmTi, Du, MK4("ID"), ALU.add, Dut + [mk_t], dmTit)
            tt("dve", qdT, qkvT[:, 0:4, gc], edB, ALU.mult, qkvt[0:4] + edBt, qdTt)
            tt("dve", Nm, psN, Dl, ALU.mult, [psNt] + Dlt, Nmt)
            tt("dve", Mm, psM, Du, ALU.mult, [psMt] + Dut, Mmt)
            tt("dve", qkT, psQ, dmTi, ALU.mult, [psQt] + dmTit, qkTt)
            P.label = "dn_chain"
            tt("dve", Nl, Nm, MK4("LV0"), ALU.mult, Nmt + [mk_t], Nlt)
            tt("dve", X, Nl, MK4("ID"), ALU.add, Nlt + [mk_t], Xt)
            tt("dve", Ml, Mm, MK4("UV0"), ALU.mult, Mmt + [mk_t], Mlt)
            tt("dve", Y, Ml, MK4("ID"), ALU.add, Mlt + [mk_t], Yt)
            for l in range(1, 6):
                last = (l == 5)
                P.label = "dn_chain"
                tt("dve", Nl, Nm, MK4("LV%d" % l), ALU.mult, Nmt + [mk_t], Nlt)
                if not last:
                    tt("dve", Ml, Mm, MK4("UV%d" % l), ALU.mult, Mmt + [mk_t], Mlt)
                pQ, pQt = psum(512, F32, [4, 128])
                for hh in range(4):
                    mm(pQ[:, hh, :], Nl[:, hh, :], Y[:, hh, :], True, True, Nlt + Yt, [pQt])
                if not last:
                    pP, pPt = psum(512, F32, [4, 128])
                    for hh in range(4):
                        mm(pP[:, hh, :], Ml[:, hh, :], X[:, hh, :], True, True, Mlt + Xt, [pPt])
                yield
                P.label = "dn_chain"
                cp("act", Qm, pQ, [pQt], Qmt)
                if not last:
                    cp("act", Pm, pP, [pPt], Pmt)
                pY, pYt = psum(512, F32, [4, 128])
                for hh in range(4):
                    mm(pY[:, hh, :], X[:, hh, :], Qm[:, hh, :], True, True, Xt + Qmt, [pYt])
                if not last:
                    pX, pXt = psum(512, F32, [4, 128])
                    for hh in range(4):
                        mm(pX[:, hh, :], Y[:, hh, :], Pm[:, hh, :], True, True, Yt + Pmt, [pXt])
                yield
                P.label = "dn_chain"
                if not last:
                    tt("dve", Y, Y, pY, ALU.add, Yt + [pYt], Yt)
                    tt("dve", X, X, pX, ALU.add, Xt + [pXt], Xt)
                else:
                    tt("dve", Ybf, Y, pY, ALU.add, Yt + [pYt], Ybft)
            P.label = "dn_uw"
            pU, pUt = psum(512, F32, [4, 128])
            for hh in range(4):
                mm(pU[:, hh, :], Ybf[:, hh, :], vb[:, hh, :], True, True, Ybft + vbt, [pUt])
            pW, pWt = psum(512, F32, [4, 128])
            for hh in range(4):
                mm(pW[:, hh, :], kbd[:, hh, :], Ybf[:, hh, :], True, True, kbdt + Ybft, [pWt])
            yield
            P.label = "dn_uw"
            cp("act", u, pU, [pUt], ut)
            act(nwT, pW, AF.Copy, [pWt], nwTt, scale=-1.0)
            tt("dve", nwz, sz[:, g, :].rearrange("p (a b) -> p a b", a=4),
               rp_h[:, RP_DNW:RP_DNW + 128].unsqueeze(1).broadcast_to([128, 4, 128]), ALU.mult, [szt[g], rp_t], nwzt)
            while scan_turn[0] != g:
                yield
            for cc in range(2):
                P.label = "dn_scan"
                r = slice(64 * cc, 64 * cc + 64)
                pV, pVt = psum(512, F32, [4, 128])
                for hh in range(4):
                    mm(pV[r, hh, :], nwT[:, hh, r], Sb_h[:, hh, :], True, True, nwTt + [Sb_t], [pVt])
                yield
                P.label = "dn_scan"
                tt("dve", vn[r, :, :], pV[r, :, :], u[r, :, :], ALU.add, [pVt] + ut, vnt)
                tt("dve", S_h[:, :, :], S_h[:, :, :], bl(e16[:, 8 + 4 * cc:12 + 4 * cc]), ALU.mult, [S_t] + e16t, [S_t])
                pO, pOt = psum(512, F32, [4, 128])
                for hh in range(4):
                    mm(pO[r, hh, :], qdT[:, hh, r], Sb_h[:, hh, :], True, False, qdTt + [Sb_t], [pOt])
                    mm(pO[r, hh, :], qkT[r, hh, r], vn[r, hh, :], False, True, qkTt + vnt, [pOt])
                pS, pSt = psum(512, F32, [4, 128])
                for hh in range(4):
                    mm(pS[:, hh, :], kdec[r, hh, :], vn[r, hh, :], True, True, kdect + vnt, [pSt])
                yield
                P.label = "dn_scan"
                tt("dve", S_h[:, :, :], S_h[:, :, :], pS, ALU.add, [S_t, pSt], [S_t])
                cp("act", Sb_h[:, :, :], S_h[:, :, :], [S_t], [Sb_t])
                cp("act", o[r, :, :], pO[r, :, :], [pOt], ot)
            scan_turn[0] = g + 1
            P.label = "dn_out"
            act(sqo, o, AF.Square, ot, sqot)
            rsum(ss, sqo, sqot, sst)
            act(ss, ss, AF.Ln, sst, sst, bias=EPS, scale=1.0 / 128)
            act(ss, ss, AF.Exp, sst, sst, scale=-0.5)
            yield
            P.label = "dn_out"
            tt("dve", sqo, o, bl(ss), ALU.mult, ot + sst, sqot)
            tt("dve", ybt, sqo, nwz, ALU.mult, sqot + nwzt, ybtt)
            pT, pTt = psum(256, BF16, [4, 128])
            for hh in range(4):
                tr(pT[:, hh, :], ybt[:, hh, :], ID_B, ybtt + [cb_t], [pTt])
            yield
            P.label = "dn_out"
            cp("act", yab[:, 4:8, gc], pT, [pTt], yabt[4:8])

        sets = [dn_bufs(), dn_bufs()]
        scan_turn = [0]
        oslots = {}

        def outproj_cg(ci, first, lastcg):
            (c0, n) = colgroups(C)[ci]
            P.label = "outproj"
            if first:
                oslots[0] = wnext(("out", 0))
                oslots[1] = wnext(("out", 1))
            for m in range(8):
                P.label = "outproj"
                slot, slot_t = oslots[m // 4]
                wv_ = wview(slot, 8, 512)
                mc = m % 4
                pr, prt = psum(n)
                for k in range(8):
                    mm(pr, wv_[:, k, mc * 128:(mc + 1) * 128], yab[:, k, c0:c0 + n], k == 0, k == 7,
                       [slot_t, yabt[k]], [prt])
                tt("dve", hT_h[:, m, c0:c0 + n], hT_h[:, m, c0:c0 + n], pr, ALU.add, [hT_t[m], prt], [hT_t[m]])
                if lastcg and m % 4 == 3:
                    wrel()
                yield

        gens = [dn_group(g, sets[g % 2]) for g in range(G)]
        if G == 5:
            gens.append(outproj_cg(0, True, False))
        run_pipeline(gens, cfg.get("dn_depth", 2), lag=8)
        if G == 5:
            for _ in outproj_cg(1, False, True):
                pass
        else:
            for _ in outproj_cg(0, True, True):
                pass
        return
        out_proj([("out", 0), ("out", 1)], yab, yabt, C)

    def swa_mixer(ti, G):
        C = 128 * G
        arena.reset()
        norm(C, PP_ANW + 8)
        qT, qTt = arena.alloc([8, C], BF16, ntok=8)
        OT, OTt = arena.alloc([8, C], BF16, ntok=8)
        qrp = TempPool(arena, [C], F32, 4)
        sqp = TempPool(arena, [C], BF16, 4)
        rsp = TempPool(arena, [C], F32, 4)
        Otp = TempPool(arena, [16, 64], BF16, 2)
        Ecp = TempPool(arena, [4, 128], BF16, 5)
        Emp = TempPool(arena, [4, 128], BF16, 5)
        Epp = TempPool(arena, [4, 128], BF16, 5)
        denp = TempPool(arena, [4], F32, 6)
        kTt_, kTtt = arena.alloc([2, C], BF16, ntok=2)
        P.label = "swa_proj"
        sq_slots = [wnext(("q", 0)), wnext(("q", 1))]
        s_kv, t_kv = wnext(("kv",))
        vk = wview(s_kv, 8, 256)

        def head_chunk(kind, c):
            P.label = "swa_proj"
            if kind == "q":
                slot, slot_t = sq_slots[c // 4]
                pq = proj(slot, slot_t, wview(slot, 8, 512), c % 4, C)
                if c % 4 == 3:
                    wrel()
                sccol = QSC
            else:
                pq = proj(s_kv, t_kv, vk, c, C)
                sccol = KSC
            qraw, qrawt = qrp.get()
            sq, sqt = sqp.get()
            for (pr, prt, c0, n) in pq:
                cp("dve", qraw[:, c0:c0 + n], pr, [prt], qrawt)
            act(sq, qraw, AF.Square, qrawt, sqt)
            yield
            P.label = "swa_proj"
            rs, rst = rsp.get()
            for (c0, n) in colgroups(C):
                pr2, pr2t = psum(n)
                mm(pr2, BLK_B, sq[:, c0:c0 + n], True, True, [cb_t] + sqt, [pr2t])
                act(rs[:, c0:c0 + n], pr2, AF.Ln, [pr2t], rst, bias=EPS, scale=1.0 / 64)
            act(rs, rs, AF.Exp, rst, rst, scale=-0.5)
            yield
            P.label = "swa_proj"
            if kind == "q":
                dst, dstt = qT[:, c, :], [qTt[c]]
            else:
                dst, dstt = kTt_[:, c, :], [kTtt[c]]
            stt("dve", dst, qraw, PPc(sccol), rs, ALU.mult, ALU.mult, qrawt + rst + [pp_t], dstt)
            if kind == "k":
                ck = c
                for hk in range(2):
                    kh = 2 * ck + hk
                    rows = slice(64 * hk, 64 * hk + 64)
                    orows = slice(64 * (1 - hk), 64 * (1 - hk) + 64)
                    cp("act", kZ_h[rows, hk * 4 + kh, 128:128 + C], kTt_[rows, ck, :], [kTtt[ck]], [kZ_t[hk * 4 + kh]])
                    P.dma("sp", kZ_h[orows, (1 - hk) * 4 + kh, 128:128 + C], kTt_[rows, ck, :], reads=[kTtt[ck]],
                          writes=[kZ_t[(1 - hk) * 4 + kh]])

        run_pipeline([head_chunk("k", 0), head_chunk("k", 1)] + [head_chunk("q", c) for c in range(8)], 4, lag=1)
        vvw = wview(s_kv, 8, 256, off=2048)
        for g in range(G):
            gc = slice(g * 128, (g + 1) * 128)
            pv, pvt = psum(256, F32, [4, 64])
            for k in range(8):
                mm(pv, hn_h[:, k, gc], vvw[:, k, :], k == 0, k == 7, [t_kv, hn_t[k]], [pvt])
            cp("act", Vb_h[:, 1 + g, :, 0:64], pv, [pvt], [Vb_t[1 + g]])
        wrel()
        if ti == 0:
            cp("dve", kZm_h[:, :, :], kZ_h[:, :, 128 + 96:128 + 128], kZ_t, [kZm_t])
            P.dma("sp", Vm_h[0:32, :, :], Vb_h[96:128, 1, :, :], reads=[Vb_t[1]], writes=[Vm_t])
        otoks = {}

        def attn_unit(g, kh):
            gc = slice(g * 128, (g + 1) * 128)
            is_meta = (ti == 0 and g == 0)
            has_prev = not (ti == 0 and g <= 1)
            P.label = "swa_attn"
            if kh == 0:
                otoks[g] = Otp.get()
            Otok, Otokt = otoks[g]
            if not is_meta:
                pM, pMt = psum(512, F32, [4, 128])
                if has_prev:
                    pP, pPt = psum(512, F32, [4, 128])
            pC, pCt = psum(512, F32, [4, 128])
            for hq in range(2):
                zi = hq * 4 + kh
                q_ap = qT[:, 2 * kh:2 * kh + 2, gc]
                qtk = [qTt[2 * kh], qTt[2 * kh + 1]]
                if not is_meta:
                    mm(pM[0:32, hq:4:2, :], kZm_h[:, zi, :], q_ap, True, True, [kZm_t] + qtk, [pMt])
                    if has_prev:
                        mm(pP[:, hq:4:2, :], kZ_h[:, zi, 128 * g:128 * g + 128], q_ap, True, True,
                           [kZ_t[zi]] + qtk, [pPt])
                mm(pC[:, hq:4:2, :], kZ_h[:, zi, 128 * (g + 1):128 * (g + 1) + 128], q_ap, True, True,
                   [kZ_t[zi]] + qtk, [pCt])
            yield
            P.label = "swa_attn"
            Ec, Ect = Ecp.get()
            act(Ec, pC, AF.Exp, [pCt], Ect)
            tt("dve", Ec, Ec, MK4("SWAMETA" if is_meta else "SWACUR"), ALU.mult, Ect + [mk_t], Ect)
            if not is_meta:
                Em, Emt = Emp.get()
                act(Em[0:32, :, :], pM[0:32, :, :], AF.Exp, [pMt], Emt)
                tt("dve", Em[0:32, :, :], Em[0:32, :, :],
                   mk_h[0:32, MI["METAK"]:MI["METAK"] + 1, :].broadcast_to([32, 4, 128]),
                   ALU.mult, Emt + [mk_t], Emt)
                if has_prev:
                    Ep, Ept = Epp.get()
                    act(Ep, pP, AF.Exp, [pPt], Ept)
                    tt("dve", Ep, Ep, MK4("SWAPREV"), ALU.mult, Ept + [mk_t], Ept)
            yield
            P.label = "swa_attn"
            pO, pOt = psum(512, F32, [4, 128])
            for gq in range(4):
                first = True
                if not is_meta:
                    mm(pO[:, gq, 0:65], Em[0:32, gq, :], Vm_h[0:32, kh, :], True, False, Emt + [Vm_t], [pOt])
                    first = False
                    if has_prev:
                        mm(pO[:, gq, 0:65], Ep[:, gq, :], Vb_h[:, g, kh, :], False, False, Ept + [Vb_t[g]], [pOt])
                mm(pO[:, gq, 0:65], Ec[:, gq, :], Vb_h[:, g + 1, kh, :], first, True, Ect + [Vb_t[g + 1]], [pOt])
            yield
            P.label = "swa_attn"
            den, dent = denp.get()
            tt("dve", den, pO[:, :, 64], rp_h[:, RP_SINK + 4 * kh:RP_SINK + 4 * kh + 4], ALU.add, [pOt, rp_t], dent)
            recip(den, den, dent, dent)
            tt("dve", Otok[:, 4 * kh:4 * kh + 4, :], pO[:, :, 0:64], den.unsqueeze(2).broadcast_to([128, 4, 64]),
               ALU.mult, [pOt] + dent, Otokt)
            if kh == 3:
                yield
                P.label = "swa_attn"
                pT, pTt = psum(512, BF16, [8, 128])
                Of = Otok.rearrange("p a b -> p (a b)")
                for c in range(8):
                    tr(pT[:, c, :], Of[:, c * 128:(c + 1) * 128], ID_B, Otokt + [cb_t], [pTt])
                yield
                P.label = "swa_attn"
                cp("act", OT[:, :, gc], pT, [pTt], OTt)

        run_pipeline([attn_unit(g, kh) for g in range(G) for kh in range(4)], 4, lag=1)
        if cfg.get("swa_dbg") == "noout":
            wnext(("o", 0)); wnext(("o", 1)); wrel(2)
        else:
            if ti == 0:
                dump("hpre", hT_h[:, 0, 128:256], hT_t)
            out_proj([("o", 0), ("o", 1)], OT, OTt, C)
            if ti == 0:
                dump("hpost", hT_h[:, 0, 128:256], hT_t)
        cp("dve", kZ_h[:, :, 0:128], kZ_h[:, :, C:C + 128], kZ_t, kZ_t)
        cp("dve", Vb_h[:, 0, :, 0:64], Vb_h[:, G, :, 0:64], [Vb_t[G]], [Vb_t[0]])

    out_t = P.tok()
    g0 = 0
    for ti, G in enumerate(GROUPS_PER_TILE):
        if ti >= cfg.get("ntiles", 99):
            break
        C = 128 * G
        arena.reset()
        load_tile(ti, g0, G)
        if cfg["mix0"]:
            even_mixer(ti, G)
        if cfg["ffn0"]:
            ffn(0, C)
        if cfg["mix1"]:
            swa_mixer(ti, G)
        if cfg["ffn1"]:
            ffn(1, C)
        arena.reset()
        store_tile(ti, g0, G)
        g0 += G
    P.wait_all("sp", [out_t, dbg_out_t])
    stats = P.emit()
    stats["dbg"] = dbg_state["names"]
    build_program.last_P = P
    return nc, stats


FULL_CFG = dict(mix0=True, ffn0=True, mix1=True, ffn1=True)
_CACHE = {}


def host_params(inp):
    f = lambda a: np.asarray(a, np.float32)
    pp = np.zeros((128, NPP), np.float32)
    for l in range(2):
        pp[:, PP_ANW + l * 8:PP_ANW + l * 8 + 8] = f(inp["attn_norm_w"])[l].reshape(8, 128).T
        pp[:, PP_FNW + l * 8:PP_FNW + l * 8 + 8] = f(inp["ffn_norm_w"])[l].reshape(8, 128).T
        fc = f(inp["ffn_conv_w"])[l].reshape(3, 22, 128)
        pp[:, PP_FC + l * 66:PP_FC + l * 66 + 66] = fc.transpose(2, 1, 0).reshape(128, 66)
    ca = f(inp["conv_a_w"])[0].reshape(3, 4, 128)
    pp[:, PP_CA:PP_CA + 12] = ca.transpose(2, 1, 0).reshape(128, 12)
    dc = f(inp["dn_conv_w"])[0].reshape(4, 12, 128)
    pp[:, PP_DC:PP_DC + 48] = dc.transpose(2, 1, 0).reshape(128, 48)
    pp[:, PP_QN] = np.tile(f(inp["swa_q_norm_w"])[0], 2)
    pp[:, PP_KN] = np.tile(f(inp["swa_k_norm_w"])[0], 2)
    rp = np.zeros((128, NRP), np.float32)
    rp[:, RP_DTB:RP_DTB + 4] = f(inp["dn_dt_bias"])[0][None, :]
    rp[:, RP_ALOG:RP_ALOG + 4] = f(inp["dn_a_log"])[0][None, :]
    rp[:, RP_DNW:RP_DNW + 128] = f(inp["dn_norm_w"])[0][None, :]
    rp[:, RP_SINK:RP_SINK + 16] = f(inp["swa_sinks"])[0][None, :]
    return pp, rp


def run(inp, cfg, built=None):
    if built is None:
        key = tuple(sorted((k, str(v)) for k, v in cfg.items()))
        if key not in _CACHE:
            _CACHE[key] = build_program(cfg)
        built = _CACHE[key]
    nc, stats = built
    f = lambda a: np.ascontiguousarray(np.asarray(a, np.float32))
    pp, rp = host_params(inp)
    masks = make_masks()
    shared = dict(meta=f(inp["meta_tokens"]), w_in=f(inp["mix_w_in"])[0], w_out=f(inp["mix_w_out"])[0],
                  wq=f(inp["swa_wq"])[0], wk=f(inp["swa_wk"])[0], wv=f(inp["swa_wv"])[0], wo=f(inp["swa_wo"])[0],
                  w_up=f(inp["ffn_w_up"]), w_dn=f(inp["ffn_w_down"]), pp=pp, rp=rp, masks=masks)
    x = f(inp["x"])
    in_maps = [dict(shared, x=x[b]) for b in range(8)]
    res = run_bass_kernel_spmd(nc, in_maps, core_ids=list(range(8)))
    if cfg.get("dbg"):
        run.dbg = [np.asarray(r["dbg"]) for r in res.results]
    return np.stack([np.asarray(r["out"], np.float32) for r in res.results], 0)


def kernel(**inputs):
    return run(inputs, FULL_CFG)
```

```python
import contextlib
import numpy as np
import concourse.bass as bass
import concourse.mybir as mybir
from concourse.bass_utils import run_bass_kernel_spmd

F32 = mybir.dt.float32
BF16 = mybir.dt.bfloat16
AF = mybir.ActivationFunctionType
ALU = mybir.AluOpType
AX = mybir.AxisListType

EPS = 1e-6
GROUPS_PER_TILE = [5, 5, 5, 5, 5, 4, 4]
CMAX = 640
NSLOT = 6
MASK_NAMES = ["ID", "ONES", "NEGONES", "BLK64", "L", "U", "MC0", "MC1", "NEGSL", "POSSU",
              "LV0", "LV1", "LV2", "LV3", "LV4", "LV5", "UV0", "UV1", "UV2", "UV3", "UV4", "UV5",
              "SWAPREV", "SWACUR", "SWAMETA", "METAK"]
MI = {n: i for i, n in enumerate(MASK_NAMES)}
PP_ANW, PP_FNW, PP_CA, PP_DC, PP_FC, PP_QN, PP_KN, NPP = 0, 16, 32, 44, 92, 224, 225, 226
RP_DTB, RP_ALOG, RP_DNW, RP_SINK, NRP = 0, 4, 8, 136, 152


class Tok:
    __slots__ = ("w", "r", "x")

    def __init__(self, x=False):
        self.w = None
        self.r = []
        self.x = x


class Op:
    __slots__ = ("eng", "fn", "dma", "waits", "need_inc", "pos", "snap", "sem", "semval", "fs")


class Prog:
    ENGS = ("pe", "act", "dve", "pool", "sp")
    NDMASEM = 12

    def __init__(self, nc):
        self.nc = nc
        self.stack = contextlib.ExitStack()
        self.e = {"pe": nc.tensor, "act": nc.scalar, "dve": nc.vector, "pool": nc.gpsimd, "sp": nc.sync}
        self.ops = []
        self.label = ""
        self.labels = []
        self.npos = {e: 0 for e in self.ENGS}
        self.known = {e: {f: -1 for f in self.ENGS} for e in self.ENGS}
        self.known_dma = {e: set() for e in self.ENGS}
        self.sem = {e: self.stack.enter_context(nc.semaphore("s_" + e)) for e in self.ENGS}
        self.use_scopes = False
        self.dsem, self.dsem_use, self.dsem_last, self.dsem_rr = {}, {}, {}, {}
        for e in ("sp", "pool"):
            self.dsem[e] = [self.stack.enter_context(nc.semaphore("d_%s%d" % (e, i))) for i in range(self.NDMASEM)]
            self.dsem_use[e] = [0] * self.NDMASEM
            self.dsem_last[e] = [None] * self.NDMASEM
            self.dsem_rr[e] = 0

    def sb(self, name, shape, dtype):
        return self.stack.enter_context(self.nc.sbuf_tensor(name, list(shape), dtype))

    def ps(self, name, shape, dtype):
        return self.stack.enter_context(self.nc.psum_tensor(name, list(shape), dtype))

    @staticmethod
    def tok(n=None):
        if n is None:
            return Tok()
        return [Tok() for _ in range(n)]

    def add(self, eng, fn, reads=(), writes=(), dma=False, track=True, fs=1 << 30):
        op = Op()
        opid = len(self.ops)
        op.eng, op.fn, op.dma = eng, fn, dma
        op.fs = fs
        op.need_inc = False
        op.sem = None
        op.semval = 0
        deps = set()
        xr = [t for t in reads if t.x]
        if xr:
            reads = [t for t in reads if not t.x]
            writes = list(writes) + xr
        for t in reads:
            if t.w is not None:
                deps.add(t.w)
        for t in writes:
            if t.w is not None:
                deps.add(t.w)
            deps.update(t.r)
        if dma:
            pool = self.dsem[eng]
            i = self.dsem_rr[eng]
            self.dsem_rr[eng] = (i + 1) % len(pool)
            if self.dsem_last[eng][i] is not None:
                deps.add(self.dsem_last[eng][i])
            self.dsem_use[eng][i] += 1
            op.sem = pool[i]
            op.semval = 16 * self.dsem_use[eng][i]
            self.dsem_last[eng][i] = opid
        known = self.known[eng]
        kd = self.known_dma[eng]
        cw = {}
        waits = []
        for d in deps:
            dop = self.ops[d]
            if dop.dma:
                if d in kd:
                    continue
                kd.add(d)
                waits.append(d)
            else:
                if dop.eng == eng:
                    if not dma and (eng == "pe" or (dop.fs >= 256 and fs >= 256)):
                        continue
                    key = "self_" + eng
                else:
                    key = dop.eng
                if known.get(key, -1) >= dop.pos:
                    continue
                if key not in cw or self.ops[cw[key]].pos < dop.pos:
                    cw[key] = d
        for key, d in cw.items():
            self.ops[d].need_inc = True
            waits.append(d)
        for d in waits:
            dop = self.ops[d]
            for f, v in dop.snap.items():
                if known.get(f, -1) < v:
                    known[f] = v
            if not dop.dma:
                key = dop.eng if dop.eng != eng else "self_" + eng
                if known.get(key, -1) < dop.pos:
                    known[key] = dop.pos
        op.waits = waits
        self.labels.append(self.label)
        op.pos = self.npos[eng]
        if fn is not None:
            self.npos[eng] += 1
        snap = {f: v for f, v in known.items() if not f.startswith("self_")}
        if not dma and fn is not None:
            snap[eng] = op.pos
        op.snap = snap
        self.ops.append(op)
        if not track:
            return opid
        for t in reads:
            t.r.append(opid)
        for t in writes:
            t.w = opid
            t.r = []
        return opid

    def dma(self, eng, out, in_, reads=(), writes=()):
        e = self.e[eng]
        return self.add(eng, lambda: e.dma_start(out=out, in_=in_), reads, writes, dma=True)

    def wait_all(self, eng, toks):
        return self.add(eng, None, reads=(), writes=toks, track=False)

    def emit(self):
        cnt = {e: 0 for e in self.ENGS}
        val = {}
        for i, op in enumerate(self.ops):
            if op.need_inc:
                cnt[op.eng] += 1
                val[i] = cnt[op.eng]
        nw = 0
        cur = None
        scope = None
        for i, op in enumerate(self.ops):
            e = self.e[op.eng]
            if self.use_scopes and self.labels[i] != cur:
                if scope is not None:
                    self.nc.leave_named_scope(cur, scope, False)
                cur = self.labels[i]
                scope = self.nc.enter_named_scope(cur, False)[0]
            for d in op.waits:
                dop = self.ops[d]
                if dop.dma:
                    e.wait_ge(dop.sem, dop.semval)
                else:
                    e.wait_ge(self.sem[dop.eng], val[d])
                nw += 1
            if op.fn is None:
                continue
            ins = op.fn()
            if op.dma:
                ins.then_inc(op.sem, 16)
            elif op.need_inc:
                ins.then_inc(self.sem[op.eng], 1)
        if scope is not None:
            self.nc.leave_named_scope(cur, scope, False)
        self.stats = dict(n_ops=len(self.ops), n_waits=nw, incs=dict(cnt), pos=dict(self.npos))
        return self.stats


class Arena:
    def __init__(self, handle, nwords):
        self.h = handle
        self.n = nwords
        self.off = 0
        self.live = []

    def reset(self):
        self.off = 0

    def alloc(self, free_shape, dtype, ntok=1, parts=128):
        nel = int(np.prod(free_shape))
        words = nel if dtype == F32 else (nel + 1) // 2
        assert self.off + words <= self.n, ("arena overflow", self.off, words, self.n)
        s, e = self.off, self.off + words
        self.off = e
        ap = self.h[:, s:e]
        if dtype != F32:
            ap = ap.bitcast(dtype)
        if len(free_shape) == 2:
            ap = ap.rearrange("p (a b) -> p a b", a=free_shape[0])
        elif len(free_shape) == 3:
            ap = ap.rearrange("p (a b c) -> p a b c", a=free_shape[0], b=free_shape[1])
        toks = [Tok() for _ in range(ntok)]
        inh = []
        keep = []
        for (os_, oe, otoks) in self.live:
            if os_ < e and s < oe:
                for t in otoks:
                    if t.w is not None:
                        inh.append(t.w)
                    inh.extend(t.r)
                if s <= os_ and oe <= e:
                    continue
            keep.append((os_, oe, otoks))
        inh = sorted(set(inh))
        for t in toks:
            t.r = list(inh)
        keep.append((s, e, toks))
        self.live = keep
        return ap, toks


def run_pipeline(gens, depth, lag=0):
    gens = list(gens)
    active = []
    nxt = 0
    while True:
        while len(active) < depth and nxt < len(gens) and (not active or active[-1][1] >= lag):
            active.append([gens[nxt], 0])
            nxt += 1
        if not active:
            break
        for a in list(active):
            try:
                next(a[0])
                a[1] += 1
            except StopIteration:
                active.remove(a)


class TempPool:
    def __init__(self, arena, free_shape, dtype, n=2):
        self.bufs = [arena.alloc(free_shape, dtype) for _ in range(n)]
        self.i = 0

    def get(self):
        b = self.bufs[self.i % len(self.bufs)]
        self.i += 1
        return b


def make_masks():
    i = np.arange(128)[:, None]
    j = np.arange(128)[None, :]
    same = (i // 64) == (j // 64)
    m = {}
    m["ID"] = (i == j)
    m["ONES"] = np.ones((128, 128), bool)
    m["NEGONES"] = -np.ones((128, 128), np.float32)
    m["BLK64"] = same
    m["L"] = same & (i <= j)
    m["U"] = same & (i > j)
    m["MC0"] = (i < 64) & (j >= 0)
    m["MC1"] = (i >= 64) & (j >= 0)
    m["NEGSL"] = np.where(same & (i > j), 0.0, -30000.0)
    m["POSSU"] = np.where(same & (j > i), 0.0, 30000.0)
    for l in range(6):
        b = 1 << l
        lv = ((i // (2 * b)) == (j // (2 * b))) & ((i % (2 * b)) >= b) & ((j % (2 * b)) < b)
        m["LV%d" % l] = lv
        m["UV%d" % l] = lv.T
    m["SWAPREV"] = (i > j)
    m["SWACUR"] = (i <= j)
    m["SWAMETA"] = (i >= 112) & (i <= j)
    m["METAK"] = (i >= 16) & (i < 32) & (j >= 0)
    out = np.zeros((128, len(MASK_NAMES), 128), np.float32)
    for n, k in MI.items():
        out[:, k, :] = m[n].astype(np.float32)
    return out.reshape(128, len(MASK_NAMES) * 128)


def build_program(cfg):
    nc = bass.Bass("TRN2", target_bir_lowering=False)
    P = Prog(nc)
    P.use_scopes = bool(cfg.get("scopes"))

    def dram(name, shape, kind="ExternalInput"):
        return nc.dram_tensor(name, list(shape), F32, kind=kind).ap()

    x_d = dram("x", [4096, 1024])
    meta_d = dram("meta", [16, 1024])
    win_d = dram("w_in", [1024, 3592])
    wout_d = dram("w_out", [1024, 1024])
    wq_d = dram("wq", [1024, 1024])
    wk_d = dram("wk", [1024, 256])
    wv_d = dram("wv", [1024, 256])
    wo_d = dram("wo", [1024, 1024])
    wup_d = dram("w_up", [2, 1024, 5632])
    wdn_d = dram("w_dn", [2, 2816, 1024])
    pp_d = dram("pp", [128, NPP])
    rp_d = dram("rp", [128, NRP])
    mk_d = dram("masks", [128, len(MASK_NAMES) * 128])
    out_d = dram("out", [4096, 1024], kind="ExternalOutput")
    dbg_d = dram("dbg", [128, 8192], kind="ExternalOutput") if cfg.get("dbg") else None
    dbg_state = dict(off=0, names=[])

    NM = len(MASK_NAMES)
    mk_h = P.sb("mk_sb", [128, NM, 128], F32)
    mk_t = P.tok()
    MK = lambda n: mk_h[:, MI[n], :]
    MK4 = lambda n: mk_h[:, MI[n]:MI[n] + 1, :].broadcast_to([128, 4, 128])
    cb_h = P.sb("cb", [128, 3, 128], BF16)
    cb_t = P.tok()
    ID_B, ONES_B, BLK_B = cb_h[:, 0, :], cb_h[:, 1, :], cb_h[:, 2, :]
    pp_h = P.sb("pp_sb", [128, NPP + 4], F32)
    pp_t = P.tok()
    rp_h = P.sb("rp_sb", [128, NRP + 8], F32)
    rp_t = P.tok()
    PPc = lambda c: pp_h[:, c:c + 1]
    hT_h = P.sb("hT", [128, 8, CMAX], F32)
    hT_t = P.tok(8)
    hn_h = P.sb("hn", [128, 8, CMAX], BF16)
    hn_t = P.tok(8)
    ring_h = [P.sb("ring%d" % i, [128, 4096], BF16) for i in range(NSLOT)]
    ring_t = P.tok(NSLOT)
    S_h = P.sb("S", [128, 4, 128], F32)
    S_t = P.tok()
    Sb_h = P.sb("Sb", [128, 4, 128], BF16)
    Sb_t = P.tok()
    haloA_h = P.sb("haloA", [128, 4, 2], F32)
    haloA_t = P.tok(4)
    haloQ_h = P.sb("haloQ", [128, 12, 3], F32)
    haloQ_t = P.tok(12)
    haloF_h = P.sb("haloF", [128, 2, 22, 2], F32)
    haloF_t = [P.tok(22), P.tok(22)]
    kZ_h = P.sb("kZ", [128, 8, 128 + CMAX], BF16)
    kZ_t = P.tok(8)
    kZm_h = P.sb("kZm", [128, 8, 32], BF16)
    kZm_t = P.tok()
    Vb_h = P.sb("Vb", [128, 6, 4, 65], BF16)
    Vb_t = P.tok(6)
    Vm_h = P.sb("Vm", [32, 4, 65], BF16)
    Vm_t = P.tok()
    arena = Arena(P.sb("arena", [128, 23800], F32), 23800)
    bank_h = [P.ps("bank%d" % i, [128, 512], F32) for i in range(8)]
    bank_t = [Tok(x=True) for _ in range(8)]
    bank_rr = [0]

    def psum(n, dtype=F32, shape=None):
        i = bank_rr[0]
        bank_rr[0] = (i + 1) % 8
        ap = bank_h[i][:, 0:n]
        if dtype != F32:
            ap = ap.bitcast(dtype)
        if shape is not None and len(shape) == 2:
            ap = ap.rearrange("p (a b) -> p a b", a=shape[0])
        return ap, bank_t[i]

    def mm(out, lhsT, rhs, start, stop, reads, writes):
        P.add("pe", lambda: nc.tensor.matmul(out, lhsT, rhs, start=start, stop=stop), reads, writes)

    def tr(out, in_, ident, reads, writes):
        P.add("pe", lambda: nc.tensor.transpose(out, in_, ident), reads, writes)

    def fsz(ap):
        return int(np.prod(ap.shape[1:]))

    def act(out, in_, func, reads, writes, bias=0.0, scale=1.0):
        P.add("act", lambda: nc.scalar.activation(out=out, in_=in_, func=func, bias=bias, scale=scale), reads, writes,
              fs=fsz(out))

    def tt(eng, out, in0, in1, op, reads, writes):
        e = P.e[eng]
        P.add(eng, lambda: e.tensor_tensor(out, in0, in1, op), reads, writes, fs=fsz(out))

    def ts(eng, out, in0, s1, s2, op0, op1, reads, writes):
        e = P.e[eng]
        if op1 is None:
            P.add(eng, lambda: e.tensor_scalar(out, in0, s1, s2, op0), reads, writes, fs=fsz(out))
        else:
            P.add(eng, lambda: e.tensor_scalar(out, in0, s1, s2, op0, op1), reads, writes, fs=fsz(out))

    def stt(eng, out, in0, scalar, in1, op0, op1, reads, writes):
        e = P.e[eng]
        P.add(eng, lambda: e.scalar_tensor_tensor(out, in0, scalar, in1, op0, op1), reads, writes, fs=fsz(out))

    def cp(eng, out, in_, reads, writes):
        e = P.e[eng]
        if eng == "act":
            P.add(eng, lambda: e.copy(out, in_), reads, writes, fs=fsz(out))
        else:
            P.add(eng, lambda: e.tensor_copy(out, in_), reads, writes, fs=fsz(out))

    def memset(eng, ap, v, writes):
        e = P.e[eng]
        P.add(eng, lambda: e.memset(ap, v), (), writes, fs=fsz(ap))

    def recip(out, in_, reads, writes):
        P.add("dve", lambda: nc.vector.reciprocal(out, in_), reads, writes, fs=fsz(out))

    def rsum(out, in_, reads, writes):
        P.add("dve", lambda: nc.vector.reduce_sum(out, in_, AX.X), reads, writes, fs=fsz(out))

    dbg_stage = P.sb("dbg_stage", [128, 1024], F32) if cfg.get("dbg") else None
    dbg_stage_t = P.tok()
    dbg_out_t = P.tok()

    def dump(name, ap2d, toks, parts=128):
        if not cfg.get("dbg") or name not in cfg["dbg"]:
            return
        n = ap2d.shape[1]
        off = dbg_state["off"]
        dbg_state["off"] += n
        dbg_state["names"].append((name, off, n, parts))
        cp("dve", dbg_stage[0:parts, 0:n], ap2d, toks, [dbg_stage_t])
        P.dma("sp", dbg_d[0:parts, off:off + n], dbg_stage[0:parts, 0:n], reads=[dbg_stage_t], writes=[dbg_out_t])

    P.dma("sp", mk_h[:, :, :], mk_d.rearrange("p (m j) -> p m j", m=NM), writes=[mk_t])
    P.dma("sp", pp_h[:, 0:NPP], pp_d[:, :], writes=[pp_t])
    P.dma("sp", rp_h[:, 0:NRP], rp_d[:, :], writes=[rp_t])
    for i, n in enumerate(["ID", "ONES", "BLK64"]):
        P.dma("pool", cb_h[:, i, :], mk_d[:, MI[n] * 128:(MI[n] + 1) * 128], writes=[cb_t])
    QSC, KSC = NPP, NPP + 1
    ts("dve", pp_h[:, QSC:QSC + 1], pp_h[:, PP_QN:PP_QN + 1], 0.125, None, ALU.mult, None, [pp_t], [pp_t])
    cp("dve", pp_h[:, KSC:KSC + 1], pp_h[:, PP_KN:PP_KN + 1], [pp_t], [pp_t])
    NEXPA = NRP
    act(rp_h[:, NEXPA:NEXPA + 4], rp_h[:, RP_ALOG:RP_ALOG + 4], AF.Exp, [rp_t], [rp_t])
    ts("dve", rp_h[:, NEXPA:NEXPA + 4], rp_h[:, NEXPA:NEXPA + 4], -1.0, None, ALU.mult, None, [rp_t], [rp_t])
    act(rp_h[:, RP_SINK:RP_SINK + 16], rp_h[:, RP_SINK:RP_SINK + 16], AF.Exp, [rp_t], [rp_t])
    memset("dve", S_h[:, :, :], 0.0, [S_t])
    memset("dve", Sb_h[:, :, :], 0.0, [Sb_t])
    memset("dve", haloA_h[:, :, :], 0.0, haloA_t)
    memset("dve", haloQ_h[:, :, :], 0.0, haloQ_t)
    memset("dve", haloF_h[:, :, :, :], 0.0, haloF_t[0] + haloF_t[1])
    memset("dve", Vb_h[:, :, :, :], 1.0, Vb_t)
    memset("dve", kZ_h[:, :, :], 0.0, kZ_t)

    def wsrc(name):
        kind = name[0]
        if kind == "in":
            b = name[1]
            if b == "ab":
                return [((8, 8), win_d[:, 3584:3592].rearrange("(k p) n -> p k n", p=128), 0)]
            return [((8, 512), win_d[:, b * 512:(b + 1) * 512].rearrange("(k p) n -> p k n", p=128), 0)]
        if kind in ("out", "q", "o"):
            src = {"out": wout_d, "q": wq_d, "o": wo_d}[kind]
            b = name[1]
            return [((8, 512), src[:, b * 512:(b + 1) * 512].rearrange("(k p) n -> p k n", p=128), 0)]
        if kind == "kv":
            return [((8, 256), wk_d[:, :].rearrange("(k p) n -> p k n", p=128), 0),
                    ((8, 256), wv_d[:, :].rearrange("(k p) n -> p k n", p=128), 2048)]
        if kind == "up":
            l, half, jb = name[1], name[2], name[3]
            n = 512 if jb < 5 else 256
            c0 = half * 2816 + jb * 512
            return [((8, n), wup_d[l, :, c0:c0 + n].rearrange("(k p) n -> p k n", p=128), 0)]
        if kind == "dn":
            l, mp, jh = name[1], name[2], name[3]
            return [((11, 256), wdn_d[l, jh * 1408:(jh + 1) * 1408, mp * 256:(mp + 1) * 256].rearrange("(j p) n -> p j n", p=128), 0)]
        raise ValueError(name)

    def tile_blocks():
        seq = []
        if cfg["mix0"]:
            seq += [("in", b) for b in range(7)] + [("in", "ab"), ("out", 0), ("out", 1)]
        if cfg["ffn0"]:
            for jb in range(6):
                seq += [("up", 0, 0, jb), ("up", 0, 1, jb)]
            seq += [("dn", 0, mp, jh) for mp in range(4) for jh in range(2)]
        if cfg["mix1"]:
            seq += [("q", 0), ("q", 1), ("kv",), ("o", 0), ("o", 1)]
        if cfg["ffn1"]:
            for jb in range(6):
                seq += [("up", 1, 0, jb), ("up", 1, 1, jb)]
            seq += [("dn", 1, mp, jh) for mp in range(4) for jh in range(2)]
        return seq

    wseq = []
    for _ in GROUPS_PER_TILE:
        wseq += tile_blocks()
    wstate = dict(issued=0, used=0, released=0)

    def w_pump():
        while wstate["issued"] < len(wseq) and wstate["issued"] - NSLOT < wstate["released"]:
            i = wstate["issued"]
            s = i % NSLOT
            for (shape, src, off) in wsrc(wseq[i]):
                n = shape[0] * shape[1]
                dst = ring_h[s][:, off:off + n].rearrange("p (k n) -> p k n", k=shape[0])
                P.dma("pool", dst, src, writes=[ring_t[s]])
            wstate["issued"] += 1

    def wnext(name):
        i = wstate["used"]
        assert wseq[i] == name, (wseq[i], name)
        w_pump()
        assert wstate["issued"] > i, "weight ring over-subscribed"
        wstate["used"] += 1
        s = i % NSLOT
        return ring_h[s], ring_t[s]

    def wrel(n=1):
        wstate["released"] += n
        assert wstate["released"] <= wstate["used"]
        w_pump()

    def wview(slot, k, n, off=0):
        return slot[:, off:off + k * n].rearrange("p (k n) -> p k n", k=k)

    def colgroups(C):
        return [(0, 512), (512, C - 512)] if C > 512 else [(0, C)]

    def norm(C, wcol):
        P.label = "norm"
        sqp = TempPool(arena, [C], BF16, 4)
        rs, rst = arena.alloc([C], F32)
        cgs = colgroups(C)
        prs = [psum(n) for (c0, n) in cgs]
        for c in range(8):
            sq, sqt = sqp.get()
            if c % 4 == 3:
                tt("pool", sq, hT_h[:, c, 0:C], hT_h[:, c, 0:C], ALU.mult, [hT_t[c]], sqt)
            else:
                act(sq, hT_h[:, c, 0:C], AF.Square, [hT_t[c]], sqt)
            for (pr, prt), (c0, n) in zip(prs, cgs):
                mm(pr, ONES_B, sq[:, c0:c0 + n], c == 0, c == 7, [cb_t] + sqt, [prt])
        for (pr, prt), (c0, n) in zip(prs, cgs):
            act(rs[:, c0:c0 + n], pr, AF.Ln, [prt], rst, bias=EPS, scale=1.0 / 1024)
        act(rs, rs, AF.Exp, rst, rst, scale=-0.5)
        for c in range(8):
            stt("dve", hn_h[:, c, 0:C], hT_h[:, c, 0:C], PPc(wcol + c), rs, ALU.mult, ALU.mult,
                [hT_t[c], pp_t] + rst, [hn_t[c]])

    def proj(slot, slot_t, kview, c, C, wcols=128):
        res = []
        for (c0, n) in colgroups(C):
            pr, prt = psum(n)
            for k in range(8):
                mm(pr, kview[:, k, c * 128:c * 128 + wcols], hn_h[:, k, c0:c0 + n], k == 0, k == 7,
                   [slot_t, hn_t[k]], [prt])
            res.append((pr, prt, c0, n))
        return res

    def conv_taps(xe, xet, C, K, wcol0, accp):
        acc, acct = accp.get()
        act(acc, xe[:, 0:C], AF.Copy, xet + [pp_t], acct, scale=PPc(wcol0))
        for j in range(1, K):
            stt("dve", acc, xe[:, j:j + C], PPc(wcol0 + j), acc, ALU.mult, ALU.add, xet + [pp_t] + acct, acct)
        return acc, acct

    def load_tile(ti, g0, G):
        C = 128 * G
        P.label = "load"
        xp = TempPool(arena, [1024], F32, 5)
        for g in range(G):
            gg = g0 + g
            xin, xint = xp.get()
            if gg == 0:
                memset("dve", xin, 0.0, xint)
                P.dma("sp", xin[112:128, :], meta_d[:, :], writes=xint)
            else:
                P.dma("sp", xin, x_d[(gg - 1) * 128:gg * 128, :], writes=xint)
            for half in range(2):
                pt, ptt = psum(512, F32, [4, 128])
                for cc in range(4):
                    c = half * 4 + cc
                    tr(pt[:, cc, :], xin[:, c * 128:(c + 1) * 128], MK("ID"), xint + [mk_t], [ptt])
                cp("act" if half == 0 else "dve", hT_h[:, half * 4:half * 4 + 4, g * 128:(g + 1) * 128], pt,
                   [ptt], hT_t[half * 4:half * 4 + 4])

    def store_tile(ti, g0, G):
        P.label = "store"
        xp = TempPool(arena, [1024], F32, 2)
        for g in range(G):
            gg = g0 + g
            if gg == 0:
                continue
            xo, xot = xp.get()
            for half in range(2):
                pt, ptt = psum(512, F32, [4, 128])
                for cc in range(4):
                    c = half * 4 + cc
                    tr(pt[:, cc, :], hT_h[:, c, g * 128:(g + 1) * 128], MK("ID"), [hT_t[c], mk_t], [ptt])
                cp("act" if half == 0 else "dve", xo[:, half * 512:(half + 1) * 512], pt.rearrange("p a b -> p (a b)"),
                   [ptt], xot)
            P.dma("sp", out_d[(gg - 1) * 128:gg * 128, :], xo, reads=xot, writes=[out_t])

    def out_proj(names, rhs_h, rhs_t, C):
        P.label = "outproj"
        slots = [wnext(n) for n in names]
        for m in range(8):
            slot, slot_t = slots[m // 4]
            wv_ = wview(slot, 8, 512)
            mc = m % 4
            for (c0, n) in colgroups(C):
                pr, prt = psum(n)
                for k in range(8):
                    mm(pr, wv_[:, k, mc * 128:(mc + 1) * 128], rhs_h[:, k, c0:c0 + n], k == 0, k == 7,
                       [slot_t, rhs_t[k]], [prt])
                tt("dve", hT_h[:, m, c0:c0 + n], hT_h[:, m, c0:c0 + n], pr, ALU.add, [hT_t[m], prt], [hT_t[m]])
            if m % 4 == 3:
                wrel()

    def ffn(l, C):
        arena.reset()
        norm(C, PP_FNW + l * 8)
        P.label = "ffn_up"
        actb, actt = arena.alloc([22, C], BF16, ntok=22)
        xep = TempPool(arena, [C + 2], F32, 3)
        accp = TempPool(arena, [C], F32, 3)
        slots = {}

        def ffn_chunk(j):
            jb, jc = j // 4, j % 4
            ncol = 512 if jb < 5 else 256
            last = (jc == ncol // 128 - 1)
            P.label = "ffn_up"
            if jc == 0:
                slots[("g", jb)] = wnext(("up", l, 0, jb))
                slots[("v", jb)] = wnext(("up", l, 1, jb))
            sg, sgt = slots[("g", jb)]
            sv, svt = slots[("v", jb)]
            pg = proj(sg, sgt, wview(sg, 8, ncol), jc, C)
            if last:
                wrel()
            xe, xet = xep.get()
            cp("dve", xe[:, 0:2], haloF_h[:, l, j, :], [haloF_t[l][j]], xet)
            for (pr, prt, c0, n) in pg:
                cp("act", xe[:, 2 + c0:2 + c0 + n], pr, [prt], xet)
            yield
            P.label = "ffn_up"
            cp("dve", haloF_h[:, l, j, :], xe[:, C:C + 2], xet, [haloF_t[l][j]])
            acc, acct = conv_taps(xe, xet, C, 3, PP_FC + l * 66 + j * 3, accp)
            pv = proj(sv, svt, wview(sv, 8, ncol), jc, C)
            if last:
                wrel()
            act(acc, acc, AF.Silu, acct, acct)
            yield
            P.label = "ffn_up"
            for (pr, prt, c0, n) in pv:
                tt("dve", actb[:, j, c0:c0 + n], pr, acc[:, c0:c0 + n], ALU.mult, [prt] + acct, [actt[j]])

        run_pipeline([ffn_chunk(j) for j in range(22)], 3, lag=1)
        P.label = "ffn_dn"
        cgs = colgroups(C)
        for mp in range(4):
            regs = {}
            for mi in range(2):
                for ci, (c0, n) in enumerate(cgs):
                    regs[(mi, ci)] = psum(n)
            for jh in range(2):
                sd, sdt = wnext(("dn", l, mp, jh))
                vd = wview(sd, 11, 256)
                for mi in range(2):
                    for ci, (c0, n) in enumerate(cgs):
                        pr, prt = regs[(mi, ci)]
                        for jj in range(11):
                            j = jh * 11 + jj
                            mm(pr, vd[:, jj, mi * 128:(mi + 1) * 128], actb[:, j, c0:c0 + n], j == 0, j == 21,
                               [sdt, actt[j]], [prt])
                wrel()
            for mi in range(2):
                m = mp * 2 + mi
                for ci, (c0, n) in enumerate(cgs):
                    pr, prt = regs[(mi, ci)]
                    tt("dve", hT_h[:, m, c0:c0 + n], hT_h[:, m, c0:c0 + n], pr, ALU.add, [hT_t[m], prt], [hT_t[m]])

    def even_mixer(ti, G):
        C = 128 * G
        arena.reset()
        yab, yabt = arena.alloc([8, C], BF16, ntok=8)
        qkvT, qkvt = arena.alloc([12, C], BF16, ntok=12)
        sz, szt = arena.alloc([G, 512], BF16, ntok=G)
        bg, bgt = arena.alloc([G, 16], F32, ntok=G)
        mark = arena.off
        norm(C, PP_ANW + 0)
        gip = TempPool(arena, [C], F32, 2)
        xep = TempPool(arena, [C + 3], F32, 4)
        accp = TempPool(arena, [C], F32, 4)
        sqp = TempPool(arena, [C], BF16, 3)
        slots = {}

        def conva_chunk(c):
            P.label = "convA"
            if c == 0:
                slots["gi"] = wnext(("in", 0))
                slots["go"] = wnext(("in", 1))
                slots["ah"] = wnext(("in", 2))
            s_gi, t_gi = slots["gi"]
            s_go, t_go = slots["go"]
            s_ah, t_ah = slots["ah"]
            p_gi = proj(s_gi, t_gi, wview(s_gi, 8, 512), c, C)
            p_ah = proj(s_ah, t_ah, wview(s_ah, 8, 512), c, C)
            gi, git = gip.get()
            xe, xet = xep.get()
            cp("dve", xe[:, 0:2], haloA_h[:, c, :], [haloA_t[c]], xet)
            for (pr, prt, c0, n) in p_gi:
                cp("act", gi[:, c0:c0 + n], pr, [prt], git)
            yield
            P.label = "convA"
            for (pr, prt, c0, n) in p_ah:
                tt("dve", xe[:, 2 + c0:2 + c0 + n], pr, gi[:, c0:c0 + n], ALU.mult, [prt] + git, xet)
            cp("dve", haloA_h[:, c, :], xe[:, C:C + 2], xet, [haloA_t[c]])
            p_go = proj(s_go, t_go, wview(s_go, 8, 512), c, C)
            if c == 3:
                wrel(3)
            acc, acct = conv_taps(xe, xet, C, 3, PP_CA + c * 3, accp)
            yield
            P.label = "convA"
            for (pr, prt, c0, n) in p_go:
                tt("dve", yab[:, c, c0:c0 + n], pr, acc[:, c0:c0 + n], ALU.mult, [prt] + acct, [yabt[c]])

        def qkv_chunk(cc):
            b, c = cc // 4, cc % 4
            P.label = "qkv"
            if c == 0:
                slots[("qkv", b)] = wnext(("in", 3 + b))
            sw, swt = slots[("qkv", b)]
            pq = proj(sw, swt, wview(sw, 8, 512), c, C)
            if c == 3:
                wrel()
            xe, xet = xep.get()
            cp("dve", xe[:, 0:3], haloQ_h[:, cc, :], [haloQ_t[cc]], xet)
            for (pr, prt, c0, n) in pq:
                cp("act", xe[:, 3 + c0:3 + c0 + n], pr, [prt], xet)
            yield
            P.label = "qkv"
            cp("dve", haloQ_h[:, cc, :], xe[:, C:C + 3], xet, [haloQ_t[cc]])
            acc, acct = conv_taps(xe, xet, C, 4, PP_DC + cc * 4, accp)
            if b == 2:
                act(qkvT[:, cc, :], acc, AF.Silu, acct, [qkvt[cc]])
                return
            act(qkvT[:, cc, :], acc, AF.Silu, acct, [qkvt[cc]])
            sq, sqt = sqp.get()
            act(sq, qkvT[:, cc, :], AF.Square, [qkvt[cc]], sqt)
            yield
            P.label = "qkv"
            for (c0, n) in colgroups(C):
                pr, prt = psum(n)
                mm(pr, ONES_B, sq[:, c0:c0 + n], True, True, [cb_t] + sqt, [prt])
                cp("dve", ssum[:, cc, c0:c0 + n], pr, [prt], [ssumt[cc]])

        ssum, ssumt = arena.alloc([8, C], F32, ntok=8)
        run_pipeline([conva_chunk(c) for c in range(4)] + [qkv_chunk(cc) for cc in range(12)], 4, lag=1)
        P.label = "qkv"
        for half in range(2):
            hsl = slice(4 * half, 4 * half + 4)
            act(ssum[:, hsl, :], ssum[:, hsl, :], AF.Ln, ssumt[4 * half:4 * half + 4], ssumt[4 * half:4 * half + 4], bias=EPS)
            act(ssum[:, hsl, :], ssum[:, hsl, :], AF.Exp, ssumt[4 * half:4 * half + 4], ssumt[4 * half:4 * half + 4], scale=-0.5)
        for cc in range(8):
            stt("dve", qkvT[:, cc, :], qkvT[:, cc, :], (128 ** -0.5) if cc < 4 else 1.0, ssum[:, cc, :],
                ALU.mult, ALU.mult, [qkvt[cc], ssumt[cc]], [qkvt[cc]])
        P.label = "ztok"
        s_z, t_z = wnext(("in", 6))
        s_ab, t_ab = wnext(("in", "ab"))
        vz, vab = wview(s_z, 8, 512), wview(s_ab, 8, 8)
        for g in range(G):
            gc = slice(g * 128, (g + 1) * 128)
            pz, pzt = psum(512)
            for k in range(8):
                mm(pz, hn_h[:, k, gc], vz[:, k, :], k == 0, k == 7, [t_z, hn_t[k]], [pzt])
            act(sz[:, g, :], pz, AF.Silu, [pzt], [szt[g]])
        for g in range(G):
            gc = slice(g * 128, (g + 1) * 128)
            pab, pabt = psum(8)
            for k in range(8):
                mm(pab, hn_h[:, k, gc], vab[:, k, :], k == 0, k == 7, [t_ab, hn_t[k]], [pabt])
            act(bg[:, g, 0:4], pab[:, 0:4], AF.Sigmoid, [pabt], [bgt[g]])
            tt("dve", bg[:, g, 12:16], pab[:, 4:8], rp_h[:, RP_DTB:RP_DTB + 4], ALU.add, [pabt, rp_t], [bgt[g]])
        ts("dve", bg[:, :, 4:8], bg[:, :, 0:4], -1.0, None, ALU.mult, None, bgt, bgt)
        act(bg[:, :, 12:16], bg[:, :, 12:16], AF.Exp, bgt, bgt)
        act(bg[:, :, 12:16], bg[:, :, 12:16], AF.Ln, bgt, bgt, bias=1.0)
        tt("dve", bg[:, :, 8:12], bg[:, :, 12:16], rp_h[:, NEXPA:NEXPA + 4].unsqueeze(1).broadcast_to([128, G, 4]),
           ALU.mult, bgt + [rp_t], bgt)
        wrel(2)
        arena.off = mark
        bl = lambda ap: ap.unsqueeze(2).broadcast_to([128, 4, 128])

        class DnBufs:
            pass

        def dn_bufs():
            B = DnBufs()
            A4 = lambda dt: arena.alloc([4, 128], dt)
            for nm in ("nkb", "kbd", "kdec", "vb", "nkbT", "qdT", "Ybf", "qkT", "nwT", "vn", "ybt",
                       "Nm", "Mm", "Nl", "Ml", "X", "Y", "Pm", "Qm"):
                setattr(B, nm, A4(BF16))
            for nm in ("gL", "Dl", "Du", "dmTi", "edB"):
                setattr(B, nm, A4(F32))
            B.u, B.o, B.sqo, B.nwz = B.gL, B.Dl, B.Du, B.dmTi
            B.e16 = arena.alloc([16], F32)
            B.bed = arena.alloc([4], F32)
            B.ss = arena.alloc([4], F32)
            return B

        def dn_group(g, B):
            gc = slice(g * 128, (g + 1) * 128)
            beta, nbeta, gg_ = bg[:, g, 0:4], bg[:, g, 4:8], bg[:, g, 8:12]
            nkb, nkbt = B.nkb; kbd, kbdt = B.kbd; kdec, kdect = B.kdec; vb, vbt = B.vb
            nkbT, nkbTt = B.nkbT; qdT, qdTt = B.qdT; Ybf, Ybft = B.Ybf; qkT, qkTt = B.qkT
            nwT, nwTt = B.nwT; vn, vnt = B.vn; ybt, ybtt = B.ybt
            Nm, Nmt = B.Nm; Mm, Mmt = B.Mm; Nl, Nlt = B.Nl; Ml, Mlt = B.Ml
            X, Xt = B.X; Y, Yt = B.Y; Pm, Pmt = B.Pm; Qm, Qmt = B.Qm
            gL, gLt = B.gL; Dl, Dlt = B.Dl; Du, Dut = B.Du; dmTi, dmTit = B.dmTi; edB, edBt = B.edB
            u, ut = B.u; o, ot = B.o; sqo, sqot = B.sqo; nwz, nwzt = B.nwz
            e16, e16t = B.e16; bed, bedt = B.bed; ss, sst = B.ss
            P.label = "dn_prep"
            pst, pstt = psum(512, BF16, [8, 128])
            for hh in range(4):
                tr(pst[:, hh, :], qkvT[:, 4 + hh, gc], ID_B, [qkvt[4 + hh], cb_t], [pstt])
                tr(pst[:, 4 + hh, :], qkvT[:, 8 + hh, gc], ID_B, [qkvt[8 + hh], cb_t], [pstt])
            pse, pset = psum(16)
            mm(pse[:, 0:4], MK("L"), gg_, True, True, [mk_t, bgt[g]], [pset])
            mm(pse[:, 4:8], MK("U"), gg_, True, True, [mk_t, bgt[g]], [pset])
            mm(pse[:, 8:12], MK("MC0"), gg_, True, True, [mk_t, bgt[g]], [pset])
            mm(pse[:, 12:16], MK("MC1"), gg_, True, True, [mk_t, bgt[g]], [pset])
            tt("dve", gL, MK4("L"), bl(gg_), ALU.mult, [mk_t, bgt[g]], gLt)
            yield
            P.label = "dn_prep"
            act(e16, pse, AF.Exp, [pset], e16t)
            tt("dve", bed, beta, e16[:, 0:4], ALU.mult, [bgt[g]] + e16t, bedt)
            tt("dve", nkb, pst[:, 0:4, :], bl(nbeta), ALU.mult, [pstt, bgt[g]], nkbt)
            psD, psDt = psum(512, F32, [4, 128])
            for hh in range(4):
                mm(psD[:, hh, :], gL[:, hh, :], MK("ONES"), True, False, gLt + [mk_t], [psDt])
                mm(psD[:, hh, :], MK("NEGONES"), gL[:, hh, :], False, True, gLt + [mk_t], [psDt])
            psB, psBt = psum(512, F32, [4, 128])
            for hh in range(4):
                mm(psB[:, hh, :], MK("ONES"), gL[:, hh, :], True, True, gLt + [mk_t], [psBt])
            psn, psnt = psum(256, BF16, [4, 128])
            for hh in range(4):
                tr(psn[:, hh, :], nkb[:, hh, :], ID_B, nkbt + [cb_t], [psnt])
            tt("dve", kbd, pst[:, 0:4, :], bl(bed), ALU.mult, [pstt] + bedt, kbdt)
            tt("dve", kdec, pst[:, 0:4, :], bl(e16[:, 4:8]), ALU.mult, [pstt] + e16t, kdect)
            tt("dve", vb, pst[:, 4:8, :], bl(beta), ALU.mult, [pstt, bgt[g]], vbt)
            yield
            P.label = "dn_prep"
            cp("act", nkbT, psn, [psnt], nkbTt)
            tt("dve", Dl, psD, MK4("NEGSL"), ALU.add, [psDt, mk_t], Dlt)
            tt("dve", Du, psD, MK4("POSSU"), ALU.add, [psDt, mk_t], Dut)
            act(Dl, Dl, AF.Exp, Dlt, Dlt)
            act(Du, Du, AF.Exp, Dut, Dut, scale=-1.0)
            act(edB, psB, AF.Exp, [psBt], edBt)
            psN, psNt = psum(512, F32, [4, 128])
            psM, psMt = psum(512, F32, [4, 128])
            psQ, psQt = psum(512, F32, [4, 128])
            for hh in range(4):
                mm(psN[:, hh, :], nkbT[:, hh, :], qkvT[:, 4 + hh, gc], True, True, nkbTt + [qkvt[4 + hh]], [psNt])
            for hh in range(4):
                mm(psM[:, hh, :], qkvT[:, 4 + hh, gc], nkbT[:, hh, :], True, True, nkbTt + [qkvt[4 + hh]], [psMt])
            for hh in range(4):
                mm(psQ[:, hh, :], qkvT[:, 4 + hh, gc], qkvT[:, hh, gc], True, True, [qkvt[4 + hh], qkvt[hh]], [psQt])
            yield
            P.label = "dn_prep"
            tt("dve", dmTi, Du, MK4("ID"), ALU.add, Dut + [mk_t], dmTit)
            tt("dve", qdT, qkvT[:, 0:4, gc], edB, ALU.mult, qkvt[0:4] + edBt, qdTt)
            tt("dve", Nm, psN, Dl, ALU.mult, [psNt] + Dlt, Nmt)
            tt("dve", Mm, psM, Du, ALU.mult, [psMt] + Dut, Mmt)
            tt("dve", qkT, psQ, dmTi, ALU.mult, [psQt] + dmTit, qkTt)
            P.label = "dn_chain"
            tt("dve", Nl, Nm, MK4("LV0"), ALU.mult, Nmt + [mk_t], Nlt)
            tt("dve", X, Nl, MK4("ID"), ALU.add, Nlt + [mk_t], Xt)
            tt("dve", Ml, Mm, MK4("UV0"), ALU.mult, Mmt + [mk_t], Mlt)
            tt("dve", Y, Ml, MK4("ID"), ALU.add, Mlt + [mk_t], Yt)
            for l in range(1, 6):
                last = (l == 5)
                P.label = "dn_chain"
                tt("dve", Nl, Nm, MK4("LV%d" % l), ALU.mult, Nmt + [mk_t], Nlt)
                if not last:
                    tt("dve", Ml, Mm, MK4("UV%d" % l), ALU.mult, Mmt + [mk_t], Mlt)
                pQ, pQt = psum(512, F32, [4, 128])
                for hh in range(4):
                    mm(pQ[:, hh, :], Nl[:, hh, :], Y[:, hh, :], True, True, Nlt + Yt, [pQt])
                if not last:
                    pP, pPt = psum(512, F32, [4, 128])
                    for hh in range(4):
                        mm(pP[:, hh, :], Ml[:, hh, :], X[:, hh, :], True, True, Mlt + Xt, [pPt])
                yield
                P.label = "dn_chain"
                cp("act", Qm, pQ, [pQt], Qmt)
                if not last:
                    cp("act", Pm, pP, [pPt], Pmt)
                pY, pYt = psum(512, F32, [4, 128])
                for hh in range(4):
                    mm(pY[:, hh, :], X[:, hh, :], Qm[:, hh, :], True, True, Xt + Qmt, [pYt])
                if not last:
                    pX, pXt = psum(512, F32, [4, 128])
                    for hh in range(4):
                        mm(pX[:, hh, :], Y[:, hh, :], Pm[:, hh, :], True, True, Yt + Pmt, [pXt])
                yield
                P.label = "dn_chain"
                if not last:
                    tt("dve", Y, Y, pY, ALU.add, Yt + [pYt], Yt)
                    tt("dve", X, X, pX, ALU.add, Xt + [pXt], Xt)
                else:
                    tt("dve", Ybf, Y, pY, ALU.add, Yt + [pYt], Ybft)
            P.label = "dn_uw"
            pU, pUt = psum(512, F32, [4, 128])
            for hh in range(4):
                mm(pU[:, hh, :], Ybf[:, hh, :], vb[:, hh, :], True, True, Ybft + vbt, [pUt])
            pW, pWt = psum(512, F32, [4, 128])
            for hh in range(4):
                mm(pW[:, hh, :], kbd[:, hh, :], Ybf[:, hh, :], True, True, kbdt + Ybft, [pWt])
            yield
            P.label = "dn_uw"
            cp("act", u, pU, [pUt], ut)
            act(nwT, pW, AF.Copy, [pWt], nwTt, scale=-1.0)
            tt("dve", nwz, sz[:, g, :].rearrange("p (a b) -> p a b", a=4),
               rp_h[:, RP_DNW:RP_DNW + 128].unsqueeze(1).broadcast_to([128, 4, 128]), ALU.mult, [szt[g], rp_t], nwzt)
            while scan_turn[0] != g:
                yield
            for cc in range(2):
                P.label = "dn_scan"
                r = slice(64 * cc, 64 * cc + 64)
                pV, pVt = psum(512, F32, [4, 128])
                for hh in range(4):
                    mm(pV[r, hh, :], nwT[:, hh, r], Sb_h[:, hh, :], True, True, nwTt + [Sb_t], [pVt])
                yield
                P.label = "dn_scan"
                tt("dve", vn[r, :, :], pV[r, :, :], u[r, :, :], ALU.add, [pVt] + ut, vnt)
                tt("dve", S_h[:, :, :], S_h[:, :, :], bl(e16[:, 8 + 4 * cc:12 + 4 * cc]), ALU.mult, [S_t] + e16t, [S_t])
                pO, pOt = psum(512, F32, [4, 128])
                for hh in range(4):
                    mm(pO[r, hh, :], qdT[:, hh, r], Sb_h[:, hh, :], True, False, qdTt + [Sb_t], [pOt])
                    mm(pO[r, hh, :], qkT[r, hh, r], vn[r, hh, :], False, True, qkTt + vnt, [pOt])
                pS, pSt = psum(512, F32, [4, 128])
                for hh in range(4):
                    mm(pS[:, hh, :], kdec[r, hh, :], vn[r, hh, :], True, True, kdect + vnt, [pSt])
                yield
                P.label = "dn_scan"
                tt("dve", S_h[:, :, :], S_h[:, :, :], pS, ALU.add, [S_t, pSt], [S_t])
                cp("act", Sb_h[:, :, :], S_h[:, :, :], [S_t], [Sb_t])
                cp("act", o[r, :, :], pO[r, :, :], [pOt], ot)
            scan_turn[0] = g + 1
            P.label = "dn_out"
            act(sqo, o, AF.Square, ot, sqot)
            rsum(ss, sqo, sqot, sst)
            act(ss, ss, AF.Ln, sst, sst, bias=EPS, scale=1.0 / 128)
            act(ss, ss, AF.Exp, sst, sst, scale=-0.5)
            yield
            P.label = "dn_out"
            tt("dve", sqo, o, bl(ss), ALU.mult, ot + sst, sqot)
            tt("dve", ybt, sqo, nwz, ALU.mult, sqot + nwzt, ybtt)
            pT, pTt = psum(256, BF16, [4, 128])
            for hh in range(4):
                tr(pT[:, hh, :], ybt[:, hh, :], ID_B, ybtt + [cb_t], [pTt])
            yield
            P.label = "dn_out"
            cp("act", yab[:, 4:8, gc], pT, [pTt], yabt[4:8])

        sets = [dn_bufs(), dn_bufs()]
        scan_turn = [0]
        oslots = {}

        def outproj_cg(ci, first, lastcg):
            (c0, n) = colgroups(C)[ci]
            P.label = "outproj"
            if first:
                oslots[0] = wnext(("out", 0))
                oslots[1] = wnext(("out", 1))
            for m in range(8):
                P.label = "outproj"
                slot, slot_t = oslots[m // 4]
                wv_ = wview(slot, 8, 512)
                mc = m % 4
                pr, prt = psum(n)
                for k in range(8):
                    mm(pr, wv_[:, k, mc * 128:(mc + 1) * 128], yab[:, k, c0:c0 + n], k == 0, k == 7,
                       [slot_t, yabt[k]], [prt])
                tt("dve", hT_h[:, m, c0:c0 + n], hT_h[:, m, c0:c0 + n], pr, ALU.add, [hT_t[m], prt], [hT_t[m]])
                if lastcg and m % 4 == 3:
                    wrel()
                yield

        gens = [dn_group(g, sets[g % 2]) for g in range(G)]
        if G == 5:
            gens.append(outproj_cg(0, True, False))
        run_pipeline(gens, cfg.get("dn_depth", 2), lag=8)
        if G == 5:
            for _ in outproj_cg(1, False, True):
                pass
        else:
            for _ in outproj_cg(0, True, True):
                pass
        return
        out_proj([("out", 0), ("out", 1)], yab, yabt, C)

    def swa_mixer(ti, G):
        C = 128 * G
        arena.reset()
        norm(C, PP_ANW + 8)
        qT, qTt = arena.alloc([8, C], BF16, ntok=8)
        OT, OTt = arena.alloc([8, C], BF16, ntok=8)
        qrp = TempPool(arena, [C], F32, 4)
        sqp = TempPool(arena, [C], BF16, 4)
        rsp = TempPool(arena, [C], F32, 4)
        Otp = TempPool(arena, [16, 64], BF16, 2)
        Ecp = TempPool(arena, [4, 128], BF16, 5)
        Emp = TempPool(arena, [4, 128], BF16, 5)
        Epp = TempPool(arena, [4, 128], BF16, 5)
        denp = TempPool(arena, [4], F32, 6)
        kTt_, kTtt = arena.alloc([2, C], BF16, ntok=2)
        P.label = "swa_proj"
        sq_slots = [wnext(("q", 0)), wnext(("q", 1))]
        s_kv, t_kv = wnext(("kv",))
        vk = wview(s_kv, 8, 256)

        def head_chunk(kind, c):
            P.label = "swa_proj"
            if kind == "q":
                slot, slot_t = sq_slots[c // 4]
                pq = proj(slot, slot_t, wview(slot, 8, 512), c % 4, C)
                if c % 4 == 3:
                    wrel()
                sccol = QSC
            else:
                pq = proj(s_kv, t_kv, vk, c, C)
                sccol = KSC
            qraw, qrawt = qrp.get()
            sq, sqt = sqp.get()
            for (pr, prt, c0, n) in pq:
                cp("act", qraw[:, c0:c0 + n], pr, [prt], qrawt)
            act(sq, qraw, AF.Square, qrawt, sqt)
            yield
            P.label = "swa_proj"
            rs, rst = rsp.get()
            for (c0, n) in colgroups(C):
                pr2, pr2t = psum(n)
                mm(pr2, BLK_B, sq[:, c0:c0 + n], True, True, [cb_t] + sqt, [pr2t])
                act(rs[:, c0:c0 + n], pr2, AF.Ln, [pr2t], rst, bias=EPS, scale=1.0 / 64)
            act(rs, rs, AF.Exp, rst, rst, scale=-0.5)
            yield
            P.label = "swa_proj"
            if kind == "q":
                dst, dstt = qT[:, c, :], [qTt[c]]
            else:
                dst, dstt = kTt_[:, c, :], [kTtt[c]]
            stt("dve", dst, qraw, PPc(sccol), rs, ALU.mult, ALU.mult, qrawt + rst + [pp_t], dstt)
            if kind == "k":
                ck = c
                for hk in range(2):
                    kh = 2 * ck + hk
                    rows = slice(64 * hk, 64 * hk + 64)
                    orows = slice(64 * (1 - hk), 64 * (1 - hk) + 64)
                    cp("act", kZ_h[rows, hk * 4 + kh, 128:128 + C], kTt_[rows, ck, :], [kTtt[ck]], [kZ_t[hk * 4 + kh]])
                    P.dma("sp", kZ_h[orows, (1 - hk) * 4 + kh, 128:128 + C], kTt_[rows, ck, :], reads=[kTtt[ck]],
                          writes=[kZ_t[(1 - hk) * 4 + kh]])

        run_pipeline([head_chunk("k", 0), head_chunk("k", 1)] + [head_chunk("q", c) for c in range(8)], 4, lag=1)
        vvw = wview(s_kv, 8, 256, off=2048)
        for g in range(G):
            gc = slice(g * 128, (g + 1) * 128)
            pv, pvt = psum(256, F32, [4, 64])
            for k in range(8):
                mm(pv, hn_h[:, k, gc], vvw[:, k, :], k == 0, k == 7, [t_kv, hn_t[k]], [pvt])
            cp("act", Vb_h[:, 1 + g, :, 0:64], pv, [pvt], [Vb_t[1 + g]])
        wrel()
        if ti == 0:
            cp("dve", kZm_h[:, :, :], kZ_h[:, :, 128 + 96:128 + 128], kZ_t, [kZm_t])
            P.dma("sp", Vm_h[0:32, :, :], Vb_h[96:128, 1, :, :], reads=[Vb_t[1]], writes=[Vm_t])
        otoks = {}

        def attn_unit(g, kh):
            gc = slice(g * 128, (g + 1) * 128)
            is_meta = (ti == 0 and g == 0)
            has_prev = not (ti == 0 and g <= 1)
            P.label = "swa_attn"
            if kh == 0:
                otoks[g] = Otp.get()
            Otok, Otokt = otoks[g]
            if not is_meta:
                pM, pMt = psum(512, F32, [4, 128])
                if has_prev:
                    pP, pPt = psum(512, F32, [4, 128])
            pC, pCt = psum(512, F32, [4, 128])
            for hq in range(2):
                zi = hq * 4 + kh
                q_ap = qT[:, 2 * kh:2 * kh + 2, gc]
                qtk = [qTt[2 * kh], qTt[2 * kh + 1]]
                if not is_meta:
                    mm(pM[0:32, hq:4:2, :], kZm_h[:, zi, :], q_ap, True, True, [kZm_t] + qtk, [pMt])
                    if has_prev:
                        mm(pP[:, hq:4:2, :], kZ_h[:, zi, 128 * g:128 * g + 128], q_ap, True, True,
                           [kZ_t[zi]] + qtk, [pPt])
                mm(pC[:, hq:4:2, :], kZ_h[:, zi, 128 * (g + 1):128 * (g + 1) + 128], q_ap, True, True,
                   [kZ_t[zi]] + qtk, [pCt])
            yield
            P.label = "swa_attn"
            Ec, Ect = Ecp.get()
            act(Ec, pC, AF.Exp, [pCt], Ect)
            tt("dve", Ec, Ec, MK4("SWAMETA" if is_meta else "SWACUR"), ALU.mult, Ect + [mk_t], Ect)
            if not is_meta:
                Em, Emt = Emp.get()
                act(Em[0:32, :, :], pM[0:32, :, :], AF.Exp, [pMt], Emt)
                tt("dve", Em[0:32, :, :], Em[0:32, :, :],
                   mk_h[0:32, MI["METAK"]:MI["METAK"] + 1, :].broadcast_to([32, 4, 128]),
                   ALU.mult, Emt + [mk_t], Emt)
                if has_prev:
                    Ep, Ept = Epp.get()
                    act(Ep, pP, AF.Exp, [pPt], Ept)
                    tt("dve", Ep, Ep, MK4("SWAPREV"), ALU.mult, Ept + [mk_t], Ept)
            yield
            P.label = "swa_attn"
            pO, pOt = psum(512, F32, [4, 128])
            for gq in range(4):
                first = True
                if not is_meta:
                    mm(pO[:, gq, 0:65], Em[0:32, gq, :], Vm_h[0:32, kh, :], True, False, Emt + [Vm_t], [pOt])
                    first = False
                    if has_prev:
                        mm(pO[:, gq, 0:65], Ep[:, gq, :], Vb_h[:, g, kh, :], False, False, Ept + [Vb_t[g]], [pOt])
                mm(pO[:, gq, 0:65], Ec[:, gq, :], Vb_h[:, g + 1, kh, :], first, True, Ect + [Vb_t[g + 1]], [pOt])
            yield
            P.label = "swa_attn"
            den, dent = denp.get()
            tt("dve", den, pO[:, :, 64], rp_h[:, RP_SINK + 4 * kh:RP_SINK + 4 * kh + 4], ALU.add, [pOt, rp_t], dent)
            recip(den, den, dent, dent)
            tt("dve", Otok[:, 4 * kh:4 * kh + 4, :], pO[:, :, 0:64], den.unsqueeze(2).broadcast_to([128, 4, 64]),
               ALU.mult, [pOt] + dent, Otokt)
            if kh == 3:
                yield
                P.label = "swa_attn"
                pT, pTt = psum(512, BF16, [8, 128])
                Of = Otok.rearrange("p a b -> p (a b)")
                for c in range(8):
                    tr(pT[:, c, :], Of[:, c * 128:(c + 1) * 128], ID_B, Otokt + [cb_t], [pTt])
                yield
                P.label = "swa_attn"
                cp("act", OT[:, :, gc], pT, [pTt], OTt)

        run_pipeline([attn_unit(g, kh) for g in range(G) for kh in range(4)], 4, lag=1)
        if cfg.get("swa_dbg") == "noout":
            wnext(("o", 0)); wnext(("o", 1)); wrel(2)
        else:
            if ti == 0:
                dump("hpre", hT_h[:, 0, 128:256], hT_t)
            out_proj([("o", 0), ("o", 1)], OT, OTt, C)
            if ti == 0:
                dump("hpost", hT_h[:, 0, 128:256], hT_t)
        cp("dve", kZ_h[:, :, 0:128], kZ_h[:, :, C:C + 128], kZ_t, kZ_t)
        cp("dve", Vb_h[:, 0, :, 0:64], Vb_h[:, G, :, 0:64], [Vb_t[G]], [Vb_t[0]])

    out_t = P.tok()
    g0 = 0
    for ti, G in enumerate(GROUPS_PER_TILE):
        if ti >= cfg.get("ntiles", 99):
            break
        C = 128 * G
        arena.reset()
        load_tile(ti, g0, G)
        if cfg["mix0"]:
            even_mixer(ti, G)
        if cfg["ffn0"]:
            ffn(0, C)
        if cfg["mix1"]:
            swa_mixer(ti, G)
        if cfg["ffn1"]:
            ffn(1, C)
        arena.reset()
        store_tile(ti, g0, G)
        g0 += G
    P.wait_all("sp", [out_t, dbg_out_t])
    stats = P.emit()
    stats["dbg"] = dbg_state["names"]
    build_program.last_P = P
    return nc, stats


FULL_CFG = dict(mix0=True, ffn0=True, mix1=True, ffn1=True)
_CACHE = {}


def host_params(inp):
    f = lambda a: np.asarray(a, np.float32)
    pp = np.zeros((128, NPP), np.float32)
    for l in range(2):
        pp[:, PP_ANW + l * 8:PP_ANW + l * 8 + 8] = f(inp["attn_norm_w"])[l].reshape(8, 128).T
        pp[:, PP_FNW + l * 8:PP_FNW + l * 8 + 8] = f(inp["ffn_norm_w"])[l].reshape(8, 128).T
        fc = f(inp["ffn_conv_w"])[l].reshape(3, 22, 128)
        pp[:, PP_FC + l * 66:PP_FC + l * 66 + 66] = fc.transpose(2, 1, 0).reshape(128, 66)
    ca = f(inp["conv_a_w"])[0].reshape(3, 4, 128)
    pp[:, PP_CA:PP_CA + 12] = ca.transpose(2, 1, 0).reshape(128, 12)
    dc = f(inp["dn_conv_w"])[0].reshape(4, 12, 128)
    pp[:, PP_DC:PP_DC + 48] = dc.transpose(2, 1, 0).reshape(128, 48)
    pp[:, PP_QN] = np.tile(f(inp["swa_q_norm_w"])[0], 2)
    pp[:, PP_KN] = np.tile(f(inp["swa_k_norm_w"])[0], 2)
    rp = np.zeros((128, NRP), np.float32)
    rp[:, RP_DTB:RP_DTB + 4] = f(inp["dn_dt_bias"])[0][None, :]
    rp[:, RP_ALOG:RP_ALOG + 4] = f(inp["dn_a_log"])[0][None, :]
    rp[:, RP_DNW:RP_DNW + 128] = f(inp["dn_norm_w"])[0][None, :]
    rp[:, RP_SINK:RP_SINK + 16] = f(inp["swa_sinks"])[0][None, :]
    return pp, rp


def run(inp, cfg, built=None):
    if built is None:
        key = tuple(sorted((k, str(v)) for k, v in cfg.items()))
        if key not in _CACHE:
            _CACHE[key] = build_program(cfg)
        built = _CACHE[key]
    nc, stats = built
    f = lambda a: np.ascontiguousarray(np.asarray(a, np.float32))
    pp, rp = host_params(inp)
    masks = make_masks()
    shared = dict(meta=f(inp["meta_tokens"]), w_in=f(inp["mix_w_in"])[0], w_out=f(inp["mix_w_out"])[0],
                  wq=f(inp["swa_wq"])[0], wk=f(inp["swa_wk"])[0], wv=f(inp["swa_wv"])[0], wo=f(inp["swa_wo"])[0],
                  w_up=f(inp["ffn_w_up"]), w_dn=f(inp["ffn_w_down"]), pp=pp, rp=rp, masks=masks)
    x = f(inp["x"])
    in_maps = [dict(shared, x=x[b]) for b in range(8)]
    res = run_bass_kernel_spmd(nc, in_maps, core_ids=list(range(8)))
    if cfg.get("dbg"):
        run.dbg = [np.asarray(r["dbg"]) for r in res.results]
    return np.stack([np.asarray(r["out"], np.float32) for r in res.results], 0)


def kernel(**inputs):
    return run(inputs, FULL_CFG)
```

```python
import contextlib
import numpy as np
import concourse.bass as bass
import concourse.mybir as mybir
from concourse.bass_utils import run_bass_kernel_spmd

F32 = mybir.dt.float32
BF16 = mybir.dt.bfloat16
AF = mybir.ActivationFunctionType
ALU = mybir.AluOpType
AX = mybir.AxisListType

EPS = 1e-6
GROUPS_PER_TILE = [5, 5, 5, 5, 5, 4, 4]
CMAX = 640
NSLOT = 6
MASK_NAMES = ["ID", "ONES", "NEGONES", "BLK64", "L", "U", "MC0", "MC1", "NEGSL", "POSSU",
              "LV0", "LV1", "LV2", "LV3", "LV4", "LV5", "UV0", "UV1", "UV2", "UV3", "UV4", "UV5",
              "SWAPREV", "SWACUR", "SWAMETA", "METAK"]
MI = {n: i for i, n in enumerate(MASK_NAMES)}
PP_ANW, PP_FNW, PP_CA, PP_DC, PP_FC, PP_QN, PP_KN, NPP = 0, 16, 32, 44, 92, 224, 225, 226
RP_DTB, RP_ALOG, RP_DNW, RP_SINK, NRP = 0, 4, 8, 136, 152


class Tok:
    __slots__ = ("w", "r", "x")

    def __init__(self, x=False):
        self.w = None
        self.r = []
        self.x = x


class Op:
    __slots__ = ("eng", "fn", "dma", "waits", "need_inc", "pos", "snap", "sem", "semval", "fs")


class Prog:
    ENGS = ("pe", "act", "dve", "pool", "sp")
    NDMASEM = 12

    def __init__(self, nc):
        self.nc = nc
        self.stack = contextlib.ExitStack()
        self.e = {"pe": nc.tensor, "act": nc.scalar, "dve": nc.vector, "pool": nc.gpsimd, "sp": nc.sync}
        self.ops = []
        self.label = ""
        self.labels = []
        self.npos = {e: 0 for e in self.ENGS}
        self.known = {e: {f: -1 for f in self.ENGS} for e in self.ENGS}
        self.known_dma = {e: set() for e in self.ENGS}
        self.sem = {e: self.stack.enter_context(nc.semaphore("s_" + e)) for e in self.ENGS}
        self.use_scopes = False
        self.dsem, self.dsem_use, self.dsem_last, self.dsem_rr = {}, {}, {}, {}
        for e in ("sp", "pool"):
            self.dsem[e] = [self.stack.enter_context(nc.semaphore("d_%s%d" % (e, i))) for i in range(self.NDMASEM)]
            self.dsem_use[e] = [0] * self.NDMASEM
            self.dsem_last[e] = [None] * self.NDMASEM
            self.dsem_rr[e] = 0

    def sb(self, name, shape, dtype):
        return self.stack.enter_context(self.nc.sbuf_tensor(name, list(shape), dtype))

    def ps(self, name, shape, dtype):
        return self.stack.enter_context(self.nc.psum_tensor(name, list(shape), dtype))

    @staticmethod
    def tok(n=None):
        if n is None:
            return Tok()
        return [Tok() for _ in range(n)]

    def add(self, eng, fn, reads=(), writes=(), dma=False, track=True, fs=1 << 30):
        op = Op()
        opid = len(self.ops)
        op.eng, op.fn, op.dma = eng, fn, dma
        op.fs = fs
        op.need_inc = False
        op.sem = None
        op.semval = 0
        deps = set()
        xr = [t for t in reads if t.x]
        if xr:
            reads = [t for t in reads if not t.x]
            writes = list(writes) + xr
        for t in reads:
            if t.w is not None:
                deps.add(t.w)
        for t in writes:
            if t.w is not None:
                deps.add(t.w)
            deps.update(t.r)
        if dma:
            pool = self.dsem[eng]
            i = self.dsem_rr[eng]
            self.dsem_rr[eng] = (i + 1) % len(pool)
            if self.dsem_last[eng][i] is not None:
                deps.add(self.dsem_last[eng][i])
            self.dsem_use[eng][i] += 1
            op.sem = pool[i]
            op.semval = 16 * self.dsem_use[eng][i]
            self.dsem_last[eng][i] = opid
        known = self.known[eng]
        kd = self.known_dma[eng]
        cw = {}
        waits = []
        for d in deps:
            dop = self.ops[d]
            if dop.dma:
                if d in kd:
                    continue
                kd.add(d)
                waits.append(d)
            else:
                if dop.eng == eng:
                    if not dma and (eng == "pe" or (dop.fs >= 256 and fs >= 256)):
                        continue
                    key = "self_" + eng
                else:
                    key = dop.eng
                if known.get(key, -1) >= dop.pos:
                    continue
                if key not in cw or self.ops[cw[key]].pos < dop.pos:
                    cw[key] = d
        for key, d in cw.items():
            self.ops[d].need_inc = True
            waits.append(d)
        for d in waits:
            dop = self.ops[d]
            for f, v in dop.snap.items():
                if known.get(f, -1) < v:
                    known[f] = v
            if not dop.dma:
                key = dop.eng if dop.eng != eng else "self_" + eng
                if known.get(key, -1) < dop.pos:
                    known[key] = dop.pos
        op.waits = waits
        self.labels.append(self.label)
        op.pos = self.npos[eng]
        if fn is not None:
            self.npos[eng] += 1
        snap = {f: v for f, v in known.items() if not f.startswith("self_")}
        if not dma and fn is not None:
            snap[eng] = op.pos
        op.snap = snap
        self.ops.append(op)
        if not track:
            return opid
        for t in reads:
            t.r.append(opid)
        for t in writes:
            t.w = opid
            t.r = []
        return opid

    def dma(self, eng, out, in_, reads=(), writes=()):
        e = self.e[eng]
        return self.add(eng, lambda: e.dma_start(out=out, in_=in_), reads, writes, dma=True)

    def wait_all(self, eng, toks):
        return self.add(eng, None, reads=(), writes=toks, track=False)

    def emit(self):
        cnt = {e: 0 for e in self.ENGS}
        val = {}
        for i, op in enumerate(self.ops):
            if op.need_inc:
                cnt[op.eng] += 1
                val[i] = cnt[op.eng]
        nw = 0
        cur = None
        scope = None
        for i, op in enumerate(self.ops):
            e = self.e[op.eng]
            if self.use_scopes and self.labels[i] != cur:
                if scope is not None:
                    self.nc.leave_named_scope(cur, scope, False)
                cur = self.labels[i]
                scope = self.nc.enter_named_scope(cur, False)[0]
            for d in op.waits:
                dop = self.ops[d]
                if dop.dma:
                    e.wait_ge(dop.sem, dop.semval)
                else:
                    e.wait_ge(self.sem[dop.eng], val[d])
                nw += 1
            if op.fn is None:
                continue
            ins = op.fn()
            if op.dma:
                ins.then_inc(op.sem, 16)
            elif op.need_inc:
                ins.then_inc(self.sem[op.eng], 1)
        if scope is not None:
            self.nc.leave_named_scope(cur, scope, False)
        self.stats = dict(n_ops=len(self.ops), n_waits=nw, incs=dict(cnt), pos=dict(self.npos))
        return self.stats


class Arena:
    def __init__(self, handle, nwords):
        self.h = handle
        self.n = nwords
        self.off = 0
        self.live = []

    def reset(self):
        self.off = 0

    def alloc(self, free_shape, dtype, ntok=1, parts=128):
        nel = int(np.prod(free_shape))
        words = nel if dtype == F32 else (nel + 1) // 2
        assert self.off + words <= self.n, ("arena overflow", self.off, words, self.n)
        s, e = self.off, self.off + words
        self.off = e
        ap = self.h[:, s:e]
        if dtype != F32:
            ap = ap.bitcast(dtype)
        if len(free_shape) == 2:
            ap = ap.rearrange("p (a b) -> p a b", a=free_shape[0])
        elif len(free_shape) == 3:
            ap = ap.rearrange("p (a b c) -> p a b c", a=free_shape[0], b=free_shape[1])
        toks = [Tok() for _ in range(ntok)]
        inh = []
        keep = []
        for (os_, oe, otoks) in self.live:
            if os_ < e and s < oe:
                for t in otoks:
                    if t.w is not None:
                        inh.append(t.w)
                    inh.extend(t.r)
                if s <= os_ and oe <= e:
                    continue
            keep.append((os_, oe, otoks))
        inh = sorted(set(inh))
        for t in toks:
            t.r = list(inh)
        keep.append((s, e, toks))
        self.live = keep
        return ap, toks


def run_pipeline(gens, depth, lag=0):
    gens = list(gens)
    active = []
    nxt = 0
    while True:
        while len(active) < depth and nxt < len(gens) and (not active or active[-1][1] >= lag):
            active.append([gens[nxt], 0])
            nxt += 1
        if not active:
            break
        for a in list(active):
            try:
                next(a[0])
                a[1] += 1
            except StopIteration:
                active.remove(a)


class TempPool:
    def __init__(self, arena, free_shape, dtype, n=2):
        self.bufs = [arena.alloc(free_shape, dtype) for _ in range(n)]
        self.i = 0

    def get(self):
        b = self.bufs[self.i % len(self.bufs)]
        self.i += 1
        return b


def make_masks():
    i = np.arange(128)[:, None]
    j = np.arange(128)[None, :]
    same = (i // 64) == (j // 64)
    m = {}
    m["ID"] = (i == j)
    m["ONES"] = np.ones((128, 128), bool)
    m["NEGONES"] = -np.ones((128, 128), np.float32)
    m["BLK64"] = same
    m["L"] = same & (i <= j)
    m["U"] = same & (i > j)
    m["MC0"] = (i < 64) & (j >= 0)
    m["MC1"] = (i >= 64) & (j >= 0)
    m["NEGSL"] = np.where(same & (i > j), 0.0, -30000.0)
    m["POSSU"] = np.where(same & (j > i), 0.0, 30000.0)
    for l in range(6):
        b = 1 << l
        lv = ((i // (2 * b)) == (j // (2 * b))) & ((i % (2 * b)) >= b) & ((j % (2 * b)) < b)
        m["LV%d" % l] = lv
        m["UV%d" % l] = lv.T
    m["SWAPREV"] = (i > j)
    m["SWACUR"] = (i <= j)
    m["SWAMETA"] = (i >= 112) & (i <= j)
    m["METAK"] = (i >= 16) & (i < 32) & (j >= 0)
    out = np.zeros((128, len(MASK_NAMES), 128), np.float32)
    for n, k in MI.items():
        out[:, k, :] = m[n].astype(np.float32)
    return out.reshape(128, len(MASK_NAMES) * 128)


def build_program(cfg):
    nc = bass.Bass("TRN2", target_bir_lowering=False)
    P = Prog(nc)
    P.use_scopes = bool(cfg.get("scopes"))

    def dram(name, shape, kind="ExternalInput"):
        return nc.dram_tensor(name, list(shape), F32, kind=kind).ap()

    x_d = dram("x", [4096, 1024])
    meta_d = dram("meta", [16, 1024])
    win_d = dram("w_in", [1024, 3592])
    wout_d = dram("w_out", [1024, 1024])
    wq_d = dram("wq", [1024, 1024])
    wk_d = dram("wk", [1024, 256])
    wv_d = dram("wv", [1024, 256])
    wo_d = dram("wo", [1024, 1024])
    wup_d = dram("w_up", [2, 1024, 5632])
    wdn_d = dram("w_dn", [2, 2816, 1024])
    pp_d = dram("pp", [128, NPP])
    rp_d = dram("rp", [128, NRP])
    mk_d = dram("masks", [128, len(MASK_NAMES) * 128])
    out_d = dram("out", [4096, 1024], kind="ExternalOutput")
    dbg_d = dram("dbg", [128, 8192], kind="ExternalOutput") if cfg.get("dbg") else None
    dbg_state = dict(off=0, names=[])

    NM = len(MASK_NAMES)
    mk_h = P.sb("mk_sb", [128, NM, 128], F32)
    mk_t = P.tok()
    MK = lambda n: mk_h[:, MI[n], :]
    MK4 = lambda n: mk_h[:, MI[n]:MI[n] + 1, :].broadcast_to([128, 4, 128])
    cb_h = P.sb("cb", [128, 3, 128], BF16)
    cb_t = P.tok()
    ID_B, ONES_B, BLK_B = cb_h[:, 0, :], cb_h[:, 1, :], cb_h[:, 2, :]
    pp_h = P.sb("pp_sb", [128, NPP + 4], F32)
    pp_t = P.tok()
    rp_h = P.sb("rp_sb", [128, NRP + 8], F32)
    rp_t = P.tok()
    PPc = lambda c: pp_h[:, c:c + 1]
    hT_h = P.sb("hT", [128, 8, CMAX], F32)
    hT_t = P.tok(8)
    hn_h = P.sb("hn", [128, 8, CMAX], BF16)
    hn_t = P.tok(8)
    ring_h = [P.sb("ring%d" % i, [128, 4096], BF16) for i in range(NSLOT)]
    ring_t = P.tok(NSLOT)
    S_h = P.sb("S", [128, 4, 128], F32)
    S_t = P.tok()
    Sb_h = P.sb("Sb", [128, 4, 128], BF16)
    Sb_t = P.tok()
    haloA_h = P.sb("haloA", [128, 4, 2], F32)
    haloA_t = P.tok(4)
    haloQ_h = P.sb("haloQ", [128, 12, 3], F32)
    haloQ_t = P.tok(12)
    haloF_h = P.sb("haloF", [128, 2, 22, 2], F32)
    haloF_t = [P.tok(22), P.tok(22)]
    kZ_h = P.sb("kZ", [128, 8, 128 + CMAX], BF16)
    kZ_t = P.tok(8)
    kZm_h = P.sb("kZm", [128, 8, 32], BF16)
    kZm_t = P.tok()
    Vb_h = P.sb("Vb", [128, 6, 4, 65], BF16)
    Vb_t = P.tok(6)
    Vm_h = P.sb("Vm", [32, 4, 65], BF16)
    Vm_t = P.tok()
    arena = Arena(P.sb("arena", [128, 23800], F32), 23800)
    bank_h = [P.ps("bank%d" % i, [128, 512], F32) for i in range(8)]
    bank_t = [Tok(x=True) for _ in range(8)]
    bank_rr = [0]

    def psum(n, dtype=F32, shape=None):
        i = bank_rr[0]
        bank_rr[0] = (i + 1) % 8
        ap = bank_h[i][:, 0:n]
        if dtype != F32:
            ap = ap.bitcast(dtype)
        if shape is not None and len(shape) == 2:
            ap = ap.rearrange("p (a b) -> p a b", a=shape[0])
        return ap, bank_t[i]

    def mm(out, lhsT, rhs, start, stop, reads, writes):
        P.add("pe", lambda: nc.tensor.matmul(out, lhsT, rhs, start=start, stop=stop), reads, writes)

    def tr(out, in_, ident, reads, writes):
        P.add("pe", lambda: nc.tensor.transpose(out, in_, ident), reads, writes)

    def fsz(ap):
        return int(np.prod(ap.shape[1:]))

    def act(out, in_, func, reads, writes, bias=0.0, scale=1.0):
        P.add("act", lambda: nc.scalar.activation(out=out, in_=in_, func=func, bias=bias, scale=scale), reads, writes,
              fs=fsz(out))

    def tt(eng, out, in0, in1, op, reads, writes):
        e = P.e[eng]
        P.add(eng, lambda: e.tensor_tensor(out, in0, in1, op), reads, writes, fs=fsz(out))

    def ts(eng, out, in0, s1, s2, op0, op1, reads, writes):
        e = P.e[eng]
        if op1 is None:
            P.add(eng, lambda: e.tensor_scalar(out, in0, s1, s2, op0), reads, writes, fs=fsz(out))
        else:
            P.add(eng, lambda: e.tensor_scalar(out, in0, s1, s2, op0, op1), reads, writes, fs=fsz(out))

    def stt(eng, out, in0, scalar, in1, op0, op1, reads, writes):
        e = P.e[eng]
        P.add(eng, lambda: e.scalar_tensor_tensor(out, in0, scalar, in1, op0, op1), reads, writes, fs=fsz(out))

    def cp(eng, out, in_, reads, writes):
        e = P.e[eng]
        if eng == "act":
            P.add(eng, lambda: e.copy(out, in_), reads, writes, fs=fsz(out))
        else:
            P.add(eng, lambda: e.tensor_copy(out, in_), reads, writes, fs=fsz(out))

    def memset(eng, ap, v, writes):
        e = P.e[eng]
        P.add(eng, lambda: e.memset(ap, v), (), writes, fs=fsz(ap))

    def recip(out, in_, reads, writes):
        P.add("dve", lambda: nc.vector.reciprocal(out, in_), reads, writes, fs=fsz(out))

    def rsum(out, in_, reads, writes):
        P.add("dve", lambda: nc.vector.reduce_sum(out, in_, AX.X), reads, writes, fs=fsz(out))

    dbg_stage = P.sb("dbg_stage", [128, 1024], F32) if cfg.get("dbg") else None
    dbg_stage_t = P.tok()
    dbg_out_t = P.tok()

    def dump(name, ap2d, toks, parts=128):
        if not cfg.get("dbg") or name not in cfg["dbg"]:
            return
        n = ap2d.shape[1]
        off = dbg_state["off"]
        dbg_state["off"] += n
        dbg_state["names"].append((name, off, n, parts))
        cp("dve", dbg_stage[0:parts, 0:n], ap2d, toks, [dbg_stage_t])
        P.dma("sp", dbg_d[0:parts, off:off + n], dbg_stage[0:parts, 0:n], reads=[dbg_stage_t], writes=[dbg_out_t])

    P.dma("sp", mk_h[:, :, :], mk_d.rearrange("p (m j) -> p m j", m=NM), writes=[mk_t])
    P.dma("sp", pp_h[:, 0:NPP], pp_d[:, :], writes=[pp_t])
    P.dma("sp", rp_h[:, 0:NRP], rp_d[:, :], writes=[rp_t])
    for i, n in enumerate(["ID", "ONES", "BLK64"]):
        P.dma("pool", cb_h[:, i, :], mk_d[:, MI[n] * 128:(MI[n] + 1) * 128], writes=[cb_t])
    QSC, KSC = NPP, NPP + 1
    ts("dve", pp_h[:, QSC:QSC + 1], pp_h[:, PP_QN:PP_QN + 1], 0.125, None, ALU.mult, None, [pp_t], [pp_t])
    cp("dve", pp_h[:, KSC:KSC + 1], pp_h[:, PP_KN:PP_KN + 1], [pp_t], [pp_t])
    NEXPA = NRP
    act(rp_h[:, NEXPA:NEXPA + 4], rp_h[:, RP_ALOG:RP_ALOG + 4], AF.Exp, [rp_t], [rp_t])
    ts("dve", rp_h[:, NEXPA:NEXPA + 4], rp_h[:, NEXPA:NEXPA + 4], -1.0, None, ALU.mult, None, [rp_t], [rp_t])
    act(rp_h[:, RP_SINK:RP_SINK + 16], rp_h[:, RP_SINK:RP_SINK + 16], AF.Exp, [rp_t], [rp_t])
    memset("dve", S_h[:, :, :], 0.0, [S_t])
    memset("dve", Sb_h[:, :, :], 0.0, [Sb_t])
    memset("dve", haloA_h[:, :, :], 0.0, haloA_t)
    memset("dve", haloQ_h[:, :, :], 0.0, haloQ_t)
    memset("dve", haloF_h[:, :, :, :], 0.0, haloF_t[0] + haloF_t[1])
    memset("dve", Vb_h[:, :, :, :], 1.0, Vb_t)
    memset("dve", kZ_h[:, :, :], 0.0, kZ_t)

    def wsrc(name):
        kind = name[0]
        if kind == "in":
            b = name[1]
            if b == "ab":
                return [((8, 8), win_d[:, 3584:3592].rearrange("(k p) n -> p k n", p=128), 0)]
            return [((8, 512), win_d[:, b * 512:(b + 1) * 512].rearrange("(k p) n -> p k n", p=128), 0)]
        if kind in ("out", "q", "o"):
            src = {"out": wout_d, "q": wq_d, "o": wo_d}[kind]
            b = name[1]
            return [((8, 512), src[:, b * 512:(b + 1) * 512].rearrange("(k p) n -> p k n", p=128), 0)]
        if kind == "kv":
            return [((8, 256), wk_d[:, :].rearrange("(k p) n -> p k n", p=128), 0),
                    ((8, 256), wv_d[:, :].rearrange("(k p) n -> p k n", p=128), 2048)]
        if kind == "up":
            l, half, jb = name[1], name[2], name[3]
            n = 512 if jb < 5 else 256
            c0 = half * 2816 + jb * 512
            return [((8, n), wup_d[l, :, c0:c0 + n].rearrange("(k p) n -> p k n", p=128), 0)]
        if kind == "dn":
            l, mp, jh = name[1], name[2], name[3]
            return [((11, 256), wdn_d[l, jh * 1408:(jh + 1) * 1408, mp * 256:(mp + 1) * 256].rearrange("(j p) n -> p j n", p=128), 0)]
        raise ValueError(name)

    def tile_blocks():
        seq = []
        if cfg["mix0"]:
            seq += [("in", b) for b in range(7)] + [("in", "ab"), ("out", 0), ("out", 1)]
        if cfg["ffn0"]:
            for jb in range(6):
                seq += [("up", 0, 0, jb), ("up", 0, 1, jb)]
            seq += [("dn", 0, mp, jh) for mp in range(4) for jh in range(2)]
        if cfg["mix1"]:
            seq += [("q", 0), ("q", 1), ("kv",), ("o", 0), ("o", 1)]
        if cfg["ffn1"]:
            for jb in range(6):
                seq += [("up", 1, 0, jb), ("up", 1, 1, jb)]
            seq += [("dn", 1, mp, jh) for mp in range(4) for jh in range(2)]
        return seq

    wseq = []
    for _ in GROUPS_PER_TILE:
        wseq += tile_blocks()
    wstate = dict(issued=0, used=0, released=0)

    def w_pump():
        while wstate["issued"] < len(wseq) and wstate["issued"] - NSLOT < wstate["released"]:
            i = wstate["issued"]
            s = i % NSLOT
            for (shape, src, off) in wsrc(wseq[i]):
                n = shape[0] * shape[1]
                dst = ring_h[s][:, off:off + n].rearrange("p (k n) -> p k n", k=shape[0])
                P.dma("pool", dst, src, writes=[ring_t[s]])
            wstate["issued"] += 1

    def wnext(name):
        i = wstate["used"]
        assert wseq[i] == name, (wseq[i], name)
        w_pump()
        assert wstate["issued"] > i, "weight ring over-subscribed"
        wstate["used"] += 1
        s = i % NSLOT
        return ring_h[s], ring_t[s]

    def wrel(n=1):
        wstate["released"] += n
        assert wstate["released"] <= wstate["used"]
        w_pump()

    def wview(slot, k, n, off=0):
        return slot[:, off:off + k * n].rearrange("p (k n) -> p k n", k=k)

    def colgroups(C):
        return [(0, 512), (512, C - 512)] if C > 512 else [(0, C)]

    def norm(C, wcol):
        P.label = "norm"
        sqp = TempPool(arena, [C], BF16, 4)
        rs, rst = arena.alloc([C], F32)
        cgs = colgroups(C)
        prs = [psum(n) for (c0, n) in cgs]
        for c in range(8):
            sq, sqt = sqp.get()
            if c == 5:
                tt("pool", sq, hT_h[:, c, 0:C], hT_h[:, c, 0:C], ALU.mult, [hT_t[c]], sqt)
            else:
                act(sq, hT_h[:, c, 0:C], AF.Square, [hT_t[c]], sqt)
            for (pr, prt), (c0, n) in zip(prs, cgs):
                mm(pr, ONES_B, sq[:, c0:c0 + n], c == 0, c == 7, [cb_t] + sqt, [prt])
        for (pr, prt), (c0, n) in zip(prs, cgs):
            act(rs[:, c0:c0 + n], pr, AF.Ln, [prt], rst, bias=EPS, scale=1.0 / 1024)
        act(rs, rs, AF.Exp, rst, rst, scale=-0.5)
        for c in range(8):
            stt("dve", hn_h[:, c, 0:C], hT_h[:, c, 0:C], PPc(wcol + c), rs, ALU.mult, ALU.mult,
                [hT_t[c], pp_t] + rst, [hn_t[c]])

    def proj(slot, slot_t, kview, c, C, wcols=128):
        res = []
        for (c0, n) in colgroups(C):
            pr, prt = psum(n)
            for k in range(8):
                mm(pr, kview[:, k, c * 128:c * 128 + wcols], hn_h[:, k, c0:c0 + n], k == 0, k == 7,
                   [slot_t, hn_t[k]], [prt])
            res.append((pr, prt, c0, n))
        return res

    def conv_taps(xe, xet, C, K, wcol0, accp):
        acc, acct = accp.get()
        act(acc, xe[:, 0:C], AF.Copy, xet + [pp_t], acct, scale=PPc(wcol0))
        for j in range(1, K):
            stt("dve", acc, xe[:, j:j + C], PPc(wcol0 + j), acc, ALU.mult, ALU.add, xet + [pp_t] + acct, acct)
        return acc, acct

    def load_tile(ti, g0, G):
        C = 128 * G
        P.label = "load"
        xp = TempPool(arena, [1024], F32, 5)
        for g in range(G):
            gg = g0 + g
            xin, xint = xp.get()
            if gg == 0:
                memset("dve", xin, 0.0, xint)
                P.dma("sp", xin[112:128, :], meta_d[:, :], writes=xint)
            else:
                P.dma("sp", xin, x_d[(gg - 1) * 128:gg * 128, :], writes=xint)
            for half in range(2):
                pt, ptt = psum(512, F32, [4, 128])
                for cc in range(4):
                    c = half * 4 + cc
                    tr(pt[:, cc, :], xin[:, c * 128:(c + 1) * 128], MK("ID"), xint + [mk_t], [ptt])
                cp("act" if half == 0 else "dve", hT_h[:, half * 4:half * 4 + 4, g * 128:(g + 1) * 128], pt,
                   [ptt], hT_t[half * 4:half * 4 + 4])

    def store_tile(ti, g0, G):
        P.label = "store"
        xp = TempPool(arena, [1024], F32, 2)
        for g in range(G):
            gg = g0 + g
            if gg == 0:
                continue
            xo, xot = xp.get()
            for half in range(2):
                pt, ptt = psum(512, F32, [4, 128])
                for cc in range(4):
                    c = half * 4 + cc
                    tr(pt[:, cc, :], hT_h[:, c, g * 128:(g + 1) * 128], MK("ID"), [hT_t[c], mk_t], [ptt])
                cp("act" if half == 0 else "dve", xo[:, half * 512:(half + 1) * 512], pt.rearrange("p a b -> p (a b)"),
                   [ptt], xot)
            P.dma("sp", out_d[(gg - 1) * 128:gg * 128, :], xo, reads=xot, writes=[out_t])

    def out_proj(names, rhs_h, rhs_t, C):
        P.label = "outproj"
        slots = [wnext(n) for n in names]
        for m in range(8):
            slot, slot_t = slots[m // 4]
            wv_ = wview(slot, 8, 512)
            mc = m % 4
            for (c0, n) in colgroups(C):
                pr, prt = psum(n)
                for k in range(8):
                    mm(pr, wv_[:, k, mc * 128:(mc + 1) * 128], rhs_h[:, k, c0:c0 + n], k == 0, k == 7,
                       [slot_t, rhs_t[k]], [prt])
                tt("dve", hT_h[:, m, c0:c0 + n], hT_h[:, m, c0:c0 + n], pr, ALU.add, [hT_t[m], prt], [hT_t[m]])
            if m % 4 == 3:
                wrel()

    def ffn(l, C):
        arena.reset()
        norm(C, PP_FNW + l * 8)
        P.label = "ffn_up"
        actb, actt = arena.alloc([22, C], BF16, ntok=22)
        xep = TempPool(arena, [C + 2], F32, 3)
        accp = TempPool(arena, [C], F32, 3)
        slots = {}

        def ffn_chunk(j):
            jb, jc = j // 4, j % 4
            ncol = 512 if jb < 5 else 256
            last = (jc == ncol // 128 - 1)
            P.label = "ffn_up"
            if jc == 0:
                slots[("g", jb)] = wnext(("up", l, 0, jb))
                slots[("v", jb)] = wnext(("up", l, 1, jb))
            sg, sgt = slots[("g", jb)]
            sv, svt = slots[("v", jb)]
            pg = proj(sg, sgt, wview(sg, 8, ncol), jc, C)
            if last:
                wrel()
            xe, xet = xep.get()
            cp("dve", xe[:, 0:2], haloF_h[:, l, j, :], [haloF_t[l][j]], xet)
            for (pr, prt, c0, n) in pg:
                cp("act", xe[:, 2 + c0:2 + c0 + n], pr, [prt], xet)
            yield
            P.label = "ffn_up"
            cp("dve", haloF_h[:, l, j, :], xe[:, C:C + 2], xet, [haloF_t[l][j]])
            acc, acct = conv_taps(xe, xet, C, 3, PP_FC + l * 66 + j * 3, accp)
            pv = proj(sv, svt, wview(sv, 8, ncol), jc, C)
            if last:
                wrel()
            act(acc, acc, AF.Silu, acct, acct)
            yield
            P.label = "ffn_up"
            for (pr, prt, c0, n) in pv:
                tt("dve", actb[:, j, c0:c0 + n], pr, acc[:, c0:c0 + n], ALU.mult, [prt] + acct, [actt[j]])

        run_pipeline([ffn_chunk(j) for j in range(22)], 3, lag=1)
        P.label = "ffn_dn"
        cgs = colgroups(C)
        for mp in range(4):
            regs = {}
            for mi in range(2):
                for ci, (c0, n) in enumerate(cgs):
                    regs[(mi, ci)] = psum(n)
            for jh in range(2):
                sd, sdt = wnext(("dn", l, mp, jh))
                vd = wview(sd, 11, 256)
                for mi in range(2):
                    for ci, (c0, n) in enumerate(cgs):
                        pr, prt = regs[(mi, ci)]
                        for jj in range(11):
                            j = jh * 11 + jj
                            mm(pr, vd[:, jj, mi * 128:(mi + 1) * 128], actb[:, j, c0:c0 + n], j == 0, j == 21,
                               [sdt, actt[j]], [prt])
                wrel()
            for mi in range(2):
                m = mp * 2 + mi
                for ci, (c0, n) in enumerate(cgs):
                    pr, prt = regs[(mi, ci)]
                    tt("dve", hT_h[:, m, c0:c0 + n], hT_h[:, m, c0:c0 + n], pr, ALU.add, [hT_t[m], prt], [hT_t[m]])

    def even_mixer(ti, G):
        C = 128 * G
        arena.reset()
        yab, yabt = arena.alloc([8, C], BF16, ntok=8)
        qkvT, qkvt = arena.alloc([12, C], BF16, ntok=12)
        sz, szt = arena.alloc([G, 512], BF16, ntok=G)
        bg, bgt = arena.alloc([G, 16], F32, ntok=G)
        mark = arena.off
        norm(C, PP_ANW + 0)
        gip = TempPool(arena, [C], F32, 2)
        xep = TempPool(arena, [C + 3], F32, 4)
        accp = TempPool(arena, [C], F32, 4)
        sqp = TempPool(arena, [C], BF16, 3)
        slots = {}

        def conva_chunk(c):
            P.label = "convA"
            if c == 0:
                slots["gi"] = wnext(("in", 0))
                slots["go"] = wnext(("in", 1))
                slots["ah"] = wnext(("in", 2))
            s_gi, t_gi = slots["gi"]
            s_go, t_go = slots["go"]
            s_ah, t_ah = slots["ah"]
            p_gi = proj(s_gi, t_gi, wview(s_gi, 8, 512), c, C)
            p_ah = proj(s_ah, t_ah, wview(s_ah, 8, 512), c, C)
            gi, git = gip.get()
            xe, xet = xep.get()
            cp("dve", xe[:, 0:2], haloA_h[:, c, :], [haloA_t[c]], xet)
            for (pr, prt, c0, n) in p_gi:
                cp("act", gi[:, c0:c0 + n], pr, [prt], git)
            yield
            P.label = "convA"
            for (pr, prt, c0, n) in p_ah:
                tt("dve", xe[:, 2 + c0:2 + c0 + n], pr, gi[:, c0:c0 + n], ALU.mult, [prt] + git, xet)
            cp("dve", haloA_h[:, c, :], xe[:, C:C + 2], xet, [haloA_t[c]])
            p_go = proj(s_go, t_go, wview(s_go, 8, 512), c, C)
            if c == 3:
                wrel(3)
            acc, acct = conv_taps(xe, xet, C, 3, PP_CA + c * 3, accp)
            yield
            P.label = "convA"
            for (pr, prt, c0, n) in p_go:
                tt("dve", yab[:, c, c0:c0 + n], pr, acc[:, c0:c0 + n], ALU.mult, [prt] + acct, [yabt[c]])

        def qkv_chunk(cc):
            b, c = cc // 4, cc % 4
            P.label = "qkv"
            if c == 0:
                slots[("qkv", b)] = wnext(("in", 3 + b))
            sw, swt = slots[("qkv", b)]
            pq = proj(sw, swt, wview(sw, 8, 512), c, C)
            if c == 3:
                wrel()
            xe, xet = xep.get()
            cp("dve", xe[:, 0:3], haloQ_h[:, cc, :], [haloQ_t[cc]], xet)
            for (pr, prt, c0, n) in pq:
                cp("act", xe[:, 3 + c0:3 + c0 + n], pr, [prt], xet)
            yield
            P.label = "qkv"
            cp("dve", haloQ_h[:, cc, :], xe[:, C:C + 3], xet, [haloQ_t[cc]])
            acc, acct = conv_taps(xe, xet, C, 4, PP_DC + cc * 4, accp)
            if b == 2:
                act(qkvT[:, cc, :], acc, AF.Silu, acct, [qkvt[cc]])
                return
            act(qkvT[:, cc, :], acc, AF.Silu, acct, [qkvt[cc]])
            sq, sqt = sqp.get()
            act(sq, qkvT[:, cc, :], AF.Square, [qkvt[cc]], sqt)
            yield
            P.label = "qkv"
            for (c0, n) in colgroups(C):
                pr, prt = psum(n)
                mm(pr, ONES_B, sq[:, c0:c0 + n], True, True, [cb_t] + sqt, [prt])
                cp("dve", ssum[:, cc, c0:c0 + n], pr, [prt], [ssumt[cc]])

        ssum, ssumt = arena.alloc([8, C], F32, ntok=8)
        run_pipeline([conva_chunk(c) for c in range(4)] + [qkv_chunk(cc) for cc in range(12)], 4, lag=1)
        P.label = "qkv"
        for half in range(2):
            hsl = slice(4 * half, 4 * half + 4)
            act(ssum[:, hsl, :], ssum[:, hsl, :], AF.Ln, ssumt[4 * half:4 * half + 4], ssumt[4 * half:4 * half + 4], bias=EPS)
            act(ssum[:, hsl, :], ssum[:, hsl, :], AF.Exp, ssumt[4 * half:4 * half + 4], ssumt[4 * half:4 * half + 4], scale=-0.5)
        for cc in range(8):
            stt("dve", qkvT[:, cc, :], qkvT[:, cc, :], (128 ** -0.5) if cc < 4 else 1.0, ssum[:, cc, :],
                ALU.mult, ALU.mult, [qkvt[cc], ssumt[cc]], [qkvt[cc]])
        P.label = "ztok"
        s_z, t_z = wnext(("in", 6))
        s_ab, t_ab = wnext(("in", "ab"))
        vz, vab = wview(s_z, 8, 512), wview(s_ab, 8, 8)
        for g in range(G):
            gc = slice(g * 128, (g + 1) * 128)
            pz, pzt = psum(512)
            for k in range(8):
                mm(pz, hn_h[:, k, gc], vz[:, k, :], k == 0, k == 7, [t_z, hn_t[k]], [pzt])
            act(sz[:, g, :], pz, AF.Silu, [pzt], [szt[g]])
        for g in range(G):
            gc = slice(g * 128, (g + 1) * 128)
            pab, pabt = psum(8)
            for k in range(8):
                mm(pab, hn_h[:, k, gc], vab[:, k, :], k == 0, k == 7, [t_ab, hn_t[k]], [pabt])
            act(bg[:, g, 0:4], pab[:, 0:4], AF.Sigmoid, [pabt], [bgt[g]])
            tt("dve", bg[:, g, 12:16], pab[:, 4:8], rp_h[:, RP_DTB:RP_DTB + 4], ALU.add, [pabt, rp_t], [bgt[g]])
        ts("dve", bg[:, :, 4:8], bg[:, :, 0:4], -1.0, None, ALU.mult, None, bgt, bgt)
        act(bg[:, :, 12:16], bg[:, :, 12:16], AF.Exp, bgt, bgt)
        act(bg[:, :, 12:16], bg[:, :, 12:16], AF.Ln, bgt, bgt, bias=1.0)
        tt("dve", bg[:, :, 8:12], bg[:, :, 12:16], rp_h[:, NEXPA:NEXPA + 4].unsqueeze(1).broadcast_to([128, G, 4]),
           ALU.mult, bgt + [rp_t], bgt)
        wrel(2)
        arena.off = mark
        bl = lambda ap: ap.unsqueeze(2).broadcast_to([128, 4, 128])

        class DnBufs:
            pass

        def dn_bufs():
            B = DnBufs()
            A4 = lambda dt: arena.alloc([4, 128], dt)
            for nm in ("nkb", "kbd", "kdec", "vb", "nkbT", "qdT", "Ybf", "qkT", "nwT", "vn", "ybt",
                       "Nm", "Mm", "Nl", "Ml", "X", "Y", "Pm", "Qm"):
                setattr(B, nm, A4(BF16))
            for nm in ("gL", "Dl", "Du", "dmTi", "edB"):
                setattr(B, nm, A4(F32))
            B.u, B.o, B.sqo, B.nwz = B.gL, B.Dl, B.Du, B.dmTi
            B.e16 = arena.alloc([16], F32)
            B.bed = arena.alloc([4], F32)
            B.ss = arena.alloc([4], F32)
            return B

        def dn_group(g, B):
            gc = slice(g * 128, (g + 1) * 128)
            beta, nbeta, gg_ = bg[:, g, 0:4], bg[:, g, 4:8], bg[:, g, 8:12]
            nkb, nkbt = B.nkb; kbd, kbdt = B.kbd; kdec, kdect = B.kdec; vb, vbt = B.vb
            nkbT, nkbTt = B.nkbT; qdT, qdTt = B.qdT; Ybf, Ybft = B.Ybf; qkT, qkTt = B.qkT
            nwT, nwTt = B.nwT; vn, vnt = B.vn; ybt, ybtt = B.ybt
            Nm, Nmt = B.Nm; Mm, Mmt = B.Mm; Nl, Nlt = B.Nl; Ml, Mlt = B.Ml
            X, Xt = B.X; Y, Yt = B.Y; Pm, Pmt = B.Pm; Qm, Qmt = B.Qm
            gL, gLt = B.gL; Dl, Dlt = B.Dl; Du, Dut = B.Du; dmTi, dmTit = B.dmTi; edB, edBt = B.edB
            u, ut = B.u; o, ot = B.o; sqo, sqot = B.sqo; nwz, nwzt = B.nwz
            e16, e16t = B.e16; bed, bedt = B.bed; ss, sst = B.ss
            P.label = "dn_prep"
            pst, pstt = psum(512, BF16, [8, 128])
            for hh in range(4):
                tr(pst[:, hh, :], qkvT[:, 4 + hh, gc], ID_B, [qkvt[4 + hh], cb_t], [pstt])
                tr(pst[:, 4 + hh, :], qkvT[:, 8 + hh, gc], ID_B, [qkvt[8 + hh], cb_t], [pstt])
            pse, pset = psum(16)
            mm(pse[:, 0:4], MK("L"), gg_, True, True, [mk_t, bgt[g]], [pset])
            mm(pse[:, 4:8], MK("U"), gg_, True, True, [mk_t, bgt[g]], [pset])
            mm(pse[:, 8:12], MK("MC0"), gg_, True, True, [mk_t, bgt[g]], [pset])
            mm(pse[:, 12:16], MK("MC1"), gg_, True, True, [mk_t, bgt[g]], [pset])
            tt("dve", gL, MK4("L"), bl(gg_), ALU.mult, [mk_t, bgt[g]], gLt)
            yield
            P.label = "dn_prep"
            act(e16, pse, AF.Exp, [pset], e16t)
            tt("dve", bed, beta, e16[:, 0:4], ALU.mult, [bgt[g]] + e16t, bedt)
            tt("dve", nkb, pst[:, 0:4, :], bl(nbeta), ALU.mult, [pstt, bgt[g]], nkbt)
            psD, psDt = psum(512, F32, [4, 128])
            for hh in range(4):
                mm(psD[:, hh, :], gL[:, hh, :], MK("ONES"), True, False, gLt + [mk_t], [psDt])
                mm(psD[:, hh, :], MK("NEGONES"), gL[:, hh, :], False, True, gLt + [mk_t], [psDt])
            psB, psBt = psum(512, F32, [4, 128])
            for hh in range(4):
                mm(psB[:, hh, :], MK("ONES"), gL[:, hh, :], True, True, gLt + [mk_t], [psBt])
            psn, psnt = psum(256, BF16, [4, 128])
            for hh in range(4):
                tr(psn[:, hh, :], nkb[:, hh, :], ID_B, nkbt + [cb_t], [psnt])
            tt("dve", kbd, pst[:, 0:4, :], bl(bed), ALU.mult, [pstt] + bedt, kbdt)
            tt("dve", kdec, pst[:, 0:4, :], bl(e16[:, 4:8]), ALU.mult, [pstt] + e16t, kdect)
            tt("dve", vb, pst[:, 4:8, :], bl(beta), ALU.mult, [pstt, bgt[g]], vbt)
            yield
            P.label = "dn_prep"
            cp("act", nkbT, psn, [psnt], nkbTt)
            tt("dve", Dl, psD, MK4("NEGSL"), ALU.add, [psDt, mk_t], Dlt)
            tt("dve", Du, psD, MK4("POSSU"), ALU.add, [psDt, mk_t], Dut)
            act(Dl, Dl, AF.Exp, Dlt, Dlt)
            act(Du, Du, AF.Exp, Dut, Dut, scale=-1.0)
            act(edB, psB, AF.Exp, [psBt], edBt)
            psN, psNt = psum(512, F32, [4, 128])
            psM, psMt = psum(512, F32, [4, 128])
            psQ, psQt = psum(512, F32, [4, 128])
            for hh in range(4):
                mm(psN[:, hh, :], nkbT[:, hh, :], qkvT[:, 4 + hh, gc], True, True, nkbTt + [qkvt[4 + hh]], [psNt])
            for hh in range(4):
                mm(psM[:, hh, :], qkvT[:, 4 + hh, gc], nkbT[:, hh, :], True, True, nkbTt + [qkvt[4 + hh]], [psMt])
            for hh in range(4):
                mm(psQ[:, hh, :], qkvT[:, 4 + hh, gc], qkvT[:, hh, gc], True, True, [qkvt[4 + hh], qkvt[hh]], [psQt])
            yield
            P.label = "dn_prep"
            tt("dve", dmTi, Du, MK4("ID"), ALU.add, Dut + [mk_t], dmTit)
            tt("dve", qdT, qkvT[:, 0:4, gc], edB, ALU.mult, qkvt[0:4] + edBt, qdTt)
            tt("dve", Nm, psN, Dl, ALU.mult, [psNt] + Dlt, Nmt)
            tt("dve", Mm, psM, Du, ALU.mult, [psMt] + Dut, Mmt)
            tt("dve", qkT, psQ, dmTi, ALU.mult, [psQt] + dmTit, qkTt)
            P.label = "dn_chain"
            tt("dve", Nl, Nm, MK4("LV0"), ALU.mult, Nmt + [mk_t], Nlt)
            tt("dve", X, Nl, MK4("ID"), ALU.add, Nlt + [mk_t], Xt)
            tt("dve", Ml, Mm, MK4("UV0"), ALU.mult, Mmt + [mk_t], Mlt)
            tt("dve", Y, Ml, MK4("ID"), ALU.add, Mlt + [mk_t], Yt)
            for l in range(1, 6):
                last = (l == 5)
                P.label = "dn_chain"
                tt("dve", Nl, Nm, MK4("LV%d" % l), ALU.mult, Nmt + [mk_t], Nlt)
                if not last:
                    tt("dve", Ml, Mm, MK4("UV%d" % l), ALU.mult, Mmt + [mk_t], Mlt)
                pQ, pQt = psum(512, F32, [4, 128])
                for hh in range(4):
                    mm(pQ[:, hh, :], Nl[:, hh, :], Y[:, hh, :], True, True, Nlt + Yt, [pQt])
                if not last:
                    pP, pPt = psum(512, F32, [4, 128])
                    for hh in range(4):
                        mm(pP[:, hh, :], Ml[:, hh, :], X[:, hh, :], True, True, Mlt + Xt, [pPt])
                yield
                P.label = "dn_chain"
                cp("act", Qm, pQ, [pQt], Qmt)
                if not last:
                    cp("act", Pm, pP, [pPt], Pmt)
                pY, pYt = psum(512, F32, [4, 128])
                for hh in range(4):
                    mm(pY[:, hh, :], X[:, hh, :], Qm[:, hh, :], True, True, Xt + Qmt, [pYt])
                if not last:
                    pX, pXt = psum(512, F32, [4, 128])
                    for hh in range(4):
                        mm(pX[:, hh, :], Y[:, hh, :], Pm[:, hh, :], True, True, Yt + Pmt, [pXt])
                yield
                P.label = "dn_chain"
                if not last:
                    tt("dve", Y, Y, pY, ALU.add, Yt + [pYt], Yt)
                    tt("dve", X, X, pX, ALU.add, Xt + [pXt], Xt)
                else:
                    tt("dve", Ybf, Y, pY, ALU.add, Yt + [pYt], Ybft)
            P.label = "dn_uw"
            pU, pUt = psum(512, F32, [4, 128])
            for hh in range(4):
                mm(pU[:, hh, :], Ybf[:, hh, :], vb[:, hh, :], True, True, Ybft + vbt, [pUt])
            pW, pWt = psum(512, F32, [4, 128])
            for hh in range(4):
                mm(pW[:, hh, :], kbd[:, hh, :], Ybf[:, hh, :], True, True, kbdt + Ybft, [pWt])
            yield
            P.label = "dn_uw"
            cp("act", u, pU, [pUt], ut)
            act(nwT, pW, AF.Copy, [pWt], nwTt, scale=-1.0)
            tt("dve", nwz, sz[:, g, :].rearrange("p (a b) -> p a b", a=4),
               rp_h[:, RP_DNW:RP_DNW + 128].unsqueeze(1).broadcast_to([128, 4, 128]), ALU.mult, [szt[g], rp_t], nwzt)
            while scan_turn[0] != g:
                yield
            for cc in range(2):
                P.label = "dn_scan"
                r = slice(64 * cc, 64 * cc + 64)
                pV, pVt = psum(512, F32, [4, 128])
                for hh in range(4):
                    mm(pV[r, hh, :], nwT[:, hh, r], Sb_h[:, hh, :], True, True, nwTt + [Sb_t], [pVt])
                yield
                P.label = "dn_scan"
                tt("dve", vn[r, :, :], pV[r, :, :], u[r, :, :], ALU.add, [pVt] + ut, vnt)
                tt("dve", S_h[:, :, :], S_h[:, :, :], bl(e16[:, 8 + 4 * cc:12 + 4 * cc]), ALU.mult, [S_t] + e16t, [S_t])
                pO, pOt = psum(512, F32, [4, 128])
                for hh in range(4):
                    mm(pO[r, hh, :], qdT[:, hh, r], Sb_h[:, hh, :], True, False, qdTt + [Sb_t], [pOt])
                    mm(pO[r, hh, :], qkT[r, hh, r], vn[r, hh, :], False, True, qkTt + vnt, [pOt])
                pS, pSt = psum(512, F32, [4, 128])
                for hh in range(4):
                    mm(pS[:, hh, :], kdec[r, hh, :], vn[r, hh, :], True, True, kdect + vnt, [pSt])
                yield
                P.label = "dn_scan"
                tt("dve", S_h[:, :, :], S_h[:, :, :], pS, ALU.add, [S_t, pSt], [S_t])
                cp("act", Sb_h[:, :, :], S_h[:, :, :], [S_t], [Sb_t])
                cp("act", o[r, :, :], pO[r, :, :], [pOt], ot)
            scan_turn[0] = g + 1
            P.label = "dn_out"
            act(sqo, o, AF.Square, ot, sqot)
            rsum(ss, sqo, sqot, sst)
            act(ss, ss, AF.Ln, sst, sst, bias=EPS, scale=1.0 / 128)
            act(ss, ss, AF.Exp, sst, sst, scale=-0.5)
            yield
            P.label = "dn_out"
            tt("dve", sqo, o, bl(ss), ALU.mult, ot + sst, sqot)
            tt("dve", ybt, sqo, nwz, ALU.mult, sqot + nwzt, ybtt)
            pT, pTt = psum(256, BF16, [4, 128])
            for hh in range(4):
                tr(pT[:, hh, :], ybt[:, hh, :], ID_B, ybtt + [cb_t], [pTt])
            yield
            P.label = "dn_out"
            cp("act", yab[:, 4:8, gc], pT, [pTt], yabt[4:8])

        sets = [dn_bufs(), dn_bufs()]
        scan_turn = [0]
        oslots = {}

        def outproj_cg(ci, first, lastcg):
            (c0, n) = colgroups(C)[ci]
            P.label = "outproj"
            if first:
                oslots[0] = wnext(("out", 0))
                oslots[1] = wnext(("out", 1))
            for m in range(8):
                P.label = "outproj"
                slot, slot_t = oslots[m // 4]
                wv_ = wview(slot, 8, 512)
                mc = m % 4
                pr, prt = psum(n)
                for k in range(8):
                    mm(pr, wv_[:, k, mc * 128:(mc + 1) * 128], yab[:, k, c0:c0 + n], k == 0, k == 7,
                       [slot_t, yabt[k]], [prt])
                tt("dve", hT_h[:, m, c0:c0 + n], hT_h[:, m, c0:c0 + n], pr, ALU.add, [hT_t[m], prt], [hT_t[m]])
                if lastcg and m % 4 == 3:
                    wrel()
                yield

        gens = [dn_group(g, sets[g % 2]) for g in range(G)]
        if G == 5:
            gens.append(outproj_cg(0, True, False))
        run_pipeline(gens, cfg.get("dn_depth", 2), lag=8)
        if G == 5:
            for _ in outproj_cg(1, False, True):
                pass
        else:
            for _ in outproj_cg(0, True, True):
                pass
        return
        out_proj([("out", 0), ("out", 1)], yab, yabt, C)

    def swa_mixer(ti, G):
        C = 128 * G
        arena.reset()
        norm(C, PP_ANW + 8)
        qT, qTt = arena.alloc([8, C], BF16, ntok=8)
        OT, OTt = arena.alloc([8, C], BF16, ntok=8)
        qrp = TempPool(arena, [C], F32, 4)
        sqp = TempPool(arena, [C], BF16, 4)
        rsp = TempPool(arena, [C], F32, 4)
        Otp = TempPool(arena, [16, 64], BF16, 2)
        Ecp = TempPool(arena, [4, 128], BF16, 5)
        Emp = TempPool(arena, [4, 128], BF16, 5)
        Epp = TempPool(arena, [4, 128], BF16, 5)
        denp = TempPool(arena, [4], F32, 6)
        kTt_, kTtt = arena.alloc([2, C], BF16, ntok=2)
        P.label = "swa_proj"
        sq_slots = [wnext(("q", 0)), wnext(("q", 1))]
        s_kv, t_kv = wnext(("kv",))
        vk = wview(s_kv, 8, 256)

        def head_chunk(kind, c):
            P.label = "swa_proj"
            if kind == "q":
                slot, slot_t = sq_slots[c // 4]
                pq = proj(slot, slot_t, wview(slot, 8, 512), c % 4, C)
                if c % 4 == 3:
                    wrel()
                sccol = QSC
            else:
                pq = proj(s_kv, t_kv, vk, c, C)
                sccol = KSC
            qraw, qrawt = qrp.get()
            sq, sqt = sqp.get()
            for (pr, prt, c0, n) in pq:
                cp("act", qraw[:, c0:c0 + n], pr, [prt], qrawt)
            act(sq, qraw, AF.Square, qrawt, sqt)
            yield
            P.label = "swa_proj"
            rs, rst = rsp.get()
            for (c0, n) in colgroups(C):
                pr2, pr2t = psum(n)
                mm(pr2, BLK_B, sq[:, c0:c0 + n], True, True, [cb_t] + sqt, [pr2t])
                act(rs[:, c0:c0 + n], pr2, AF.Ln, [pr2t], rst, bias=EPS, scale=1.0 / 64)
            act(rs, rs, AF.Exp, rst, rst, scale=-0.5)
            yield
            P.label = "swa_proj"
            if kind == "q":
                dst, dstt = qT[:, c, :], [qTt[c]]
            else:
                dst, dstt = kTt_[:, c, :], [kTtt[c]]
            stt("dve", dst, qraw, PPc(sccol), rs, ALU.mult, ALU.mult, qrawt + rst + [pp_t], dstt)
            if kind == "k":
                ck = c
                for hk in range(2):
                    kh = 2 * ck + hk
                    rows = slice(64 * hk, 64 * hk + 64)
                    orows = slice(64 * (1 - hk), 64 * (1 - hk) + 64)
                    cp("act", kZ_h[rows, hk * 4 + kh, 128:128 + C], kTt_[rows, ck, :], [kTtt[ck]], [kZ_t[hk * 4 + kh]])
                    P.dma("sp", kZ_h[orows, (1 - hk) * 4 + kh, 128:128 + C], kTt_[rows, ck, :], reads=[kTtt[ck]],
                          writes=[kZ_t[(1 - hk) * 4 + kh]])

        run_pipeline([head_chunk("k", 0), head_chunk("k", 1)] + [head_chunk("q", c) for c in range(8)], 4, lag=1)
        vvw = wview(s_kv, 8, 256, off=2048)
        for g in range(G):
            gc = slice(g * 128, (g + 1) * 128)
            pv, pvt = psum(256, F32, [4, 64])
            for k in range(8):
                mm(pv, hn_h[:, k, gc], vvw[:, k, :], k == 0, k == 7, [t_kv, hn_t[k]], [pvt])
            cp("act", Vb_h[:, 1 + g, :, 0:64], pv, [pvt], [Vb_t[1 + g]])
        wrel()
        if ti == 0:
            cp("dve", kZm_h[:, :, :], kZ_h[:, :, 128 + 96:128 + 128], kZ_t, [kZm_t])
            P.dma("sp", Vm_h[0:32, :, :], Vb_h[96:128, 1, :, :], reads=[Vb_t[1]], writes=[Vm_t])
        otoks = {}

        def attn_unit(g, kh):
            gc = slice(g * 128, (g + 1) * 128)
            is_meta = (ti == 0 and g == 0)
            has_prev = not (ti == 0 and g <= 1)
            P.label = "swa_attn"
            if kh == 0:
                otoks[g] = Otp.get()
            Otok, Otokt = otoks[g]
            if not is_meta:
                pM, pMt = psum(512, F32, [4, 128])
                if has_prev:
                    pP, pPt = psum(512, F32, [4, 128])
            pC, pCt = psum(512, F32, [4, 128])
            for hq in range(2):
                zi = hq * 4 + kh
                q_ap = qT[:, 2 * kh:2 * kh + 2, gc]
                qtk = [qTt[2 * kh], qTt[2 * kh + 1]]
                if not is_meta:
                    mm(pM[0:32, hq:4:2, :], kZm_h[:, zi, :], q_ap, True, True, [kZm_t] + qtk, [pMt])
                    if has_prev:
                        mm(pP[:, hq:4:2, :], kZ_h[:, zi, 128 * g:128 * g + 128], q_ap, True, True,
                           [kZ_t[zi]] + qtk, [pPt])
                mm(pC[:, hq:4:2, :], kZ_h[:, zi, 128 * (g + 1):128 * (g + 1) + 128], q_ap, True, True,
                   [kZ_t[zi]] + qtk, [pCt])
            yield
            P.label = "swa_attn"
            Ec, Ect = Ecp.get()
            act(Ec, pC, AF.Exp, [pCt], Ect)
            tt("dve", Ec, Ec, MK4("SWAMETA" if is_meta else "SWACUR"), ALU.mult, Ect + [mk_t], Ect)
            if not is_meta:
                Em, Emt = Emp.get()
                act(Em[0:32, :, :], pM[0:32, :, :], AF.Exp, [pMt], Emt)
                tt("dve", Em[0:32, :, :], Em[0:32, :, :],
                   mk_h[0:32, MI["METAK"]:MI["METAK"] + 1, :].broadcast_to([32, 4, 128]),
                   ALU.mult, Emt + [mk_t], Emt)
                if has_prev:
                    Ep, Ept = Epp.get()
                    act(Ep, pP, AF.Exp, [pPt], Ept)
                    tt("dve", Ep, Ep, MK4("SWAPREV"), ALU.mult, Ept + [mk_t], Ept)
            yield
            P.label = "swa_attn"
            pO, pOt = psum(512, F32, [4, 128])
            for gq in range(4):
                first = True
                if not is_meta:
                    mm(pO[:, gq, 0:65], Em[0:32, gq, :], Vm_h[0:32, kh, :], True, False, Emt + [Vm_t], [pOt])
                    first = False
                    if has_prev:
                        mm(pO[:, gq, 0:65], Ep[:, gq, :], Vb_h[:, g, kh, :], False, False, Ept + [Vb_t[g]], [pOt])
                mm(pO[:, gq, 0:65], Ec[:, gq, :], Vb_h[:, g + 1, kh, :], first, True, Ect + [Vb_t[g + 1]], [pOt])
            yield
            P.label = "swa_attn"
            den, dent = denp.get()
            tt("dve", den, pO[:, :, 64], rp_h[:, RP_SINK + 4 * kh:RP_SINK + 4 * kh + 4], ALU.add, [pOt, rp_t], dent)
            recip(den, den, dent, dent)
            tt("dve", Otok[:, 4 * kh:4 * kh + 4, :], pO[:, :, 0:64], den.unsqueeze(2).broadcast_to([128, 4, 64]),
               ALU.mult, [pOt] + dent, Otokt)
            if kh == 3:
                yield
                P.label = "swa_attn"
                pT, pTt = psum(512, BF16, [8, 128])
                Of = Otok.rearrange("p a b -> p (a b)")
                for c in range(8):
                    tr(pT[:, c, :], Of[:, c * 128:(c + 1) * 128], ID_B, Otokt + [cb_t], [pTt])
                yield
                P.label = "swa_attn"
                cp("act", OT[:, :, gc], pT, [pTt], OTt)

        run_pipeline([attn_unit(g, kh) for g in range(G) for kh in range(4)], 4, lag=1)
        if cfg.get("swa_dbg") == "noout":
            wnext(("o", 0)); wnext(("o", 1)); wrel(2)
        else:
            if ti == 0:
                dump("hpre", hT_h[:, 0, 128:256], hT_t)
            out_proj([("o", 0), ("o", 1)], OT, OTt, C)
            if ti == 0:
                dump("hpost", hT_h[:, 0, 128:256], hT_t)
        cp("dve", kZ_h[:, :, 0:128], kZ_h[:, :, C:C + 128], kZ_t, kZ_t)
        cp("dve", Vb_h[:, 0, :, 0:64], Vb_h[:, G, :, 0:64], [Vb_t[G]], [Vb_t[0]])

    out_t = P.tok()
    g0 = 0
    for ti, G in enumerate(GROUPS_PER_TILE):
        if ti >= cfg.get("ntiles", 99):
            break
        C = 128 * G
        arena.reset()
        load_tile(ti, g0, G)
        if cfg["mix0"]:
            even_mixer(ti, G)
        if cfg["ffn0"]:
            ffn(0, C)
        if cfg["mix1"]:
            swa_mixer(ti, G)
        if cfg["ffn1"]:
            ffn(1, C)
        arena.reset()
        store_tile(ti, g0, G)
        g0 += G
    P.wait_all("sp", [out_t, dbg_out_t])
    stats = P.emit()
    stats["dbg"] = dbg_state["names"]
    build_program.last_P = P
    return nc, stats


FULL_CFG = dict(mix0=True, ffn0=True, mix1=True, ffn1=True)
_CACHE = {}


def host_params(inp):
    f = lambda a: np.asarray(a, np.float32)
    pp = np.zeros((128, NPP), np.float32)
    for l in range(2):
        pp[:, PP_ANW + l * 8:PP_ANW + l * 8 + 8] = f(inp["attn_norm_w"])[l].reshape(8, 128).T
        pp[:, PP_FNW + l * 8:PP_FNW + l * 8 + 8] = f(inp["ffn_norm_w"])[l].reshape(8, 128).T
        fc = f(inp["ffn_conv_w"])[l].reshape(3, 22, 128)
        pp[:, PP_FC + l * 66:PP_FC + l * 66 + 66] = fc.transpose(2, 1, 0).reshape(128, 66)
    ca = f(inp["conv_a_w"])[0].reshape(3, 4, 128)
    pp[:, PP_CA:PP_CA + 12] = ca.transpose(2, 1, 0).reshape(128, 12)
    dc = f(inp["dn_conv_w"])[0].reshape(4, 12, 128)
    pp[:, PP_DC:PP_DC + 48] = dc.transpose(2, 1, 0).reshape(128, 48)
    pp[:, PP_QN] = np.tile(f(inp["swa_q_norm_w"])[0], 2)
    pp[:, PP_KN] = np.tile(f(inp["swa_k_norm_w"])[0], 2)
    rp = np.zeros((128, NRP), np.float32)
    rp[:, RP_DTB:RP_DTB + 4] = f(inp["dn_dt_bias"])[0][None, :]
    rp[:, RP_ALOG:RP_ALOG + 4] = f(inp["dn_a_log"])[0][None, :]
    rp[:, RP_DNW:RP_DNW + 128] = f(inp["dn_norm_w"])[0][None, :]
    rp[:, RP_SINK:RP_SINK + 16] = f(inp["swa_sinks"])[0][None, :]
    return pp, rp


def run(inp, cfg, built=None):
    if built is None:
        key = tuple(sorted((k, str(v)) for k, v in cfg.items()))
        if key not in _CACHE:
            _CACHE[key] = build_program(cfg)
        built = _CACHE[key]
    nc, stats = built
    f = lambda a: np.ascontiguousarray(np.asarray(a, np.float32))
    pp, rp = host_params(inp)
    masks = make_masks()
    shared = dict(meta=f(inp["meta_tokens"]), w_in=f(inp["mix_w_in"])[0], w_out=f(inp["mix_w_out"])[0],
                  wq=f(inp["swa_wq"])[0], wk=f(inp["swa_wk"])[0], wv=f(inp["swa_wv"])[0], wo=f(inp["swa_wo"])[0],
                  w_up=f(inp["ffn_w_up"]), w_dn=f(inp["ffn_w_down"]), pp=pp, rp=rp, masks=masks)
    x = f(inp["x"])
    in_maps = [dict(shared, x=x[b]) for b in range(8)]
    res = run_bass_kernel_spmd(nc, in_maps, core_ids=list(range(8)))
    if cfg.get("dbg"):
        run.dbg = [np.asarray(r["dbg"]) for r in res.results]
    return np.stack([np.asarray(r["out"], np.float32) for r in res.results], 0)


def kernel(**inputs):
    return run(inputs, FULL_CFG)
```
